# Optimizing a Trainium2 kernel written in Bass

```python
import math
import jax, jax.numpy as jnp
from jax import lax
import numpy as np

D_MODEL = 1024
BATCH = 1
SEQ = 16384
DEPTH = 2
DEC_BATCH = 32
DEC_SEQ = 1
PAST_LEN = 16384
PAGE_SIZE = 128

N_MIXERS = 2
N_POOL_LAYERS = (DEPTH + 1) // 2
N_ATTN_LAYERS = DEPTH // 2
D_FF = 2816
RMS_EPS = 1e-6
POOL_WINDOWS = (2, 4, 8, 16)
N_POOL_GROUPS = 4
POOL_GROUP_DIM = D_MODEL // N_POOL_GROUPS
POOL_STATE_LEN = max(POOL_WINDOWS) - 1
ATTN_WINDOWS = (128, 512, 2048)
ATTN_DILATIONS = (1, 4, 16)
N_ATTN_GROUPS = 3
HEAD_DIM = 64
HEADS_PER_GROUP = D_MODEL // 128
GROUP_WIDTH = HEADS_PER_GROUP * HEAD_DIM
QKV_WIDTH = N_ATTN_GROUPS * 3 * GROUP_WIDTH
N_ATTN_HEADS = N_ATTN_GROUPS * HEADS_PER_GROUP
Q_BLOCK = 128
ATTN_SCALE = HEAD_DIM ** -0.5
N_BUCKETS = 32
MAX_EXACT = N_BUCKETS // 2
MAX_DISTANCE = 2048
NEG_INF = -1e30

kernel_name = 'pool_dilated_attn_macaron_step'


def _rmsnorm(x, g):
    xf = x.astype(jnp.float32)
    y = xf * lax.rsqrt(jnp.mean(xf * xf, axis=-1, keepdims=True) + RMS_EPS)
    return (y * g.astype(jnp.float32)).astype(x.dtype)


def _ffn_half(x, g, w_gate, w_up, w_down):
    h = _rmsnorm(x, g)
    return x + 0.5 * ((jax.nn.silu(h @ w_gate) * (h @ w_up)) @ w_down)


def _t5_bucket(dist):
    distf = jnp.maximum(dist, 1).astype(jnp.float32)
    log_b = MAX_EXACT + (jnp.log(distf / MAX_EXACT) / math.log(MAX_DISTANCE / MAX_EXACT)
                         * (N_BUCKETS - MAX_EXACT)).astype(jnp.int32)
    log_b = jnp.minimum(log_b, N_BUCKETS - 1)
    return jnp.where(dist < MAX_EXACT, dist, log_b)


def _group_bias(rel_bias, g):
    n_keys = ATTN_WINDOWS[g] // ATTN_DILATIONS[g] + 1
    dist = jnp.arange(n_keys, dtype=jnp.int32) * ATTN_DILATIONS[g]
    tab = rel_bias[:, g * HEADS_PER_GROUP:(g + 1) * HEADS_PER_GROUP]
    return jnp.take(tab, _t5_bucket(dist), axis=0).T.astype(jnp.float32)


def _softmax_lse(s, valid):
    s = jnp.where(valid, s, NEG_INF)
    m = jnp.max(s, axis=-1, keepdims=True)
    e = jnp.exp(s - m)
    den = jnp.sum(e, axis=-1, keepdims=True)
    return e / den, (m + jnp.log(den))[..., 0]


def _pool_mixer(h, prefix, pos0, w_in, w_group, scale, w_out):
    B, T, D = h.shape
    u = h @ w_in
    cat = jnp.concatenate([prefix.astype(u.dtype), u], axis=1)
    c = jnp.pad(jnp.cumsum(cat.astype(jnp.float32), axis=1), ((0, 0), (1, 0), (0, 0)))
    end = c[:, POOL_STATE_LEN + 1:]
    pos = pos0 + jnp.arange(T)
    means = []
    for g, w in enumerate(POOL_WINDOWS):
        sl = slice(g * POOL_GROUP_DIM, (g + 1) * POOL_GROUP_DIM)
        start = c[:, POOL_STATE_LEN + 1 - w:POOL_STATE_LEN + 1 - w + T, sl]
        count = jnp.minimum(w, pos + 1).astype(jnp.float32)[None, :, None]
        means.append((end[..., sl] - start) / count)
    z = (jnp.concatenate(means, axis=-1) - u.astype(jnp.float32)).astype(u.dtype)
    z = jnp.einsum('btgc,gcd->btgd', z.reshape(B, T, N_POOL_GROUPS, POOL_GROUP_DIM), w_group)
    y = (z.reshape(B, T, D) * scale) @ w_out
    return y, cat[:, -POOL_STATE_LEN:]


def _dilated_attn_prompt(q, k, v, bias, window, dil):
    B, S, H, E = q.shape
    n_win = window // dil
    span = dil * Q_BLOCK
    sp = -(-S // span) * span
    nb = sp // span

    def to_blocks(a):
        a = jnp.pad(a, ((0, 0), (0, sp - S), (0, 0), (0, 0)))
        return a.reshape(B, nb, Q_BLOCK, dil, H, E).transpose(0, 3, 1, 2, 4, 5)

    def with_prev(a):
        prev = jnp.pad(a, ((0, 0), (0, 0), (1, 0), (0, 0), (0, 0), (0, 0)))[:, :, :-1]
        return jnp.concatenate([prev, a], axis=3)

    qb = to_blocks(q)
    kk = with_prev(to_blocks(k))
    vv = with_prev(to_blocks(v))
    a_idx = jnp.arange(Q_BLOCK)[:, None]
    b_idx = jnp.arange(2 * Q_BLOCK)[None, :]
    j = Q_BLOCK + a_idx - b_idx
    band = (j >= 0) & (j <= n_win)
    n_idx = jnp.arange(nb)[:, None, None]
    valid = band[None] & ((n_idx > 0) | (b_idx[None] >= Q_BLOCK))
    bias_qk = bias[:, jnp.clip(j, 0, n_win)]
    s = jnp.einsum('brnqhe,brnkhe->brnhqk', qb, kk).astype(jnp.float32) * ATTN_SCALE + bias_qk
    p, lse = _softmax_lse(s, valid[None, None, :, None])
    o = jnp.einsum('brnhqk,brnkhe->brnqhe', p, vv.astype(jnp.float32))
    o = o.transpose(0, 2, 3, 1, 4, 5).reshape(B, sp, H, E)[:, :S]
    lse = lse.transpose(0, 2, 4, 1, 3).reshape(B, sp, H)[:, :S]
    return o, lse


def _dilated_attn_step(q, k_new, v_new, k_buf, v_buf, bias, window, dil):
    T = q.shape[1]
    L = k_buf.shape[1]
    n_win = window // dil
    k_all = jnp.concatenate([k_buf.astype(k_new.dtype), k_new], axis=1)
    v_all = jnp.concatenate([v_buf.astype(v_new.dtype), v_new], axis=1)
    idx = L + jnp.arange(T)[:, None] - dil * jnp.arange(n_win + 1)[None, :]
    valid = idx >= 0
    kg = jnp.take(k_all, jnp.maximum(idx, 0), axis=1)
    vg = jnp.take(v_all, jnp.maximum(idx, 0), axis=1)
    s = jnp.einsum('bthe,btjhe->bhtj', q, kg).astype(jnp.float32) * ATTN_SCALE + bias[None, :, None, :]
    p, lse = _softmax_lse(s, valid[None, None])
    o = jnp.einsum('bhtj,btjhe->bthe', p, vg.astype(jnp.float32))
    return o, lse.transpose(0, 2, 1), k_all[:, T:], v_all[:, T:]


def _merge_groups(outs, lses, w_out, dtype):
    alpha = jax.nn.softmax(jnp.stack(lses, axis=0), axis=0)
    o = jnp.sum(alpha[..., None] * jnp.stack(outs, axis=0), axis=0)
    B, T = o.shape[:2]
    return o.reshape(B, T, GROUP_WIDTH).astype(dtype) @ w_out


def _qkv(h, w_qkv):
    B, T, _ = h.shape
    return (h @ w_qkv).reshape(B, T, N_ATTN_GROUPS, 3, HEADS_PER_GROUP, HEAD_DIM)


def setup_inputs(seed: int = 0) -> dict:
    key = jax.random.key(seed)
    ks = jax.random.split(key, 40)
    f32 = jnp.float32

    def nrm(k, shape, scale):
        return jax.random.normal(k, shape, f32) * scale

    inp = {}
    inp['x_prompt'] = nrm(ks[0], (BATCH, SEQ, D_MODEL), 1.0)
    inp['x_sample'] = nrm(ks[1], (DEC_BATCH, DEC_SEQ, D_MODEL), 1.0)
    inp['state_pool'] = nrm(ks[2], (N_POOL_LAYERS, DEC_BATCH, POOL_STATE_LEN, D_MODEL), 1.0)
    for g, w in enumerate(ATTN_WINDOWS):
        L = min(w, PAST_LEN)
        shp = (N_ATTN_LAYERS, DEC_BATCH, L, HEADS_PER_GROUP, HEAD_DIM)
        inp['cache_k_w%d' % w] = nrm(ks[3 + 2 * g], shp, 1.0)
        inp['cache_v_w%d' % w] = nrm(ks[4 + 2 * g], shp, 1.0)
    inp['ffn1_norm'] = 1.0 + nrm(ks[10], (DEPTH, D_MODEL), 0.05)
    inp['ffn1_w_gate'] = nrm(ks[11], (DEPTH, D_MODEL, D_FF), D_MODEL ** -0.5)
    inp['ffn1_w_up'] = nrm(ks[12], (DEPTH, D_MODEL, D_FF), D_MODEL ** -0.5)
    inp['ffn1_w_down'] = nrm(ks[13], (DEPTH, D_FF, D_MODEL), D_FF ** -0.5)
    inp['mix_norm'] = 1.0 + nrm(ks[14], (DEPTH, D_MODEL), 0.05)
    inp['pool_w_in'] = nrm(ks[15], (N_POOL_LAYERS, D_MODEL, D_MODEL), D_MODEL ** -0.5)
    inp['pool_w_group'] = nrm(ks[16], (N_POOL_LAYERS, N_POOL_GROUPS, POOL_GROUP_DIM, POOL_GROUP_DIM), POOL_GROUP_DIM ** -0.5)
    inp['pool_scale'] = 1.0 + nrm(ks[17], (N_POOL_LAYERS, D_MODEL), 0.05)
    inp['pool_w_out'] = nrm(ks[18], (N_POOL_LAYERS, D_MODEL, D_MODEL), D_MODEL ** -0.5)
    inp['attn_w_qkv'] = nrm(ks[19], (N_ATTN_LAYERS, D_MODEL, QKV_WIDTH), D_MODEL ** -0.5)
    inp['attn_w_out'] = nrm(ks[20], (N_ATTN_LAYERS, GROUP_WIDTH, D_MODEL), GROUP_WIDTH ** -0.5)
    inp['rel_bias'] = nrm(ks[21], (N_BUCKETS, N_ATTN_HEADS), 0.3)
    inp['ffn2_norm'] = 1.0 + nrm(ks[22], (DEPTH, D_MODEL), 0.05)
    inp['ffn2_w_gate'] = nrm(ks[23], (DEPTH, D_MODEL, D_FF), D_MODEL ** -0.5)
    inp['ffn2_w_up'] = nrm(ks[24], (DEPTH, D_MODEL, D_FF), D_MODEL ** -0.5)
    inp['ffn2_w_down'] = nrm(ks[25], (DEPTH, D_FF, D_MODEL), D_FF ** -0.5)
    inp['final_norm'] = 1.0 + nrm(ks[26], (D_MODEL,), 0.05)
    return inp


def reference(x_prompt, x_sample, state_pool, cache_k_w128, cache_v_w128, cache_k_w512, cache_v_w512,
              cache_k_w2048, cache_v_w2048, ffn1_norm, ffn1_w_gate, ffn1_w_up, ffn1_w_down, mix_norm,
              pool_w_in, pool_w_group, pool_scale, pool_w_out, attn_w_qkv, attn_w_out, rel_bias,
              ffn2_norm, ffn2_w_gate, ffn2_w_up, ffn2_w_down, final_norm):
    cache_k = (cache_k_w128, cache_k_w512, cache_k_w2048)
    cache_v = (cache_v_w128, cache_v_w512, cache_v_w2048)
    biases = [_group_bias(rel_bias, g) for g in range(N_ATTN_GROUPS)]
    xp, xs = x_prompt, x_sample
    S = xp.shape[1]
    pool_p, pool_s = [], []
    kp = [[] for _ in range(N_ATTN_GROUPS)]
    vp = [[] for _ in range(N_ATTN_GROUPS)]
    ksn = [[] for _ in range(N_ATTN_GROUPS)]
    vsn = [[] for _ in range(N_ATTN_GROUPS)]
    for i in range(DEPTH):
        xp = _ffn_half(xp, ffn1_norm[i], ffn1_w_gate[i], ffn1_w_up[i], ffn1_w_down[i])
        xs = _ffn_half(xs, ffn1_norm[i], ffn1_w_gate[i], ffn1_w_up[i], ffn1_w_down[i])
        hp = _rmsnorm(xp, mix_norm[i])
        hs = _rmsnorm(xs, mix_norm[i])
        li = i // N_MIXERS
        if i % N_MIXERS == 0:
            zeros = jnp.zeros((hp.shape[0], POOL_STATE_LEN, D_MODEL), hp.dtype)
            yp, st_p = _pool_mixer(hp, zeros, 0, pool_w_in[li], pool_w_group[li], pool_scale[li], pool_w_out[li])
            ys, st_s = _pool_mixer(hs, state_pool[li], PAST_LEN, pool_w_in[li], pool_w_group[li],
                                   pool_scale[li], pool_w_out[li])
            pool_p.append(st_p)
            pool_s.append(st_s)
        else:
            qkv_p = _qkv(hp, attn_w_qkv[li])
            qkv_s = _qkv(hs, attn_w_qkv[li])
            outs_p, lses_p, outs_s, lses_s = [], [], [], []
            for g in range(N_ATTN_GROUPS):
                w, d = ATTN_WINDOWS[g], ATTN_DILATIONS[g]
                q, k, v = qkv_p[:, :, g, 0], qkv_p[:, :, g, 1], qkv_p[:, :, g, 2]
                o, lse = _dilated_attn_prompt(q, k, v, biases[g], w, d)
                outs_p.append(o)
                lses_p.append(lse)
                Lp = min(w, S)
                kp[g].append(k[:, S - Lp:])
                vp[g].append(v[:, S - Lp:])
                o, lse, kb, vb = _dilated_attn_step(qkv_s[:, :, g, 0], qkv_s[:, :, g, 1], qkv_s[:, :, g, 2],
                                                    cache_k[g][li], cache_v[g][li], biases[g], w, d)
                outs_s.append(o)
                lses_s.append(lse)
                ksn[g].append(kb)
                vsn[g].append(vb)
            yp = _merge_groups(outs_p, lses_p, attn_w_out[li], hp.dtype)
            ys = _merge_groups(outs_s, lses_s, attn_w_out[li], hs.dtype)
        xp = xp + yp
        xs = xs + ys
        xp = _ffn_half(xp, ffn2_norm[i], ffn2_w_gate[i], ffn2_w_up[i], ffn2_w_down[i])
        xs = _ffn_half(xs, ffn2_norm[i], ffn2_w_gate[i], ffn2_w_up[i], ffn2_w_down[i])
    y_prompt = _rmsnorm(xp, final_norm)
    y_sample = _rmsnorm(xs, final_norm)
    return (y_prompt, y_sample, jnp.stack(pool_p), jnp.stack(pool_s),
            jnp.stack(kp[0]), jnp.stack(vp[0]), jnp.stack(ksn[0]), jnp.stack(vsn[0]),
            jnp.stack(kp[1]), jnp.stack(vp[1]), jnp.stack(ksn[1]), jnp.stack(vsn[1]),
            jnp.stack(kp[2]), jnp.stack(vp[2]), jnp.stack(ksn[2]), jnp.stack(vsn[2]))
```

```python
import contextlib
import math
import numpy as np
import concourse.bass as bass
import concourse.mybir as mybir
from concourse.bass_utils import run_bass_kernel_spmd

F32 = mybir.dt.float32
BF16 = mybir.dt.bfloat16
AF = mybir.ActivationFunctionType
ALU = mybir.AluOpType
AX = mybir.AxisListType

NC = 8
D = 1024
DFF = 2816
NF = DFF // 128
FSPLIT = [(0, 8), (8, 8), (16, 6)]
TM = 2048
XT = 16
NT = TM + XT
NS = 4
NEG = -30000.0
WIN = (128, 512, 2048)
DIL = (1, 4, 16)
POOLW = (2, 4, 8, 16)
EPS = 1e-6
PADDED = ("ffn1_w_gate", "ffn1_w_up", "ffn1_w_down", "ffn2_w_gate", "ffn2_w_up", "ffn2_w_down", "pool_w_in",
          "pool_w_group", "pool_w_out", "attn_w_qkv", "attn_w_out")
DEBUG_STOP = None
CORES = list(range(8))
ATT_LVL = 9


class Prog:
    def __init__(self, nc, es):
        self.nc = nc
        self.q = {e: [] for e in ("pe", "act", "dve", "pool", "sp")}
        self.sems = {}
        for e in ("pe", "act", "dve", "pool"):
            self.sems[e] = es.enter_context(nc.semaphore("s_" + e))
        self.ecnt = {e: 0 for e in ("pe", "act", "dve", "pool")}
        self.dsems = {"sp": [], "pool": [], "act": []}
        for e, n in (("sp", 24), ("pool", 16), ("act", 8)):
            for i in range(n):
                k = "d_%s_%d" % (e, i)
                self.sems[k] = es.enter_context(nc.semaphore(k))
                self.dsems[e].append(k)
        self.dval = {k: 0 for e in self.dsems for k in self.dsems[e]}
        self.dnext = {e: 0 for e in self.dsems}
        self.waited = {e: {} for e in self.q}
        self.lastw = {}
        self.readers = {}
        self.out_tickets = []
        self.rgroup = {}
        self.rout = {}
        self.rpre = {}

    REGION = {"xst": ("A", 0), "sq": ("A", 1), "a": ("A", 2), "z": ("A", 3), "tt": ("A", 4), "ys": ("A", 5),
              "wd": ("S", 0), "sg": ("S", 0), "u": ("S", 1), "t": ("S", 1), "wkv": ("S", 2), "stg": ("S", 2),
              "h": ("H", 0), "at": ("H", 1), "ts": ("H", 1), "et": ("H", 1), "smp": ("H", 1),
              "kst": ("A", 6), "yst": ("A", 5), "tth": ("A", 4), "acc": ("A", 4), "tb": ("S", 3), "on": ("S", 4),
              "yo": ("S", 5), "ohs": ("W", 1), "wgu": ("W", 0)}

    def op(self, eng, fn, reads=(), writes=(), dma=False, out=False):
        deps = {}

        def add(k, v):
            if v > deps.get(k, 0):
                deps[k] = v

        regs = set()
        for key in list(reads) + list(writes):
            rg = self.REGION.get(key[0])
            if rg is None:
                continue
            r, grp = rg
            if self.rgroup.get(r) != grp:
                self.rpre[r] = dict(self.rout.get(r, {}))
                for k, v in self.rpre[r].items():
                    pass
                self.rout[r] = dict(self.rpre[r])
                self.rgroup[r] = grp
            regs.add(r)
        for r in regs:
            for k, v in self.rpre.get(r, {}).items():
                add(k, v)

        for key in reads:
            t = self.lastw.get(key)
            if t is not None:
                add(*t)
        for key in writes:
            t = self.lastw.get(key)
            if t is not None:
                add(*t)
            for k, v in self.readers.get(key, {}).items():
                add(k, v)
        waits = []
        wd = self.waited[eng]
        for k, v in deps.items():
            if eng == "pe" and k == "pe":
                continue
            if wd.get(k, 0) < v:
                waits.append((self.sems[k], v))
                wd[k] = v
        if dma:
            pool = self.dsems[eng]
            i = self.dnext[eng]
            self.dnext[eng] = (i + 1) % len(pool)
            k = pool[i]
            prev = self.dval[k]
            if prev > 0 and wd.get(k, 0) < prev:
                waits.append((self.sems[k], prev))
                wd[k] = prev
            val = prev + 16
            self.dval[k] = val
            tk = (k, val)
            inc = 16
        else:
            self.ecnt[eng] += 1
            tk = (eng, self.ecnt[eng])
            inc = 1
        sem = self.sems[tk[0]]

        def run(e, waits=waits, fn=fn, sem=sem, inc=inc):
            for s, v in waits:
                e.wait_ge(s, v)
            fn(e).then_inc(sem, inc)

        self.q[eng].append(run)
        for key in writes:
            self.lastw[key] = tk
            self.readers[key] = {}
        for key in reads:
            r = self.readers.setdefault(key, {})
            if r.get(tk[0], 0) < tk[1]:
                r[tk[0]] = tk[1]
        for r in regs:
            ro = self.rout.setdefault(r, {})
            if ro.get(tk[0], 0) < tk[1]:
                ro[tk[0]] = tk[1]
        if out:
            self.out_tickets.append(tk)
        return tk

    def finish(self):
        need = {}
        for k, v in self.out_tickets:
            need[k] = max(need.get(k, 0), v)
        items = [(self.sems[k], v) for k, v in need.items()]

        def run(e, items=items):
            for s, v in items:
                e.wait_ge(s, v)

        self.q["sp"].append(run)


def t5_bucket_np(dist):
    dist = np.asarray(dist, np.int32)
    distf = np.maximum(dist, 1).astype(np.float32)
    v = np.log(distf / np.float32(16.0)) / np.float32(math.log(2048 / 16)) * np.float32(16.0)
    log_b = np.minimum(16 + v.astype(np.int32), 31)
    return np.where(dist < 16, dist, log_b)


def host_consts():
    c = {}
    c["ident"] = np.eye(128, dtype=np.float32)
    oh = np.zeros((3, 2, 32, 256), np.float32)
    mk = np.zeros((2, 256), np.float32)
    ohs = np.zeros((3, 32, 128), np.float32)
    for g in range(3):
        d = DIL[g]
        for y in range(256):
            if y <= 127:
                oh[g, 0, t5_bucket_np((y + 1) * d), y] = 1.0
            j = y - 127
            if 0 <= j <= 127:
                oh[g, 1, t5_bucket_np(j * d), y] = 1.0
        for i in range(128):
            ohs[g, t5_bucket_np((128 - i) * d), i] = 1.0
    mk[0, 128:] = NEG
    mk[1, :127] = NEG
    mk[1, 255] = NEG
    c["oh"] = np.ascontiguousarray(oh.transpose(2, 0, 1, 3).reshape(32, 3 * 2 * 256))
    c["mk"] = np.ascontiguousarray(np.broadcast_to(mk.reshape(1, 512), (128, 512)))
    c["ohs"] = np.ascontiguousarray(ohs.transpose(1, 0, 2).reshape(32, 3 * 128))
    return c


def build_program():
    nc = bass.Bass("TRN2", target_bir_lowering=False)

    def din(name, shape, dt=F32):
        return nc.dram_tensor(name, list(shape), dt, kind="ExternalInput")

    def dout(name, shape, dt=F32):
        return nc.dram_tensor(name, list(shape), dt, kind="ExternalOutput")

    def dscr(name, shape, dt):
        return nc.dram_tensor(name, list(shape), dt, kind="Internal")

    xa_d = din("xa", [NT, D]).ap()
    xb_d = din("xb", [NT, D]).ap()
    stp_d = din("stp", [NS, 15, D]).ap()
    ck_d = [din("ck%d" % g, [NS, WIN[g], 512]).ap() for g in range(3)]
    cv_d = [din("cv%d" % g, [NS, WIN[g], 512]).ap() for g in range(3)]
    w = {}
    for nm, shp in (("ffn1_norm", [2, D]), ("ffn1_w_gate", [2, D, DFF]), ("ffn1_w_up", [2, D, DFF]),
                    ("ffn1_w_down", [2, DFF, D]), ("mix_norm", [2, D]), ("pool_w_in", [1, D, D]),
                    ("pool_w_group", [1, 4, 256, 256]), ("pool_scale", [1, D]), ("pool_w_out", [1, D, D]),
                    ("attn_w_qkv", [1, D, 4608]), ("attn_w_out", [1, 512, D]), ("rel_bias", [32, 24]),
                    ("ffn2_norm", [2, D]), ("ffn2_w_gate", [2, D, DFF]), ("ffn2_w_up", [2, D, DFF]),
                    ("ffn2_w_down", [2, DFF, D]), ("final_norm", [1, D])):
        if nm in PADDED:
            rows = int(np.prod(shp[:-1]))
            flat = din(nm, [rows + 1, shp[-1]]).ap()[0:rows, :]
            if len(shp) == 3:
                w[nm] = flat.rearrange("(l k) n -> l k n", l=shp[0])
            else:
                w[nm] = flat.rearrange("(l g k) n -> l g k n", l=shp[0], g=shp[1])
        else:
            w[nm] = din(nm, shp).ap()
    ident_d = din("ident", [128, 128]).ap()
    oh_d = din("oh", [32, 1536]).ap()
    mk_d = din("mk", [128, 512]).ap()
    ohs_d = din("ohs", [32, 384]).ap()
    rc_d = din("rc", [128, 128]).ap()
    hm_d = din("hm", [128, 1]).ap()

    y_d = dout("y", [NT, D]).ap()
    poolp_d = dout("poolp", [15, D]).ap()
    pools_d = dout("pools", [NS, 15, D]).ap()
    kp_d = [dout("kp%d" % g, [WIN[g], 512]).ap() for g in range(3)]
    vp_d = [dout("vp%d" % g, [WIN[g], 512]).ap() for g in range(3)]
    ks_d = [dout("ks%d" % g, [NS, WIN[g], 512]).ap() for g in range(3)]
    vs_d = [dout("vs%d" % g, [NS, WIN[g], 512]).ap() for g in range(3)]
    dbg_d = dout("dbg", [128, 8 * NT]).ap() if DEBUG_STOP else None

    kt_s = dscr("kt_s", [2, 3, 4, 128, TM], BF16).ap()
    v_s = dscr("v_s", [2, 3, 16, 128, 512], BF16).ap()
    q_s = dscr("q_s", [3, 4, 128, TM], BF16).ap()
    tt_s = dscr("tt_s", [6, 128, 2048], F32)

    es = contextlib.ExitStack()
    with es:
        BIG_WORDS = 44992
        big = es.enter_context(nc.sbuf_tensor("big", [128, BIG_WORDS], F32))
        ptr = [0]

        def carve(nwords):
            o = ptr[0]
            ptr[0] += nwords
            assert ptr[0] <= BIG_WORDS, ptr[0]
            return o

        def vf(off, n):
            return big[:, off:off + n]

        def vb(off, n):
            return big[:, off:off + n // 2].bitcast(BF16)

        oX = carve(8 * NT)
        oH = carve(8 * NT // 2)
        oA = carve(8 * NT // 2)
        oW = carve(2 * 2 * 8 * 128 // 2)
        oS = carve(6208)
        oR = carve(2 * 512)
        oC = carve(1664)
        X = vf(oX, 8 * NT).rearrange("p (c t) -> p c t", c=8)
        HB = vb(oH, 8 * NT).rearrange("p (c t) -> p c t", c=8)
        A = vb(oA, 8 * NT).rearrange("p (c t) -> p c t", c=8)
        WGU = vb(oW, 4096).rearrange("p (s j k n) -> p s j k n", s=2, j=2, k=8)
        WD = vb(oS, 8 * 1024).rearrange("p (f n) -> p f n", f=8)
        SG = vf(oS + 4096, 1024).rearrange("p (s n) -> p s n", s=2)
        RS = vf(oR, 1024).rearrange("p (s n) -> p s n", s=2)
        co = [oC]

        def cf(n):
            o = co[0]
            co[0] += n
            assert co[0] <= oC + 1664
            return vf(o, n)

        IDENT = cf(128)
        ONESB = cf(64).bitcast(BF16)
        IDB = cf(64).bitcast(BF16)
        NRM = cf(64).rearrange("p (v c) -> p v c", v=8)
        RCT = cf(128).rearrange("p (a g n) -> p a g n", a=2, g=4)
        HM = cf(1)
        UPRE = cf(120).rearrange("p (c n) -> p c n", c=8)
        US = cf(512).rearrange("p (c s n) -> p c s n", c=8, s=4)
        QS = cf(24).bitcast(BF16).rearrange("p (g h s) -> p g h s", g=3, h=4)
        KS = cf(24).bitcast(BF16).rearrange("p (g h s) -> p g h s", g=3, h=4)
        KSF = cf(48).rearrange("p (g h s) -> p g h s", g=3, h=4)
        VSF = cf(48).rearrange("p (g h s) -> p g h s", g=3, h=4)
        RB = cf(24)
        BS = cf(24).rearrange("p (g h) -> p g h", g=3)
        B0 = cf(24)
        ES = cf(96)
        E0 = cf(96)
        ESB = cf(48).bitcast(BF16)
        E0B = cf(48).bitcast(BF16)

        PS = [es.enter_context(nc.psum_tensor("ps%d" % i, [128, 512], F32)) for i in range(8)]
        p = Prog(nc, es)

        TILES = [(0, 512), (512, 512), (1024, 512), (1536, 512), (TM, XT)]

        def mm_group(out_ap, pairs, reads, writes):
            n = len(pairs)

            def fn(e):
                ins = None
                for i, (l, r) in enumerate(pairs):
                    ins = e.matmul(out_ap, lhsT=l, rhs=r, start=(i == 0), stop=(i == n - 1))
                return ins

            return p.op("pe", fn, reads=reads, writes=writes)

        def dma(eng, out_ap, in_ap, reads, writes, out=False):
            return p.op(eng, lambda e: e.dma_start(out=out_ap, in_=in_ap), reads=reads, writes=writes, dma=True,
                        out=out)

        def wchunk(wap, c0, ncols=128):
            return wap[:, c0:c0 + ncols].rearrange("(k p) n -> p k n", p=128)

        slot_ctr = [0]

        def next_slot():
            s = slot_ctr[0]
            slot_ctr[0] ^= 1
            return s

        def load_wslot(wap, c0, s, j, nk=8):
            dma("pool", WGU[:, s, j, 0:nk, :], wchunk(wap, c0), reads=[], writes=[("wgu", s, j)])

        psr = [0]

        def next_bank(lo=0, hi=6):
            b = lo + psr[0] % (hi - lo)
            psr[0] += 1
            return b

        dma("sp", IDENT, ident_d, [], [("ident",)])
        p.op("dve", lambda e: e.memset(ONESB, 1.0), writes=[("ones",)])
        p.op("dve", lambda e: e.tensor_copy(out=IDB, in_=IDENT), reads=[("ident",)], writes=[("idb",)])
        nvec = [w["ffn1_norm"][0], w["ffn1_norm"][1], w["ffn2_norm"][0], w["ffn2_norm"][1], w["mix_norm"][0],
                w["mix_norm"][1], w["pool_scale"][0], w["final_norm"][0]]
        for i, v in enumerate(nvec):
            p.op("sp", lambda e, i=i, v=v: e.dma_start(out=NRM[:, i, :], in_=v.rearrange("(c p) -> p c", p=128),
                                                       allow_slow_non_contiguous=True),
                 writes=[("nrm", i)], dma=True)
        dma("sp", RCT, rc_d.rearrange("p (a g n) -> p a g n", a=2, g=4), [], [("rct",)])
        dma("sp", HM, hm_d, [], [("hm",)])
        dma("sp", RB[0:32, :], w["rel_bias"], [], [("rb",)])
        dma("sp", B0[0:1, :], w["rel_bias"][0:1, :], [], [("b0",)])
        NV = {"f1l0": 0, "f1l1": 1, "f2l0": 2, "f2l1": 3, "mix0": 4, "mix1": 5, "pscale": 6, "final": 7}

        def load_x(x_d):
            XST = vf(oA, 2048).rearrange("p (s n) -> p s n", s=2)
            for tb in range(17):
                rows = 128 if tb < 16 else XT
                s = tb % 2
                dma("sp", XST[0:rows, s, :], x_d[tb * 128:tb * 128 + rows, :], [], [("xst", s)])
                for half in range(2):
                    b = next_bank()

                    def fn(e, s=s, rows=rows, half=half, b=b):
                        ins = None
                        for cc in range(4):
                            c = half * 4 + cc
                            ins = e.transpose(PS[b][:, cc * 128:cc * 128 + rows],
                                              in_=XST[0:rows, s, c * 128:(c + 1) * 128],
                                              identity=IDENT[0:rows, 0:rows])
                        return ins

                    p.op("pe", fn, reads=[("xst", s), ("ident",)], writes=[("ps", b)])
                    t = min(tb // 4, 4)
                    src = PS[b][:, :].rearrange("p (c n) -> p c n", c=4)[:, :, 0:rows]
                    dst = X[:, half * 4:half * 4 + 4, tb * 128:tb * 128 + rows]
                    eng = "act" if half == 0 else "dve"
                    if eng == "act":
                        p.op("act", lambda e, dst=dst, src=src: e.copy(out=dst, in_=src), reads=[("ps", b)],
                             writes=[("x", c, t) for c in range(half * 4, half * 4 + 4)])
                    else:
                        p.op("dve", lambda e, dst=dst, src=src: e.tensor_copy(out=dst, in_=src), reads=[("ps", b)],
                             writes=[("x", c, t) for c in range(half * 4, half * 4 + 4)])

        def rmsnorm(vec, tiles, dst=None, dst_keys="h", f32_out=None):
            SQ = vb(oA, 8 * 512 * 2).rearrange("p (s c n) -> p s c n", s=2, c=8)
            for ti in tiles:
                c0, n = TILES[ti]
                s = ti % 2
                p.op("act", lambda e, s=s, c0=c0, n=n: e.activation(out=SQ[:, s, :, 0:n], in_=X[:, :, c0:c0 + n],
                                                                     func=AF.Square),
                     reads=[("x", c, ti) for c in range(8)], writes=[("sq", s)])
                b = 6
                mm_group(PS[b][:, 0:n], [(ONESB, SQ[:, s, c, 0:n]) for c in range(8)],
                         reads=[("sq", s), ("ones",)], writes=[("ps", b)])
                p.op("act", lambda e, s=s, n=n, b=b: e.activation(out=RS[:, s, 0:n], in_=PS[b][:, 0:n], func=AF.Sqrt,
                                                                   scale=1.0 / D, bias=EPSB),
                     reads=[("ps", b), ("epsb",)], writes=[("rs", s)])
                p.op("dve", lambda e, s=s, n=n: e.reciprocal(out=RS[:, s, 0:n], in_=RS[:, s, 0:n]),
                     reads=[("rs", s)], writes=[("rs", s)])
                for c in range(8):
                    if f32_out is None:
                        o = HB[:, c, c0:c0 + n]
                        wk = [("h", c, ti)]
                    else:
                        o = f32_out(c, ti)
                        wk = [("yst", c)]
                    p.op("dve", lambda e, o=o, c=c, c0=c0, n=n, s=s: e.scalar_tensor_tensor(
                        out=o, in0=X[:, c, c0:c0 + n], scalar=NRM[:, vec, c:c + 1], in1=RS[:, s, 0:n],
                        op0=ALU.mult, op1=ALU.mult),
                         reads=[("x", c, ti), ("rs", s), ("nrm", vec)], writes=wk)

        def ffn(wg, wu, wd_, vec, tiles):
            rmsnorm(vec, tiles)
            for (f0, nf) in FSPLIT:
                dma("pool", WD[:, 0:nf, :], wd_[f0 * 128:(f0 + nf) * 128, :].rearrange("(f p) n -> p f n", p=128),
                    [], [("wd",)])
                for fi in range(nf):
                    f = f0 + fi
                    s = next_slot()
                    load_wslot(wg, f * 128, s, 0)
                    load_wslot(wu, f * 128, s, 1)
                    for ti in tiles:
                        c0, n = TILES[ti]
                        bg = (psr[0] % 2)
                        bu = 2 + (psr[0] % 2)
                        psr[0] += 1
                        hk = [("h", c, ti) for c in range(8)]
                        mm_group(PS[bg][:, 0:n], [(WGU[:, s, 0, k, :], HB[:, k, c0:c0 + n]) for k in range(8)],
                                 reads=hk + [("wgu", s, 0)], writes=[("ps", bg)])
                        mm_group(PS[bu][:, 0:n], [(WGU[:, s, 1, k, :], HB[:, k, c0:c0 + n]) for k in range(8)],
                                 reads=hk + [("wgu", s, 1)], writes=[("ps", bu)])
                        sg = bg
                        p.op("act", lambda e, sg=sg, bg=bg, n=n: e.activation(out=SG[:, sg, 0:n], in_=PS[bg][:, 0:n],
                                                                              func=AF.Silu),
                             reads=[("ps", bg)], writes=[("sg", sg)])
                        p.op("dve", lambda e, sg=sg, bu=bu, n=n, fi=fi, c0=c0: e.tensor_tensor(
                            out=A[:, fi, c0:c0 + n], in0=SG[:, sg, 0:n], in1=PS[bu][:, 0:n], op=ALU.mult),
                             reads=[("sg", sg), ("ps", bu)], writes=[("a", fi, ti)])
                for m in range(8):
                    for ti in tiles:
                        c0, n = TILES[ti]
                        b = 4 + (psr[0] % 2)
                        psr[0] += 1
                        mm_group(PS[b][:, 0:n],
                                 [(WD[:, fi, m * 128:(m + 1) * 128], A[:, fi, c0:c0 + n]) for fi in range(nf)],
                                 reads=[("a", fi, ti) for fi in range(nf)] + [("wd",)], writes=[("ps", b)])
                        p.op("dve", lambda e, b=b, n=n, m=m, c0=c0: e.scalar_tensor_tensor(
                            out=X[:, m, c0:c0 + n], in0=PS[b][:, 0:n], scalar=0.5, in1=X[:, m, c0:c0 + n],
                            op0=ALU.mult, op1=ALU.add),
                             reads=[("ps", b), ("x", m, ti)], writes=[("x", m, ti)])

        def pool_mixer(pas):
            LU = NT
            U = vf(oS, LU)
            T1 = vf(oS + LU, LU)
            T2 = vf(oS + 2 * LU, LU)
            Z = A
            tiles = [0, 1, 2, 3, 4]
            rmsnorm(NV["mix0"], tiles)
            p.op("pool", lambda e: e.memset(U[:, 0:1], 0.0), writes=[("u",)])
            if pas == 1:
                ST = vf(oS + 3 * LU - 1024, 1024)
                for s in range(NS):
                    dma("sp", ST[0:15, :], stp_d[s], [], [("t", 1)])
                    for half in range(2):
                        b = next_bank()

                        def fn(e, half=half, b=b):
                            ins = None
                            for cc in range(4):
                                c = half * 4 + cc
                                ins = e.transpose(PS[b][:, cc * 128:cc * 128 + 15], in_=ST[0:15, c * 128:(c + 1) * 128],
                                                  identity=IDENT[0:15, 0:15])
                            return ins

                        p.op("pe", fn, reads=[("t", 1), ("ident",)], writes=[("ps", b)])
                        p.op("act", lambda e, half=half, b=b, s=s: e.copy(
                            out=US[:, half * 4:half * 4 + 4, s, 0:15],
                            in_=PS[b][:, :].rearrange("p (c n) -> p c n", c=4)[:, :, 0:15]),
                             reads=[("ps", b)], writes=[("us", s, half)])
                for s in range(NS):
                    dma("sp", pools_d[s, 0:14, :], stp_d[s, 1:15, :], [], [], out=True)
            for c in range(8):
                g = c // 2
                wwin = POOLW[g]
                s = next_slot()
                load_wslot(w["pool_w_in"][0], c * 128, s, 0)
                for ti in tiles:
                    c0, n = TILES[ti]
                    b = next_bank(0, 4)
                    mm_group(PS[b][:, 0:n], [(WGU[:, s, 0, k, :], HB[:, k, c0:c0 + n]) for k in range(8)],
                             reads=[("h", k, ti) for k in range(8)] + [("wgu", s, 0)], writes=[("ps", b)])
                    if ti < 4:
                        p.op("act", lambda e, b=b, c0=c0, n=n: e.copy(out=U[:, 16 + c0:16 + c0 + n], in_=PS[b][:, 0:n]),
                             reads=[("ps", b)], writes=[("u",)])
                    elif pas == 0:
                        p.op("act", lambda e, b=b: e.copy(out=U[:, 1:16], in_=PS[b][:, 0:15]),
                             reads=[("ps", b)], writes=[("u",)])
                    else:
                        p.op("act", lambda e, b=b, c=c: e.copy(out=US[:, c, :, 15], in_=PS[b][:, 0:4]),
                             reads=[("ps", b)] + [("us", s_, c // 4) for s_ in range(NS)],
                             writes=[("usn", c)] + [("us", s_, c // 4) for s_ in range(NS)])
                if pas == 0:
                    p.op("act", lambda e, c=c: e.copy(out=UPRE[:, c, :], in_=U[:, 16 + TM - 15:16 + TM]),
                         reads=[("u",)], writes=[("upre", c)])
                else:
                    p.op("act", lambda e, c=c: e.copy(out=U[:, 1:16], in_=UPRE[:, c, :]),
                         reads=[("upre", c)], writes=[("u",)])
                    b = 7
                    p.op("pe", lambda e, b=b: e.transpose(PS[b][0:15, 0:128], in_=U[:, 16 + TM - 15:16 + TM],
                                                           identity=IDENT),
                         reads=[("u",), ("ident",)], writes=[("ps", b)])
                    p.op("act", lambda e, b=b, c=c: e.copy(out=PST[0:15, c * 128:(c + 1) * 128], in_=PS[b][0:15, 0:128]),
                         reads=[("ps", b)], writes=[("pst", c)])
                src = U
                bufs = [T1, T2]
                sh = 1
                lo = 1
                for lvl in range(g + 1):
                    dstb = bufs[lvl % 2]
                    lo2 = lo + sh
                    p.op("pool", lambda e, dstb=dstb, src=src, lo2=lo2, sh=sh: e.tensor_tensor(
                        out=dstb[:, lo2:LU], in0=src[:, lo2:LU], in1=src[:, lo2 - sh:LU - sh], op=ALU.add),
                         reads=[("u",)] if lvl == 0 else [("t", (lvl - 1) % 2)],
                         writes=[("t", lvl % 2)])
                    src = dstb
                    lo = lo2
                    sh *= 2
                lastk = ("t", g % 2)
                p.op("dve", lambda e, src=src, c=c, wwin=wwin: e.scalar_tensor_tensor(
                    out=Z[:, c, 0:TM], in0=src[:, 16:16 + TM], scalar=1.0 / wwin, in1=U[:, 16:16 + TM],
                    op0=ALU.mult, op1=ALU.subtract),
                     reads=[lastk, ("u",)], writes=[("z", c)])
                p.op("dve", lambda e, src=src, g=g: e.tensor_tensor(out=RS[:, 0, 0:16], in0=src[:, 16:32],
                                                                     in1=RCT[:, pas, g, :], op=ALU.mult),
                     reads=[lastk, ("rct",)], writes=[("rs", 0)])
                p.op("dve", lambda e, c=c: e.tensor_tensor(out=Z[:, c, 0:16], in0=RS[:, 0, 0:16], in1=U[:, 16:32],
                                                            op=ALU.subtract),
                     reads=[("rs", 0), ("u",)], writes=[("z", c)])
                if pas == 1:
                    p.op("dve", lambda e, c=c, wwin=wwin: e.tensor_reduce(out=RS[:, 1, 0:4], in_=US[:, c, :, 16 - wwin:16],
                                                                          axis=AX.X, op=ALU.add),
                         reads=[("usn", c)], writes=[("rs", 1)])
                    p.op("dve", lambda e, c=c, wwin=wwin: e.scalar_tensor_tensor(
                        out=Z[:, c, TM:TM + 4], in0=RS[:, 1, 0:4], scalar=1.0 / wwin, in1=US[:, c, :, 15],
                        op0=ALU.mult, op1=ALU.subtract),
                         reads=[("rs", 1), ("usn", c)], writes=[("z", c)])
                    p.op("dve", lambda e, c=c: e.memset(Z[:, c, TM + 4:NT], 0.0), writes=[("z", c)])
            if pas == 1:
                dma("sp", poolp_d, PST[0:15, :], [("pst", c) for c in range(8)], [], out=True)
                for half in range(2):
                    b = next_bank()

                    def fn(e, half=half, b=b):
                        ins = None
                        for cc in range(4):
                            c = half * 4 + cc
                            ins = e.transpose(PS[b][0:4, cc * 128:(cc + 1) * 128], in_=US[:, c, :, 15], identity=IDENT)
                        return ins

                    p.op("pe", fn, reads=[("usn", c) for c in range(8)] + [("ident",)], writes=[("ps", b)])
                    p.op("dve", lambda e, half=half, b=b: e.tensor_copy(out=PST[32:36, half * 512:(half + 1) * 512],
                                                                        in_=PS[b][0:4, :]),
                         reads=[("ps", b)], writes=[("pst2", half)])
                dma("sp", pools_d[:, 14, :], PST[32:36, :], [("pst2", 0), ("pst2", 1)], [], out=True)
            mt = [0, 1, 2, 3] + ([4] if pas == 1 else [])
            for c in range(8):
                g = c // 2
                s = next_slot()
                dma("pool", WGU[:, s, 0, 0:2, :],
                    w["pool_w_group"][0, g][:, (c % 2) * 128:(c % 2) * 128 + 128].rearrange("(k p) n -> p k n", p=128),
                    [], [("wgu", s, 0)])
                for ti in mt:
                    c0, n = TILES[ti]
                    b = next_bank(0, 4)
                    mm_group(PS[b][:, 0:n], [(WGU[:, s, 0, k, :], Z[:, 2 * g + k, c0:c0 + n]) for k in range(2)],
                             reads=[("z", 2 * g), ("z", 2 * g + 1), ("wgu", s, 0)], writes=[("ps", b)])
                    p.op("act", lambda e, b=b, n=n, c=c, c0=c0: e.activation(
                        out=HB[:, c, c0:c0 + n], in_=PS[b][:, 0:n], func=AF.Copy, scale=NRM[:, NV["pscale"], c:c + 1]),
                         reads=[("ps", b), ("nrm", NV["pscale"])], writes=[("h", c, ti)])
            for m in range(8):
                s = next_slot()
                load_wslot(w["pool_w_out"][0], m * 128, s, 0)
                for ti in mt:
                    c0, n = TILES[ti]
                    b = 4 + (psr[0] % 2)
                    psr[0] += 1
                    mm_group(PS[b][:, 0:n], [(WGU[:, s, 0, k, :], HB[:, k, c0:c0 + n]) for k in range(8)],
                             reads=[("h", k, ti) for k in range(8)] + [("wgu", s, 0)], writes=[("ps", b)])
                    p.op("dve", lambda e, b=b, n=n, m=m, c0=c0: e.tensor_tensor(
                        out=X[:, m, c0:c0 + n], in0=PS[b][:, 0:n], in1=X[:, m, c0:c0 + n], op=ALU.add),
                         reads=[("ps", b), ("x", m, ti)], writes=[("x", m, ti)])


        flip = {"kst": 0, "vst": 0, "ts": 0, "acc": 0}

        def attn_qkv(pas):
            tiles = [0, 1, 2, 3] + ([4] if pas == 1 else [])
            rmsnorm(NV["mix1"], tiles)
            wq = w["attn_w_qkv"][0]
            KST = vb(oA, 2 * TM).rearrange("p (s n) -> p s n", s=2)
            WKV = vb(oS, 8 * 1024).rearrange("p (k n) -> p k n", k=8)
            VST = vb(oS + 4096, 1024).rearrange("p (s n) -> p s n", s=2)
            KVF = vf(oS + 4608, 1024)
            hall = [("h", k, ti) for k in range(8) for ti in range(4)]
            for g in range(3):
                d = DIL[g]
                nb = 16 // d
                for which in ((0, 1) if pas == 1 else (1,)):
                    for hp in range(4):
                        col = g * 1536 + which * 512 + hp * 128
                        s = next_slot()
                        load_wslot(wq, col, s, 0)
                        ks = flip["kst"]
                        flip["kst"] ^= 1
                        for ti in tiles:
                            c0, n = TILES[ti]
                            b = next_bank(0, 4)
                            mm_group(PS[b][:, 0:n], [(WGU[:, s, 0, k, :], HB[:, k, c0:c0 + n]) for k in range(8)],
                                     reads=[("h", k, ti) for k in range(8)] + [("wgu", s, 0)], writes=[("ps", b)])
                            if ti < 4:
                                dst = KST[:, ks, :].rearrange("p (r m) -> p r m", r=d)[:, :, c0 // d:(c0 + n) // d]
                                src = PS[b][:, 0:n].rearrange("p (m r) -> p r m", r=d)
                                if ti % 2 == 0:
                                    p.op("act", lambda e, dst=dst, src=src: e.copy(out=dst, in_=src),
                                         reads=[("ps", b)], writes=[("kst", ks)])
                                else:
                                    p.op("dve", lambda e, dst=dst, src=src: e.tensor_copy(out=dst, in_=src),
                                         reads=[("ps", b)], writes=[("kst", ks)])
                            elif which == 0:
                                p.op("dve", lambda e, b=b, g=g, hp=hp: e.tensor_copy(out=QS[:, g, hp, :], in_=PS[b][:, 0:4]),
                                     reads=[("ps", b)], writes=[("qs", g, hp)])
                            else:
                                p.op("dve", lambda e, b=b, g=g, hp=hp: e.tensor_copy(out=KS[:, g, hp, :], in_=PS[b][:, 0:4]),
                                     reads=[("ps", b)], writes=[("ks", g, hp)])
                                p.op("dve", lambda e, b=b, g=g, hp=hp: e.tensor_copy(out=KSF[:, g, hp, :], in_=PS[b][:, 0:4]),
                                     reads=[("ps", b)], writes=[("ksf", g, hp)])
                        dd = q_s[g, hp] if which == 0 else kt_s[pas, g, hp]
                        dma("sp", dd, KST[:, ks, :], [("kst", ks)], [("qk_s", pas, g, hp, which)])
                dma("pool", WKV, wq[:, g * 1536 + 512:g * 1536 + 1536].rearrange("(k p) n -> p k n", p=128), [],
                    [("wkv",)])
                W = WIN[g]
                for blk in range(16):
                    r, n_ = blk // nb, blk % nb
                    start = n_ * 128 * d + r
                    lhs = [HB[:, k, start:start + 127 * d + 1:d] for k in range(8)]
                    need_out = (pas == 1) and (start >= TM - W)
                    bv = next_bank(0, 4)
                    mm_group(PS[bv][:, :], [(lhs[k], WKV[:, k, 512:1024]) for k in range(8)],
                             reads=hall + [("wkv",)], writes=[("ps", bv)])
                    vs = flip["vst"]
                    flip["vst"] ^= 1
                    p.op("act", lambda e, vs=vs, bv=bv: e.copy(out=VST[:, vs, :], in_=PS[bv][:, :]),
                         reads=[("ps", bv)], writes=[("stg", "v", vs)])
                    dma("sp", v_s[pas, g, blk], VST[:, vs, :], [("stg", "v", vs)], [("v_s", pas, g)])
                    if need_out:
                        bk = next_bank(0, 4)
                        mm_group(PS[bk][:, :], [(lhs[k], WKV[:, k, 0:512]) for k in range(8)],
                                 reads=hall + [("wkv",)], writes=[("ps", bk)])
                        p.op("dve", lambda e, bk=bk: e.tensor_copy(out=KVF[:, 0:512], in_=PS[bk][:, :]),
                             reads=[("ps", bk)], writes=[("stg", "kf")])
                        p.op("dve", lambda e, bv=bv: e.tensor_copy(out=KVF[:, 512:1024], in_=PS[bv][:, :]),
                             reads=[("ps", bv)], writes=[("stg", "vf")])
                        t0 = start - (TM - W)
                        dma("sp", kp_d[g][t0:t0 + 127 * d + 1:d, :], KVF[:, 0:512], [("stg", "kf")], [], out=True)
                        dma("sp", vp_d[g][t0:t0 + 127 * d + 1:d, :], KVF[:, 512:1024], [("stg", "vf")], [], out=True)
                if pas == 1:
                    for hp in range(4):
                        b = next_bank(0, 4)
                        mm_group(PS[b][:, 0:XT],
                                 [(WKV[:, k, 512 + hp * 128:512 + (hp + 1) * 128], HB[:, k, TM:TM + XT]) for k in range(8)],
                                 reads=[("h", k, 4) for k in range(8)] + [("wkv",)], writes=[("ps", b)])
                        p.op("dve", lambda e, b=b, g=g, hp=hp: e.tensor_copy(out=VSF[:, g, hp, :], in_=PS[b][:, 0:4]),
                             reads=[("ps", b)], writes=[("vsf", g, hp)])

        def build_tt():
            OH = vf(oS, 1536)
            MKT = vf(oS + 1536, 512)
            TST = vf(oS + 2048, 2048)
            dma("sp", OH[0:32, :], oh_d, [], [("tb", "oh")])
            dma("sp", MKT, mk_d, [], [("tb", "mk")])
            for g in range(3):
                for cp in range(2):
                    def fn(e, g=g, cp=cp):
                        ins = None
                        for h in range(8):
                            ins = e.matmul(PS[h // 2][:, (h % 2) * 256:(h % 2) * 256 + 256],
                                           lhsT=RB[0:32, 8 * g + h:8 * g + h + 1].broadcast_to([32, 128]),
                                           rhs=OH[0:32, (g * 2 + cp) * 256:(g * 2 + cp) * 256 + 256], start=True, stop=True)
                        return ins

                    p.op("pe", fn, reads=[("rb",), ("tb", "oh")], writes=[("ps", b) for b in range(4)])
                    for h in range(8):
                        p.op("dve", lambda e, h=h, cp=cp: e.tensor_tensor(
                            out=TST[:, h * 256:(h + 1) * 256], in0=PS[h // 2][:, (h % 2) * 256:(h % 2) * 256 + 256],
                            in1=MKT[:, cp * 256:(cp + 1) * 256], op=ALU.add),
                             reads=[("ps", h // 2), ("tb", "mk")], writes=[("tb", "tst")])
                    dma("sp", tt_s.ap()[g * 2 + cp], TST, [("tb", "tst")], [("tt_s", g, cp)])

        def attention():
            KTo = vb(oH, TM)
            KTh = vb(oH + 1024, TM)
            Q = vb(oH + 2048, TM)
            Vo = vb(oH + 3072, TM).rearrange("p (b f) -> p b f", b=16)
            Vh = vb(oH + 4096, TM).rearrange("p (b f) -> p b f", b=16)
            ET = vb(oH + 5120, 1024).rearrange("p (s n) -> p s n", s=2)
            TS = vf(oH + 5632, 1024).rearrange("p (s n) -> p s n", s=2)
            TTH = vf(oA, 3072).rearrange("p (s g k j a) -> p s g k j a", s=2, g=3, k=2, j=2)
            OACC = vf(oA + 3072, TM)
            DACC = vf(oA + 3072 + TM, TM)
            ON = vb(oS, 4 * NT).rearrange("p (k t) -> p k t", k=4)
            p.op("pool", lambda e: e.memset(ON[:, :, TM:NT], 0.0), writes=[("on", "x")])
            for hp in range(4):
                sl = hp % 2
                for g in range(3):
                    for cp in range(2):
                        src = bass.AP(tt_s, (g * 2 + cp) * 128 * 2048 + 127 + hp * 512, [[2047, 128], [256, 2], [1, 128]])
                        dma("sp", TTH[:, sl, g, cp, :, :], src, [("tt_s", g, cp)], [("tth", sl)])
                for g in range(3):
                    d = DIL[g]
                    nb = 16 // d
                    dma("sp", KTo, kt_s[1, g, hp], [("qk_s", 1, g, hp, 1)], [("at", "kto")])
                    dma("sp", KTh, kt_s[0, g, hp], [("qk_s", 0, g, hp, 1)], [("at", "kth")])
                    dma("sp", Q, q_s[g, hp], [("qk_s", 1, g, hp, 0)], [("at", "q")])
                    dma("sp", Vo, v_s[1, g].rearrange("b a f -> a b f")[:, :, hp * 128:(hp + 1) * 128], [("v_s", 1, g)],
                        [("at", "vo")])
                    dma("sp", Vh, v_s[0, g].rearrange("b a f -> a b f")[:, :, hp * 128:(hp + 1) * 128], [("v_s", 0, g)],
                        [("at", "vh")])
                    for blk4 in range(4 if ATT_LVL >= 1 else 0):
                        bo = 4 + 2 * flip["acc"]
                        bd = bo + 1
                        flip["acc"] ^= 1
                        for i in range(4):
                            blk = blk4 * 4 + i
                            r, n_ = blk // nb, blk % nb
                            halo = (n_ == 0)
                            if halo:
                                pb = r * nb + nb - 1
                                prevK, prevV = KTh[:, pb * 128:(pb + 1) * 128], Vh[:, pb, :]
                            else:
                                prevK, prevV = KTo[:, (blk - 1) * 128:blk * 128], Vo[:, blk - 1, :]
                            curK, curV = KTo[:, blk * 128:(blk + 1) * 128], Vo[:, blk, :]
                            qc = Q[:, blk * 128:(blk + 1) * 128]
                            bs0 = next_bank(0, 4)
                            bs1 = next_bank(0, 4)
                            bsl = (bs0, bs1)

                            def fn(e, prevK=prevK, curK=curK, qc=qc, bsl=bsl):
                                ins = None
                                for hh in range(2):
                                    for kb, KK in enumerate((prevK, curK)):
                                        ins = e.matmul(PS[bsl[hh]][:, kb * 128:(kb + 1) * 128],
                                                       lhsT=KK[64 * hh:64 * hh + 64, :], rhs=qc[64 * hh:64 * hh + 64, :],
                                                       start=True, stop=True)
                                return ins

                            p.op("pe", fn, reads=[("at", "kto"), ("at", "kth"), ("at", "q")],
                                 writes=[("ps", bs0), ("ps", bs1)])
                            if ATT_LVL < 2:
                                continue
                            ts = flip["ts"]
                            flip["ts"] ^= 1
                            for hh in range(2):
                                p.op("dve", lambda e, ts=ts, hh=hh, bsl=bsl, sl=sl, g=g: e.scalar_tensor_tensor(
                                    out=TS[:, ts, hh * 256:(hh + 1) * 256].rearrange("p (k a) -> p k a", k=2),
                                    in0=PS[bsl[hh]][:, 0:256].rearrange("p (k a) -> p k a", k=2), scalar=0.125,
                                    in1=TTH[:, sl, g, :, hh, :], op0=ALU.mult, op1=ALU.add),
                                     reads=[("ps", bsl[hh]), ("tth", sl)], writes=[("ts", ts, hh)])
                            if ATT_LVL < 3:
                                continue
                            tsk = [("ts", ts, 0), ("ts", ts, 1)]
                            if halo:
                                TSv = TS[:, ts, :].rearrange("p (h k a) -> p h k a", h=2, k=2)
                                ETv = ET[:, ts, :].rearrange("p (h k a) -> p h k a", h=2, k=2)
                                p.op("act", lambda e, TSv=TSv, ETv=ETv: e.activation(out=ETv[:, :, 0, :], in_=TSv[:, :, 0, :],
                                                                                     func=AF.Exp, bias=HM),
                                     reads=tsk + [("hm",)], writes=[("et", ts, 0)])
                                p.op("act", lambda e, TSv=TSv, ETv=ETv: e.activation(out=ETv[:, :, 1, :], in_=TSv[:, :, 1, :],
                                                                                     func=AF.Exp),
                                     reads=tsk, writes=[("et", ts, 1)])
                            else:
                                p.op("act", lambda e, ts=ts: e.activation(out=ET[:, ts, :], in_=TS[:, ts, :], func=AF.Exp),
                                     reads=tsk, writes=[("et", ts, 0), ("et", ts, 1)])
                            if ATT_LVL < 4:
                                continue

                            def fn2(e, prevV=prevV, curV=curV, ts=ts, bo=bo, bd=bd, i=i):
                                ins = None
                                for hh in range(2):
                                    for kb, VV in enumerate((prevV, curV)):
                                        ins = e.matmul(PS[bo][64 * hh:64 * hh + 64, i * 128:(i + 1) * 128],
                                                       lhsT=VV[:, 64 * hh:64 * hh + 64],
                                                       rhs=ET[:, ts, (hh * 2 + kb) * 128:(hh * 2 + kb + 1) * 128],
                                                       start=(kb == 0), stop=(kb == 1))
                                for hh in range(2):
                                    for kb in range(2):
                                        ins = e.matmul(PS[bd][64 * hh:64 * hh + 64, i * 128:(i + 1) * 128],
                                                       lhsT=ONESB[:, 0:64],
                                                       rhs=ET[:, ts, (hh * 2 + kb) * 128:(hh * 2 + kb + 1) * 128],
                                                       start=(kb == 0), stop=(kb == 1))
                                return ins

                            p.op("pe", fn2, reads=[("et", ts, 0), ("et", ts, 1), ("at", "vo"), ("at", "vh"), ("ones",)],
                                 writes=[("ps", bo), ("ps", bd)])
                        if ATT_LVL < 5:
                            continue
                        if d == 1:
                            dsts = [a_[:, blk4 * 512:(blk4 + 1) * 512] for a_ in (OACC, DACC)]
                            srcs = [PS[bo][:, :], PS[bd][:, :]]
                        elif d == 4:
                            dsts = [a_.rearrange("p (m r) -> p r m", r=4)[:, blk4, :] for a_ in (OACC, DACC)]
                            srcs = [PS[bo][:, :], PS[bd][:, :]]
                        else:
                            dsts = [a_.rearrange("p (m r) -> p r m", r=16)[:, blk4 * 4:blk4 * 4 + 4, :] for a_ in (OACC, DACC)]
                            srcs = [PS[bo][:, :].rearrange("p (r m) -> p r m", r=4),
                                    PS[bd][:, :].rearrange("p (r m) -> p r m", r=4)]
                        for j, (dst, src, bb) in enumerate(zip(dsts, srcs, (bo, bd))):
                            key = ("acc", j, blk4) if d == 1 else ("acc", j)
                            allk = [("acc", j)] + [("acc", j, q_) for q_ in range(4)]
                            if g == 0:
                                if j == 0:
                                    p.op("act", lambda e, dst=dst, src=src: e.copy(out=dst, in_=src),
                                         reads=[("ps", bb)], writes=allk)
                                else:
                                    p.op("dve", lambda e, dst=dst, src=src: e.tensor_copy(out=dst, in_=src),
                                         reads=[("ps", bb)], writes=allk)
                            else:
                                p.op("dve", lambda e, dst=dst, src=src: e.tensor_tensor(out=dst, in0=src, in1=dst, op=ALU.add),
                                     reads=[("ps", bb)] + allk, writes=allk)
                allk0 = [("acc", 0)] + [("acc", 0, q_) for q_ in range(4)]
                allk1 = [("acc", 1)] + [("acc", 1, q_) for q_ in range(4)]
                if ATT_LVL < 6:
                    continue
                p.op("dve", lambda e: e.reciprocal(out=DACC, in_=DACC), reads=allk1, writes=allk1)
                p.op("dve", lambda e, hp=hp: e.tensor_tensor(out=ON[:, hp, 0:TM], in0=OACC, in1=DACC, op=ALU.mult),
                     reads=allk0 + allk1, writes=[("on", hp)])
            return ON

        def sample_attn(ON):
            KC = vb(oH + 6656, 512)
            KCT = vb(oH + 6912, 512).rearrange("p (h k) -> p h k", h=4)
            VC = vb(oH + 7168, 512)
            SM = vf(oH + 7424, 832)
            S0 = SM[:, 0:48]
            E0f = SM[:, 48:96]
            B0T = SM[:, 96:108]
            PRD = SM[:, 108:156]
            NUM = SM[:, 156:204]
            DEN = SM[:, 204:252]
            NUMT = SM[:, 252:268]
            DENT = SM[:, 268:284]
            ESf = SM[:, 284:380]
            ESb = SM[:, 380:428].bitcast(BF16)
            BD = SM[:, 428:492].bitcast(BF16)
            PRB = SM[:, 492:516].bitcast(BF16)
            QSF = SM[:, 516:564]
            KNR = SM[:, 564:820]
            OHS = vf(oW, 384)
            dma("sp", OHS[0:32, :], ohs_d, [], [("ohs",)])
            p.op("dve", lambda e: e.memset(BD, 0.0), writes=[("smp", "bd")])
            p.op("dve", lambda e: e.memset(BD[0:64, 0:64], 1.0), writes=[("smp", "bd")])
            p.op("dve", lambda e: e.memset(BD[64:128, 64:128], 1.0), writes=[("smp", "bd")])
            for hh in range(2):
                src = bass.AP(w["rel_bias"].tensor, hh, [[0, 64], [2, 12]])
                p.op("sp", lambda e, hh=hh, src=src: e.dma_start(out=B0T[64 * hh:64 * hh + 64, :], in_=src,
                                                                 allow_slow_non_contiguous=True),
                     writes=[("smp", "b0t", hh)], dma=True)
            b = next_bank(0, 4)

            def fnb(e, b=b):
                ins = None
                for g in range(3):
                    ins = e.matmul(PS[b][:, g * 8:(g + 1) * 8], lhsT=OHS[0:32, g * 128:(g + 1) * 128],
                                   rhs=RB[0:32, 8 * g:8 * g + 8], start=True, stop=True)
                return ins

            p.op("pe", fnb, reads=[("ohs",), ("rb",)], writes=[("ps", b)])
            p.op("dve", lambda e, b=b: e.tensor_copy(out=BS, in_=PS[b][:, 0:24].rearrange("p (g h) -> p g h", g=3)),
                 reads=[("ps", b)], writes=[("smp", "bs")])
            qkeys = [("qs", g, hp) for g in range(3) for hp in range(4)]
            kkeys = [("ks", g, hp) for g in range(3) for hp in range(4)]
            kfkeys = [("ksf", g, hp) for g in range(3) for hp in range(4)]
            vfkeys = [("vsf", g, hp) for g in range(3) for hp in range(4)]
            p.op("dve", lambda e: e.tensor_copy(out=QSF, in_=QS.rearrange("p g h s -> p (g h s)")), reads=qkeys,
                 writes=[("smp", "qsf")])
            p.op("dve", lambda e: e.tensor_tensor(out=PRD, in0=QSF, in1=KSF.rearrange("p g h s -> p (g h s)"), op=ALU.mult),
                 reads=[("smp", "qsf")] + kfkeys, writes=[("smp", "prd")])
            p.op("dve", lambda e: e.tensor_copy(out=PRB, in_=PRD), reads=[("smp", "prd")], writes=[("smp", "prb")])
            p.op("dve", lambda e: e.tensor_tensor(out=NUM, in0=PRD, in1=PRB, op=ALU.subtract),
                 reads=[("smp", "prd"), ("smp", "prb")], writes=[("smp", "num")])
            p.op("dve", lambda e: e.tensor_copy(out=ESb[:, 0:48], in_=NUM), reads=[("smp", "num")], writes=[("smp", "esb")])
            b0 = next_bank(0, 4)

            def fn0(e, b0=b0):
                e.matmul(PS[b0][:, 0:48], lhsT=BD, rhs=PRB, start=True, stop=False)
                return e.matmul(PS[b0][:, 0:48], lhsT=BD, rhs=ESb[:, 0:48], start=False, stop=True)

            p.op("pe", fn0, reads=[("smp", "bd"), ("smp", "prb"), ("smp", "esb")], writes=[("ps", b0)])
            p.op("dve", lambda e, b0=b0: e.scalar_tensor_tensor(
                out=S0.rearrange("p (x s) -> p x s", s=4), in0=PS[b0][:, 0:48].rearrange("p (x s) -> p x s", s=4),
                scalar=0.125, in1=B0T.unsqueeze(2).broadcast_to([128, 12, 4]), op0=ALU.mult, op1=ALU.add),
                 reads=[("ps", b0), ("smp", "b0t", 0), ("smp", "b0t", 1)], writes=[("smp", "s0")])
            p.op("act", lambda e: e.activation(out=E0f, in_=S0, func=AF.Exp), reads=[("smp", "s0")], writes=[("smp", "e0")])
            bso, bsd = 4, 5
            for s in range(NS):
                for g in range(3):
                    W = WIN[g]
                    dma("sp", ks_d[g][s, 0:W - 1, :], ck_d[g][s, 1:W, :], [], [], out=True)
                    dma("sp", vs_d[g][s, 0:W - 1, :], cv_d[g][s, 1:W, :], [], [], out=True)
            first = True
            for s in range(NS):
                for g in range(3):
                    d = DIL[g]
                    W = WIN[g]
                    dma("pool", KC, ck_d[g][s, 0:W:d, :], [], [("smp", "kc")])
                    dma("pool", VC, cv_d[g][s, 0:W:d, :], [], [("smp", "vc")])
                    bt = 7
                    PT = PS[bt][:, 0:256].bitcast(BF16)

                    def fnt(e, PT=PT):
                        ins = None
                        for hp in range(4):
                            ins = e.transpose(PT[:, hp * 128:(hp + 1) * 128], in_=KC[:, hp * 128:(hp + 1) * 128], identity=IDB)
                        return ins

                    p.op("pe", fnt, reads=[("smp", "kc"), ("idb",)], writes=[("ps", bt)])
                    p.op("act", lambda e, PT=PT: e.copy(out=KCT.rearrange("p h k -> p (h k)"), in_=PT),
                         reads=[("ps", bt)], writes=[("smp", "kct")])
                    bsc = (next_bank(0, 4), next_bank(0, 4))

                    def fns(e, g=g, s=s, bsc=bsc):
                        ins = None
                        for hh in range(2):
                            for hp in range(4):
                                ins = e.matmul(PS[bsc[hh]][:, hp:hp + 1], lhsT=KCT[64 * hh:64 * hh + 64, hp, :],
                                               rhs=QS[64 * hh:64 * hh + 64, g, hp, s:s + 1], start=True, stop=True)
                        return ins

                    p.op("pe", fns, reads=[("smp", "kct")] + qkeys, writes=[("ps", bsc[0]), ("ps", bsc[1])])
                    for hh in range(2):
                        p.op("dve", lambda e, g=g, bsc=bsc, hh=hh: e.scalar_tensor_tensor(
                            out=ESf[:, 0:8].rearrange("p (a b) -> p b a", b=2)[:, hh, :], in0=PS[bsc[hh]][:, 0:4], scalar=0.125,
                            in1=BS[:, g, :].rearrange("p (a b) -> p b a", b=2)[:, hh, :], op0=ALU.mult, op1=ALU.add),
                             reads=[("ps", bsc[hh]), ("smp", "bs")], writes=[("smp", "esf", hh)])
                    p.op("act", lambda e: e.activation(out=ESb[:, 48:56], in_=ESf[:, 0:8], func=AF.Exp),
                         reads=[("smp", "esf", 0), ("smp", "esf", 1)], writes=[("smp", "esx")])

                    def fnp(e, g=g, s=s, first=first):
                        ins = None
                        for h in range(8):
                            hp, hh = h // 2, h % 2
                            col = (g * 4 + hp) * 4 + s
                            ins = e.matmul(PS[bso][64 * hh:64 * hh + 64, col:col + 1], lhsT=VC[:, h * 64:(h + 1) * 64],
                                           rhs=ESb[:, 48 + h:49 + h], start=True, stop=True)
                            ins = e.matmul(PS[bsd][64 * hh:64 * hh + 64, col:col + 1], lhsT=ONESB[:, 0:64],
                                           rhs=ESb[:, 48 + h:49 + h], start=True, stop=True)
                        return ins

                    p.op("pe", fnp, reads=[("smp", "esx"), ("smp", "vc"), ("ones",)], writes=[("ps", bso), ("ps", bsd)])
                    first = False
            p.op("dve", lambda e: e.tensor_tensor(out=PRD, in0=E0f, in1=VSF.rearrange("p g h s -> p (g h s)"), op=ALU.mult),
                 reads=[("smp", "e0")] + vfkeys, writes=[("smp", "prd")])
            p.op("dve", lambda e: e.tensor_tensor(out=NUM, in0=PS[bso][:, 0:48], in1=PRD, op=ALU.add),
                 reads=[("ps", bso), ("smp", "prd")], writes=[("smp", "num")])
            p.op("dve", lambda e: e.tensor_tensor(out=DEN, in0=PS[bsd][:, 0:48], in1=E0f, op=ALU.add),
                 reads=[("ps", bsd), ("smp", "e0")], writes=[("smp", "den")])
            p.op("dve", lambda e: e.tensor_reduce(out=NUMT, in_=NUM.rearrange("p (g x) -> p x g", g=3), axis=AX.X, op=ALU.add),
                 reads=[("smp", "num")], writes=[("smp", "numt")])
            p.op("dve", lambda e: e.tensor_reduce(out=DENT, in_=DEN.rearrange("p (g x) -> p x g", g=3), axis=AX.X, op=ALU.add),
                 reads=[("smp", "den")], writes=[("smp", "dent")])
            p.op("dve", lambda e: e.reciprocal(out=DENT, in_=DENT), reads=[("smp", "dent")], writes=[("smp", "dent")])
            p.op("dve", lambda e: e.tensor_tensor(out=ON[:, :, TM:TM + 4], in0=NUMT.rearrange("p (h s) -> p h s", h=4),
                                                   in1=DENT.rearrange("p (h s) -> p h s", h=4), op=ALU.mult),
                 reads=[("smp", "numt"), ("smp", "dent"), ("on", "x")], writes=[("on", "x")])
            for which, SRC, keys, outd in ((0, KSF, kfkeys, ks_d), (1, VSF, vfkeys, vs_d)):
                for g in range(3):
                    W = WIN[g]
                    b = next_bank(0, 4)

                    def fnr(e, SRC=SRC, g=g, b=b):
                        ins = None
                        for hp in range(4):
                            ins = e.transpose(PS[b][0:4, hp * 128:(hp + 1) * 128], in_=SRC[:, g, hp, :], identity=IDENT)
                        return ins

                    p.op("pe", fnr, reads=keys + [("ident",)], writes=[("ps", b)])
                    st = PST[64:68, 0:512] if which == 0 else PST[64:68, 512:1024]
                    p.op("dve", lambda e, st=st, b=b: e.tensor_copy(out=st, in_=PS[b][0:4, :]), reads=[("ps", b)],
                         writes=[("pst3", which)])
                    dma("sp", outd[g][:, W - 1, :], st, [("pst3", which)], [], out=True)

        def attn_out(ON):
            for m in range(8):
                s = next_slot()
                dma("pool", WGU[:, s, 0, 0:4, :], wchunk(w["attn_w_out"][0], m * 128), [], [("wgu", s, 0)])
                for ti in range(5):
                    c0, n = TILES[ti]
                    b = 4 + (psr[0] % 2)
                    psr[0] += 1
                    mm_group(PS[b][:, 0:n], [(WGU[:, s, 0, k, :], ON[:, k, c0:c0 + n]) for k in range(4)],
                             reads=[("on", k) for k in range(4)] + [("on", "x"), ("wgu", s, 0)], writes=[("ps", b)])
                    p.op("dve", lambda e, b=b, n=n, m=m, c0=c0: e.tensor_tensor(
                        out=X[:, m, c0:c0 + n], in0=PS[b][:, 0:n], in1=X[:, m, c0:c0 + n], op=ALU.add),
                         reads=[("ps", b), ("x", m, ti)], writes=[("x", m, ti)])

        def final_out():
            YST = vf(oA, 4096).rearrange("p (c n) -> p c n", c=8)
            YO = vf(oS, 2048).rearrange("p (s n) -> p s n", s=2)
            yo = [0]
            for ti in range(5):
                c0, n = TILES[ti]
                rmsnorm(NV["final"], [ti], f32_out=lambda c, ti_: YST[:, c, 0:TILES[ti_][1]])
                for j in range(max(1, n // 128)):
                    rows = min(128, n)
                    s = yo[0]
                    yo[0] ^= 1
                    for half in range(2):
                        b = next_bank(0, 4)

                        def fn(e, half=half, b=b, j=j, rows=rows):
                            ins = None
                            for cc in range(4):
                                c = half * 4 + cc
                                ins = e.transpose(PS[b][0:rows, cc * 128:(cc + 1) * 128],
                                                  in_=YST[:, c, j * 128:j * 128 + rows], identity=IDENT)
                            return ins

                        p.op("pe", fn, reads=[("yst", c) for c in range(8)] + [("ident",)], writes=[("ps", b)])
                        if half == 0:
                            p.op("act", lambda e, s=s, b=b, rows=rows: e.copy(out=YO[0:rows, s, 0:512], in_=PS[b][0:rows, :]),
                                 reads=[("ps", b)], writes=[("yo", s, 0)])
                        else:
                            p.op("dve", lambda e, s=s, b=b, rows=rows: e.tensor_copy(out=YO[0:rows, s, 512:1024],
                                                                                     in_=PS[b][0:rows, :]),
                                 reads=[("ps", b)], writes=[("yo", s, 1)])
                    r0 = c0 + j * 128
                    dma("sp", y_d[r0:r0 + rows, :], YO[0:rows, s, :], [("yo", s, 0), ("yo", s, 1)], [], out=True)

        EPSB = cf(1)
        p.op("dve", lambda e: e.memset(EPSB, EPS), writes=[("epsb",)])
        PSTo = carve(1024)
        PST = vf(PSTo, 1024)

        def dump_x():
            dma("sp", dbg_d, big[:, oX:oX + 8 * NT], [("x", c, t) for c in range(8) for t in range(5)], [], out=True)

        L0 = 0
        stop = [False]

        def chk(name):
            if DEBUG_STOP == name:
                dump_x()
                stop[0] = True
            return stop[0]

        for pas in (0, 1):
            if DEBUG_STOP and DEBUG_STOP.startswith("B") and pas == 0:
                pass
            tag = "AB"[pas]
            load_x(xa_d if pas == 0 else xb_d)
            if chk(tag + "load"):
                break
            ffn(w["ffn1_w_gate"][0], w["ffn1_w_up"][0], w["ffn1_w_down"][0], NV["f1l0"], [0, 1, 2, 3, 4])
            if chk(tag + "f1l0"):
                break
            pool_mixer(pas)
            if chk(tag + "pool"):
                break
            mt = [0, 1, 2, 3] + ([4] if pas == 1 else [])
            ffn(w["ffn2_w_gate"][0], w["ffn2_w_up"][0], w["ffn2_w_down"][0], NV["f2l0"], mt)
            if chk(tag + "f2l0"):
                break
            ffn(w["ffn1_w_gate"][1], w["ffn1_w_up"][1], w["ffn1_w_down"][1], NV["f1l1"], mt)
            if chk(tag + "f1l1"):
                break
            attn_qkv(pas)
            if chk(tag + "qkv"):
                break
            if pas == 1:
                build_tt()
                if chk("Btt"):
                    break
                ON = attention()
                if chk("Batt"):
                    break
                sample_attn(ON)
                if chk("Bsmp"):
                    break
                attn_out(ON)
                if chk("Battn"):
                    break
                ffn(w["ffn2_w_gate"][1], w["ffn2_w_up"][1], w["ffn2_w_down"][1], NV["f2l1"], [0, 1, 2, 3, 4])
                if chk("Bf2l1"):
                    break
                final_out()

        p.finish()
        block = es.enter_context(nc.Block())

        @block.tensor
        def _(e):
            for f in p.q["pe"]:
                f(e)

        @block.scalar
        def _(e):
            for f in p.q["act"]:
                f(e)

        @block.vector
        def _(e):
            for f in p.q["dve"]:
                f(e)

        @block.gpsimd
        def _(e):
            for f in p.q["pool"]:
                f(e)

        @block.sync
        def _(e):
            for f in p.q["sp"]:
                f(e)

    return nc


_CACHE = {}


def kernel(**inputs):
    inp = {k: np.ascontiguousarray(np.asarray(v)) for k, v in inputs.items()}
    xp = inp["x_prompt"][0]
    xs = inp["x_sample"][:, 0, :]
    consts = host_consts()
    if "nc" not in _CACHE:
        _CACHE["nc"] = build_program()
    nc = _CACHE["nc"]
    in_maps = []
    wnames = ["ffn1_norm", "ffn1_w_gate", "ffn1_w_up", "ffn1_w_down", "mix_norm", "pool_w_in", "pool_w_group",
              "pool_scale", "pool_w_out", "attn_w_qkv", "attn_w_out", "rel_bias", "ffn2_norm", "ffn2_w_gate",
              "ffn2_w_up", "ffn2_w_down"]
    for c in CORES:
        m = {}
        xa = np.zeros((NT, D), np.float32)
        xb = np.zeros((NT, D), np.float32)
        if c > 0:
            xa[0:TM] = xp[TM * (c - 1):TM * c]
            lo = TM * (c - 1) - 15
            if lo >= 0:
                xa[TM:TM + 15] = xp[lo:lo + 15]
        xb[0:TM] = xp[TM * c:TM * (c + 1)]
        xb[TM:TM + NS] = xs[NS * c:NS * (c + 1)]
        m["xa"] = xa
        m["xb"] = xb
        m["stp"] = inp["state_pool"][0, NS * c:NS * (c + 1)]
        cks = (inp["cache_k_w128"], inp["cache_k_w512"], inp["cache_k_w2048"])
        cvs = (inp["cache_v_w128"], inp["cache_v_w512"], inp["cache_v_w2048"])
        for g in range(3):
            m["ck%d" % g] = cks[g][0, NS * c:NS * (c + 1)].reshape(NS, WIN[g], 512)
            m["cv%d" % g] = cvs[g][0, NS * c:NS * (c + 1)].reshape(NS, WIN[g], 512)
        for nm in wnames:
            if nm in PADDED:
                a = inp[nm]
                flat = a.reshape(-1, a.shape[-1])
                m[nm] = np.concatenate([flat, np.full((1, a.shape[-1]), float(c), np.float32)], axis=0)
            else:
                m[nm] = inp[nm]
        m["final_norm"] = inp["final_norm"].reshape(1, D)
        m["ident"] = consts["ident"]
        m["oh"] = consts["oh"]
        m["mk"] = consts["mk"]
        m["ohs"] = consts["ohs"]
        rc = np.zeros((128, 2, 4, 16), np.float32)
        for pa in range(2):
            for g in range(4):
                if (pa == 1 and c == 0) or (pa == 0 and c == 1):
                    cnt = np.minimum(POOLW[g], np.arange(16) + 1).astype(np.float32)
                else:
                    cnt = np.full(16, POOLW[g], np.float32)
                rc[:, pa, g, :] = 1.0 / cnt
        m["rc"] = rc.reshape(128, 128)
        m["hm"] = np.full((128, 1), NEG if c == 0 else 0.0, np.float32)
        in_maps.append({k: np.ascontiguousarray(v, dtype=np.float32) for k, v in m.items()})
    res = run_bass_kernel_spmd(nc, in_maps, core_ids=list(range(len(CORES))))
    r = res.results
    _CACHE["last"] = r
    if len(CORES) != NC:
        return None
    y_prompt = np.concatenate([r[c]["y"][0:TM] for c in range(NC)], axis=0)[None]
    y_sample = np.concatenate([r[c]["y"][TM:TM + NS] for c in range(NC)], axis=0)[:, None, :]
    pool_p = r[NC - 1]["poolp"][None, None]
    pool_s = np.concatenate([r[c]["pools"] for c in range(NC)], axis=0)[None]
    outs = [y_prompt, y_sample, pool_p, pool_s]
    for g in range(3):
        W = WIN[g]
        outs.append(r[NC - 1]["kp%d" % g].reshape(1, 1, W, 8, 64))
        outs.append(r[NC - 1]["vp%d" % g].reshape(1, 1, W, 8, 64))
        outs.append(np.concatenate([r[c]["ks%d" % g] for c in range(NC)], axis=0).reshape(1, NC * NS, W, 8, 64))
        outs.append(np.concatenate([r[c]["vs%d" % g] for c in range(NC)], axis=0).reshape(1, NC * NS, W, 8, 64))
    return tuple(np.ascontiguousarray(o, dtype=np.float32) for o in outs)
```

```python
import contextlib
import math
import numpy as np
import concourse.bass as bass
import concourse.mybir as mybir
from concourse.bass_utils import run_bass_kernel_spmd

F32 = mybir.dt.float32
BF16 = mybir.dt.bfloat16
AF = mybir.ActivationFunctionType
ALU = mybir.AluOpType
AX = mybir.AxisListType

NC = 8
D = 1024
DFF = 2816
NF = DFF // 128
FSPLIT = [(0, 8), (8, 8), (16, 6)]
TM = 2048
XT = 16
NT = TM + XT
NS = 4
NEG = -30000.0
WIN = (128, 512, 2048)
DIL = (1, 4, 16)
POOLW = (2, 4, 8, 16)
EPS = 1e-6
PADDED = ("ffn1_w_gate", "ffn1_w_up", "ffn1_w_down", "ffn2_w_gate", "ffn2_w_up", "ffn2_w_down", "pool_w_in",
          "pool_w_group", "pool_w_out", "attn_w_qkv", "attn_w_out")
DEBUG_STOP = None
CORES = list(range(8))
ATT_LVL = 9
PIPE = True


class Prog:
    def __init__(self, nc, es):
        self.nc = nc
        self.q = {e: [] for e in ("pe", "act", "dve", "pool", "sp")}
        self.sems = {}
        for e in ("pe", "act", "dve", "pool"):
            self.sems[e] = es.enter_context(nc.semaphore("s_" + e))
        self.ecnt = {e: 0 for e in ("pe", "act", "dve", "pool")}
        self.dsems = {"sp": [], "pool": [], "act": []}
        for e, n in (("sp", 24), ("pool", 16), ("act", 8)):
            for i in range(n):
                k = "d_%s_%d" % (e, i)
                self.sems[k] = es.enter_context(nc.semaphore(k))
                self.dsems[e].append(k)
        self.dval = {k: 0 for e in self.dsems for k in self.dsems[e]}
        self.dnext = {e: 0 for e in self.dsems}
        self.waited = {e: {} for e in self.q}
        self.lastw = {}
        self.readers = {}
        self.out_tickets = []
        self.rgroup = {}
        self.rout = {}
        self.rpre = {}

    REGION = {"xst": ("A", 0), "a": ("A", 2), "z": ("A", 3), "tt": ("A", 4), "ys": ("A", 5),
              "wd": ("S", 0), "sg": ("S", 0), "u": ("S", 1), "t": ("S", 1), "wkv": ("S", 2), "stg": ("S", 2),
              "h": ("H", 0), "at": ("H", 1), "ts": ("H", 1), "et": ("H", 1), "smp": ("H", 2),
              "kst": ("A", 6), "yst": ("A", 5), "tth": ("A", 4), "acc": ("A", 4), "tb": ("S", 3), "on": ("S", 4),
              "yo": ("S", 5), "ohs": ("W", 1), "wgu": ("W", 0), "atS": ("S", 4), "atW": ("W", 2), "atA": ("A", 4)}

    def op(self, eng, fn, reads=(), writes=(), dma=False, out=False):
        deps = {}

        def add(k, v):
            if v > deps.get(k, 0):
                deps[k] = v

        regs = set()
        for key in list(reads) + list(writes):
            rg = self.REGION.get(key[0])
            if rg is None:
                continue
            r, grp = rg
            if self.rgroup.get(r) != grp:
                self.rpre[r] = dict(self.rout.get(r, {}))
                for k, v in self.rpre[r].items():
                    pass
                self.rout[r] = dict(self.rpre[r])
                self.rgroup[r] = grp
            regs.add(r)
        for r in regs:
            for k, v in self.rpre.get(r, {}).items():
                add(k, v)

        for key in reads:
            t = self.lastw.get(key)
            if t is not None:
                add(*t)
        for key in writes:
            t = self.lastw.get(key)
            if t is not None:
                add(*t)
            for k, v in self.readers.get(key, {}).items():
                add(k, v)
        waits = []
        wd = self.waited[eng]
        for k, v in deps.items():
            if eng == "pe" and k == "pe":
                continue
            if wd.get(k, 0) < v:
                waits.append((self.sems[k], v))
                wd[k] = v
        if dma:
            pool = self.dsems[eng]
            i = self.dnext[eng]
            self.dnext[eng] = (i + 1) % len(pool)
            k = pool[i]
            prev = self.dval[k]
            if prev > 0 and wd.get(k, 0) < prev:
                waits.append((self.sems[k], prev))
                wd[k] = prev
            val = prev + 16
            self.dval[k] = val
            tk = (k, val)
            inc = 16
        else:
            self.ecnt[eng] += 1
            tk = (eng, self.ecnt[eng])
            inc = 1
        sem = self.sems[tk[0]]

        def run(e, waits=waits, fn=fn, sem=sem, inc=inc):
            for s, v in waits:
                e.wait_ge(s, v)
            fn(e).then_inc(sem, inc)

        self.q[eng].append(run)
        for key in writes:
            self.lastw[key] = tk
            self.readers[key] = {}
        for key in reads:
            r = self.readers.setdefault(key, {})
            if r.get(tk[0], 0) < tk[1]:
                r[tk[0]] = tk[1]
        for r in regs:
            ro = self.rout.setdefault(r, {})
            if ro.get(tk[0], 0) < tk[1]:
                ro[tk[0]] = tk[1]
        if out:
            self.out_tickets.append(tk)
        return tk

    def finish(self):
        need = {}
        for k, v in self.out_tickets:
            need[k] = max(need.get(k, 0), v)
        items = [(self.sems[k], v) for k, v in need.items()]

        def run(e, items=items):
            for s, v in items:
                e.wait_ge(s, v)

        self.q["sp"].append(run)


def t5_bucket_np(dist):
    dist = np.asarray(dist, np.int32)
    distf = np.maximum(dist, 1).astype(np.float32)
    v = np.log(distf / np.float32(16.0)) / np.float32(math.log(2048 / 16)) * np.float32(16.0)
    log_b = np.minimum(16 + v.astype(np.int32), 31)
    return np.where(dist < 16, dist, log_b)


def host_consts():
    c = {}
    c["ident"] = np.eye(128, dtype=np.float32)
    oh = np.zeros((3, 2, 32, 256), np.float32)
    mk = np.zeros((2, 256), np.float32)
    ohs = np.zeros((3, 32, 128), np.float32)
    for g in range(3):
        d = DIL[g]
        for y in range(256):
            if y <= 127:
                oh[g, 0, t5_bucket_np((y + 1) * d), y] = 1.0
            j = y - 127
            if 0 <= j <= 127:
                oh[g, 1, t5_bucket_np(j * d), y] = 1.0
        for i in range(128):
            ohs[g, t5_bucket_np((128 - i) * d), i] = 1.0
    mk[0, 128:] = NEG
    mk[1, :127] = NEG
    mk[1, 255] = NEG
    c["oh"] = np.ascontiguousarray(oh.transpose(2, 0, 1, 3).reshape(32, 3 * 2 * 256))
    c["mk"] = np.ascontiguousarray(np.broadcast_to(mk.reshape(1, 512), (128, 512)))
    c["ohs"] = np.ascontiguousarray(ohs.transpose(1, 0, 2).reshape(32, 3 * 128))
    return c


def build_program():
    nc = bass.Bass("TRN2", target_bir_lowering=False)

    def din(name, shape, dt=F32):
        return nc.dram_tensor(name, list(shape), dt, kind="ExternalInput")

    def dout(name, shape, dt=F32):
        return nc.dram_tensor(name, list(shape), dt, kind="ExternalOutput")

    def dscr(name, shape, dt):
        return nc.dram_tensor(name, list(shape), dt, kind="Internal")

    xa_d = din("xa", [NT, D]).ap()
    xb_d = din("xb", [NT, D]).ap()
    stp_d = din("stp", [NS, 15, D]).ap()
    ck_d = [din("ck%d" % g, [NS, WIN[g], 512]).ap() for g in range(3)]
    cv_d = [din("cv%d" % g, [NS, WIN[g], 512]).ap() for g in range(3)]
    w = {}
    for nm, shp in (("ffn1_norm", [2, D]), ("ffn1_w_gate", [2, D, DFF]), ("ffn1_w_up", [2, D, DFF]),
                    ("ffn1_w_down", [2, DFF, D]), ("mix_norm", [2, D]), ("pool_w_in", [1, D, D]),
                    ("pool_w_group", [1, 4, 256, 256]), ("pool_scale", [1, D]), ("pool_w_out", [1, D, D]),
                    ("attn_w_qkv", [1, D, 4608]), ("attn_w_out", [1, 512, D]), ("rel_bias", [32, 24]),
                    ("ffn2_norm", [2, D]), ("ffn2_w_gate", [2, D, DFF]), ("ffn2_w_up", [2, D, DFF]),
                    ("ffn2_w_down", [2, DFF, D]), ("final_norm", [1, D])):
        if nm in PADDED:
            rows = int(np.prod(shp[:-1]))
            flat = din(nm, [rows + 1, shp[-1]]).ap()[0:rows, :]
            if len(shp) == 3:
                w[nm] = flat.rearrange("(l k) n -> l k n", l=shp[0])
            else:
                w[nm] = flat.rearrange("(l g k) n -> l g k n", l=shp[0], g=shp[1])
        else:
            w[nm] = din(nm, shp).ap()
    ident_d = din("ident", [128, 128]).ap()
    oh_d = din("oh", [32, 1536]).ap()
    mk_d = din("mk", [128, 512]).ap()
    ohs_d = din("ohs", [32, 384]).ap()
    rc_d = din("rc", [128, 128]).ap()
    hm_d = din("hm", [128, 1]).ap()

    y_d = dout("y", [NT, D]).ap()
    poolp_d = dout("poolp", [15, D]).ap()
    pools_d = dout("pools", [NS, 15, D]).ap()
    kp_d = [dout("kp%d" % g, [WIN[g], 512]).ap() for g in range(3)]
    vp_d = [dout("vp%d" % g, [WIN[g], 512]).ap() for g in range(3)]
    ks_d = [dout("ks%d" % g, [NS, WIN[g], 512]).ap() for g in range(3)]
    vs_d = [dout("vs%d" % g, [NS, WIN[g], 512]).ap() for g in range(3)]
    dbg_d = dout("dbg", [128, 8 * NT]).ap() if DEBUG_STOP else None

    kt_s = dscr("kt_s", [2, 3, 4, 128, TM], BF16).ap()
    v_s = dscr("v_s", [2, 3, 16, 128, 512], BF16).ap()
    q_s = dscr("q_s", [3, 4, 128, TM], BF16).ap()
    tt_s = dscr("tt_s", [6, 128, 2048], F32)

    es = contextlib.ExitStack()
    with es:
        BIG_WORDS = 47040
        big = es.enter_context(nc.sbuf_tensor("big", [128, BIG_WORDS], F32))
        ptr = [0]

        def carve(nwords):
            o = ptr[0]
            ptr[0] += nwords
            assert ptr[0] <= BIG_WORDS, ptr[0]
            return o

        def vf(off, n):
            return big[:, off:off + n]

        def vb(off, n):
            return big[:, off:off + n // 2].bitcast(BF16)

        oX = carve(8 * NT)
        oH = carve(8 * NT // 2)
        oA = carve(8 * NT // 2)
        oW = carve(2 * 2 * 8 * 128 // 2)
        oS = carve(6208)
        oR = carve(2 * 512)
        oQ = carve(2048)
        oC = carve(1664)
        X = vf(oX, 8 * NT).rearrange("p (c t) -> p c t", c=8)
        HB = vb(oH, 8 * NT).rearrange("p (c t) -> p c t", c=8)
        A = vb(oA, 8 * NT).rearrange("p (c t) -> p c t", c=8)
        WGU = vb(oW, 4096).rearrange("p (s j k n) -> p s j k n", s=2, j=2, k=8)
        WD = vb(oS, 8 * 1024).rearrange("p (f n) -> p f n", f=8)
        SG = vf(oS + 4096, 1024).rearrange("p (s n) -> p s n", s=2)
        RS = vf(oR, 1024).rearrange("p (s n) -> p s n", s=2)
        co = [oC]

        def cf(n):
            o = co[0]
            co[0] += n
            assert co[0] <= oC + 1664
            return vf(o, n)

        IDENT = cf(128)
        ONESB = cf(64).bitcast(BF16)
        IDB = cf(64).bitcast(BF16)
        NRM = cf(64).rearrange("p (v c) -> p v c", v=8)
        RCT = cf(128).rearrange("p (a g n) -> p a g n", a=2, g=4)
        HM = cf(1)
        UPRE = cf(120).rearrange("p (c n) -> p c n", c=8)
        US = cf(512).rearrange("p (c s n) -> p c s n", c=8, s=4)
        QS = cf(24).bitcast(BF16).rearrange("p (g h s) -> p g h s", g=3, h=4)
        KS = cf(24).bitcast(BF16).rearrange("p (g h s) -> p g h s", g=3, h=4)
        KSF = cf(48).rearrange("p (g h s) -> p g h s", g=3, h=4)
        VSF = cf(48).rearrange("p (g h s) -> p g h s", g=3, h=4)
        RB = cf(24)
        BS = cf(24).rearrange("p (g h) -> p g h", g=3)
        B0 = cf(24)
        ES = cf(96)
        E0 = cf(96)
        ESB = cf(48).bitcast(BF16)
        E0B = cf(48).bitcast(BF16)

        PS = [es.enter_context(nc.psum_tensor("ps%d" % i, [128, 512], F32)) for i in range(8)]
        p = Prog(nc, es)

        TILES = [(0, 512), (512, 512), (1024, 512), (1536, 512), (TM, XT)]

        def mm_group(out_ap, pairs, reads, writes):
            n = len(pairs)

            def fn(e):
                ins = None
                for i, (l, r) in enumerate(pairs):
                    ins = e.matmul(out_ap, lhsT=l, rhs=r, start=(i == 0), stop=(i == n - 1))
                return ins

            return p.op("pe", fn, reads=reads, writes=writes)

        def dma(eng, out_ap, in_ap, reads, writes, out=False):
            return p.op(eng, lambda e: e.dma_start(out=out_ap, in_=in_ap), reads=reads, writes=writes, dma=True,
                        out=out)

        def wchunk(wap, c0, ncols=128):
            return wap[:, c0:c0 + ncols].rearrange("(k p) n -> p k n", p=128)

        slot_ctr = [0]

        def next_slot():
            s = slot_ctr[0]
            slot_ctr[0] ^= 1
            return s

        def load_wslot(wap, c0, s, j, nk=8):
            dma("pool", WGU[:, s, j, 0:nk, :], wchunk(wap, c0), reads=[], writes=[("wgu", s, j)])

        psr = [0]

        def next_bank(lo=0, hi=6):
            b = lo + psr[0] % (hi - lo)
            psr[0] += 1
            return b

        dma("sp", IDENT, ident_d, [], [("ident",)])
        p.op("dve", lambda e: e.memset(ONESB, 1.0), writes=[("ones",)])
        p.op("dve", lambda e: e.tensor_copy(out=IDB, in_=IDENT), reads=[("ident",)], writes=[("idb",)])
        nvec = [w["ffn1_norm"][0], w["ffn1_norm"][1], w["ffn2_norm"][0], w["ffn2_norm"][1], w["mix_norm"][0],
                w["mix_norm"][1], w["pool_scale"][0], w["final_norm"][0]]
        for i, v in enumerate(nvec):
            p.op("sp", lambda e, i=i, v=v: e.dma_start(out=NRM[:, i, :], in_=v.rearrange("(c p) -> p c", p=128),
                                                       allow_slow_non_contiguous=True),
                 writes=[("nrm", i)], dma=True)
        dma("sp", RCT, rc_d.rearrange("p (a g n) -> p a g n", a=2, g=4), [], [("rct",)])
        dma("sp", HM, hm_d, [], [("hm",)])
        dma("sp", RB[0:32, :], w["rel_bias"], [], [("rb",)])
        dma("sp", B0[0:1, :], w["rel_bias"][0:1, :], [], [("b0",)])
        NV = {"f1l0": 0, "f1l1": 1, "f2l0": 2, "f2l1": 3, "mix0": 4, "mix1": 5, "pscale": 6, "final": 7}

        def load_x(x_d, nxt=None):
            XST = vf(oA, 2048).rearrange("p (s n) -> p s n", s=2)
            for tb in range(17):
                rows = 128 if tb < 16 else XT
                s = tb % 2
                dma("sp", XST[0:rows, s, :], x_d[tb * 128:tb * 128 + rows, :], [], [("xst", s)])
                for half in range(2):
                    b = next_bank()

                    def fn(e, s=s, rows=rows, half=half, b=b):
                        ins = None
                        for cc in range(4):
                            c = half * 4 + cc
                            ins = e.transpose(PS[b][:, cc * 128:cc * 128 + rows],
                                              in_=XST[0:rows, s, c * 128:(c + 1) * 128],
                                              identity=IDENT[0:rows, 0:rows])
                        return ins

                    p.op("pe", fn, reads=[("xst", s), ("ident",)], writes=[("ps", b)])
                    t = min(tb // 4, 4)
                    src = PS[b][:, :].rearrange("p (c n) -> p c n", c=4)[:, :, 0:rows]
                    dst = X[:, half * 4:half * 4 + 4, tb * 128:tb * 128 + rows]
                    eng = "act" if half == 0 else "dve"
                    if eng == "act":
                        p.op("act", lambda e, dst=dst, src=src: e.copy(out=dst, in_=src), reads=[("ps", b)],
                             writes=[("x", c, t) for c in range(half * 4, half * 4 + 4)])
                    else:
                        p.op("dve", lambda e, dst=dst, src=src: e.tensor_copy(out=dst, in_=src), reads=[("ps", b)],
                             writes=[("x", c, t) for c in range(half * 4, half * 4 + 4)])
                if nxt is not None and tb in (3, 7, 11, 15, 16):
                    nxt.tile_done(min(tb // 4, 4))
            if nxt is not None:
                nxt.flush()

        SQ = vb(oQ, 4096).rearrange("p (c n) -> p c n", c=8)

        def norm_p1(ti):
            c0, n = TILES[ti]
            p.op("act", lambda e, c0=c0, n=n: e.activation(out=SQ[:, :, 0:n], in_=X[:, :, c0:c0 + n], func=AF.Square),
                 reads=[("x", c, ti) for c in range(8)], writes=[("sq",)])

        def norm_p2(vec, ti, f32_out=None):
            c0, n = TILES[ti]
            s = ti % 2
            b = 6
            mm_group(PS[b][:, 0:n], [(ONESB, SQ[:, c, 0:n]) for c in range(8)],
                     reads=[("sq",), ("ones",)], writes=[("ps", b)])
            p.op("act", lambda e, s=s, n=n, b=b: e.activation(out=RS[:, s, 0:n], in_=PS[b][:, 0:n], func=AF.Sqrt,
                                                               scale=1.0 / D, bias=EPSB),
                 reads=[("ps", b), ("epsb",)], writes=[("rs", s)])
            p.op("dve", lambda e, s=s, n=n: e.reciprocal(out=RS[:, s, 0:n], in_=RS[:, s, 0:n]),
                 reads=[("rs", s)], writes=[("rs", s)])
            for c in range(8):
                if f32_out is None:
                    o = HB[:, c, c0:c0 + n]
                    wk = [("h", c, ti)]
                else:
                    o = f32_out(c, ti)
                    wk = [("yst", c)]
                p.op("dve", lambda e, o=o, c=c, c0=c0, n=n, s=s: e.scalar_tensor_tensor(
                    out=o, in0=X[:, c, c0:c0 + n], scalar=NRM[:, vec, c:c + 1], in1=RS[:, s, 0:n],
                    op0=ALU.mult, op1=ALU.mult),
                     reads=[("x", c, ti), ("rs", s), ("nrm", vec)], writes=wk)

        def rmsnorm(vec, tiles, f32_out=None):
            for ti in tiles:
                norm_p1(ti)
                norm_p2(vec, ti, f32_out)

        class NormPipe:
            def __init__(self, vec):
                self.vec = vec
                self.pending = None

            def tile_done(self, ti):
                if self.pending is not None:
                    norm_p2(self.vec, self.pending)
                norm_p1(ti)
                self.pending = ti

            def flush(self):
                if self.pending is not None:
                    norm_p2(self.vec, self.pending)
                    self.pending = None

        def ffn(wg, wu, wd_, vec, tiles, prenormed=False, nxt=None):
            if not prenormed:
                rmsnorm(vec, tiles)
            for gi, (f0, nf) in enumerate(FSPLIT):
                for fi in range(nf):
                    f = f0 + fi
                    s = next_slot()
                    load_wslot(wg, f * 128, s, 0)
                    load_wslot(wu, f * 128, s, 1)
                    if fi == 1:
                        dma("pool", WD[:, 0:nf, :],
                            wd_[f0 * 128:(f0 + nf) * 128, :].rearrange("(f p) n -> p f n", p=128), [], [("wd",)])
                    for ti in tiles:
                        c0, n = TILES[ti]
                        bg = (psr[0] % 2)
                        bu = 2 + (psr[0] % 2)
                        psr[0] += 1
                        hk = [("h", c, ti) for c in range(8)]
                        mm_group(PS[bg][:, 0:n], [(WGU[:, s, 0, k, :], HB[:, k, c0:c0 + n]) for k in range(8)],
                                 reads=hk + [("wgu", s, 0)], writes=[("ps", bg)])
                        mm_group(PS[bu][:, 0:n], [(WGU[:, s, 1, k, :], HB[:, k, c0:c0 + n]) for k in range(8)],
                                 reads=hk + [("wgu", s, 1)], writes=[("ps", bu)])
                        sg = bg
                        p.op("act", lambda e, sg=sg, bg=bg, n=n: e.activation(out=SG[:, sg, 0:n], in_=PS[bg][:, 0:n],
                                                                              func=AF.Silu),
                             reads=[("ps", bg)], writes=[("sg", sg)])
                        p.op("dve", lambda e, sg=sg, bu=bu, n=n, fi=fi, c0=c0: e.tensor_tensor(
                            out=A[:, fi, c0:c0 + n], in0=SG[:, sg, 0:n], in1=PS[bu][:, 0:n], op=ALU.mult),
                             reads=[("sg", sg), ("ps", bu)], writes=[("a", fi, ti)])
                last = (gi == len(FSPLIT) - 1) and (nxt is not None)
                order = [(m, ti) for ti in tiles for m in range(8)] if last else [(m, ti) for m in range(8) for ti in tiles]
                for (m, ti) in order:
                    c0, n = TILES[ti]
                    b = 4 + (psr[0] % 2)
                    psr[0] += 1
                    mm_group(PS[b][:, 0:n],
                             [(WD[:, fi, m * 128:(m + 1) * 128], A[:, fi, c0:c0 + n]) for fi in range(nf)],
                             reads=[("a", fi, ti) for fi in range(nf)] + [("wd",)], writes=[("ps", b)])
                    p.op("dve", lambda e, b=b, n=n, m=m, c0=c0: e.scalar_tensor_tensor(
                        out=X[:, m, c0:c0 + n], in0=PS[b][:, 0:n], scalar=0.5, in1=X[:, m, c0:c0 + n],
                        op0=ALU.mult, op1=ALU.add),
                         reads=[("ps", b), ("x", m, ti)], writes=[("x", m, ti)])
                    if last and m == 7:
                        nxt.tile_done(ti)
                if last:
                    nxt.flush()

        def pool_mixer(pas, prenormed=False, nxt=None):
            LU = NT
            U = vf(oS, LU)
            T1 = vf(oS + LU, LU)
            T2 = vf(oS + 2 * LU, LU)
            Z = A
            tiles = [0, 1, 2, 3, 4]
            if not prenormed:
                rmsnorm(NV["mix0"], tiles)
            p.op("pool", lambda e: e.memset(U[:, 0:1], 0.0), writes=[("u",)])
            if pas == 1:
                ST = vf(oS + 3 * LU - 1024, 1024)
                for s in range(NS):
                    dma("sp", ST[0:15, :], stp_d[s], [], [("t", 1)])
                    for half in range(2):
                        b = next_bank()

                        def fn(e, half=half, b=b):
                            ins = None
                            for cc in range(4):
                                c = half * 4 + cc
                                ins = e.transpose(PS[b][:, cc * 128:cc * 128 + 15], in_=ST[0:15, c * 128:(c + 1) * 128],
                                                  identity=IDENT[0:15, 0:15])
                            return ins

                        p.op("pe", fn, reads=[("t", 1), ("ident",)], writes=[("ps", b)])
                        p.op("act", lambda e, half=half, b=b, s=s: e.copy(
                            out=US[:, half * 4:half * 4 + 4, s, 0:15],
                            in_=PS[b][:, :].rearrange("p (c n) -> p c n", c=4)[:, :, 0:15]),
                             reads=[("ps", b)], writes=[("us", s, half)])
                for s in range(NS):
                    dma("sp", pools_d[s, 0:14, :], stp_d[s, 1:15, :], [], [], out=True)
            for c in range(8):
                g = c // 2
                wwin = POOLW[g]
                s = next_slot()
                load_wslot(w["pool_w_in"][0], c * 128, s, 0)
                for ti in tiles:
                    c0, n = TILES[ti]
                    b = next_bank(0, 4)
                    mm_group(PS[b][:, 0:n], [(WGU[:, s, 0, k, :], HB[:, k, c0:c0 + n]) for k in range(8)],
                             reads=[("h", k, ti) for k in range(8)] + [("wgu", s, 0)], writes=[("ps", b)])
                    if ti < 4:
                        p.op("act", lambda e, b=b, c0=c0, n=n: e.copy(out=U[:, 16 + c0:16 + c0 + n], in_=PS[b][:, 0:n]),
                             reads=[("ps", b)], writes=[("u",)])
                    elif pas == 0:
                        p.op("act", lambda e, b=b: e.copy(out=U[:, 1:16], in_=PS[b][:, 0:15]),
                             reads=[("ps", b)], writes=[("u",)])
                    else:
                        p.op("act", lambda e, b=b, c=c: e.copy(out=US[:, c, :, 15], in_=PS[b][:, 0:4]),
                             reads=[("ps", b)] + [("us", s_, c // 4) for s_ in range(NS)],
                             writes=[("usn", c)] + [("us", s_, c // 4) for s_ in range(NS)])
                if pas == 0:
                    p.op("act", lambda e, c=c: e.copy(out=UPRE[:, c, :], in_=U[:, 16 + TM - 15:16 + TM]),
                         reads=[("u",)], writes=[("upre", c)])
                else:
                    p.op("act", lambda e, c=c: e.copy(out=U[:, 1:16], in_=UPRE[:, c, :]),
                         reads=[("upre", c)], writes=[("u",)])
                    b = 7
                    p.op("pe", lambda e, b=b: e.transpose(PS[b][0:15, 0:128], in_=U[:, 16 + TM - 15:16 + TM],
                                                           identity=IDENT),
                         reads=[("u",), ("ident",)], writes=[("ps", b)])
                    p.op("act", lambda e, b=b, c=c: e.copy(out=PST[0:15, c * 128:(c + 1) * 128], in_=PS[b][0:15, 0:128]),
                         reads=[("ps", b)], writes=[("pst", c)])
                src = U
                bufs = [T1, T2]
                sh = 1
                lo = 1
                for lvl in range(g + 1):
                    dstb = bufs[lvl % 2]
                    lo2 = lo + sh
                    p.op("dve", lambda e, dstb=dstb, src=src, lo2=lo2, sh=sh: e.tensor_tensor(
                        out=dstb[:, lo2:LU], in0=src[:, lo2:LU], in1=src[:, lo2 - sh:LU - sh], op=ALU.add),
                         reads=[("u",)] if lvl == 0 else [("t", (lvl - 1) % 2)],
                         writes=[("t", lvl % 2)])
                    src = dstb
                    lo = lo2
                    sh *= 2
                lastk = ("t", g % 2)
                p.op("dve", lambda e, src=src, c=c, wwin=wwin: e.scalar_tensor_tensor(
                    out=Z[:, c, 0:TM], in0=src[:, 16:16 + TM], scalar=1.0 / wwin, in1=U[:, 16:16 + TM],
                    op0=ALU.mult, op1=ALU.subtract),
                     reads=[lastk, ("u",)], writes=[("z", c)])
                p.op("dve", lambda e, src=src, g=g: e.tensor_tensor(out=RS[:, 0, 0:16], in0=src[:, 16:32],
                                                                     in1=RCT[:, pas, g, :], op=ALU.mult),
                     reads=[lastk, ("rct",)], writes=[("rs", 0)])
                p.op("dve", lambda e, c=c: e.tensor_tensor(out=Z[:, c, 0:16], in0=RS[:, 0, 0:16], in1=U[:, 16:32],
                                                            op=ALU.subtract),
                     reads=[("rs", 0), ("u",)], writes=[("z", c)])
                if pas == 1:
                    p.op("dve", lambda e, c=c, wwin=wwin: e.tensor_reduce(out=RS[:, 1, 0:4], in_=US[:, c, :, 16 - wwin:16],
                                                                          axis=AX.X, op=ALU.add),
                         reads=[("usn", c)], writes=[("rs", 1)])
                    p.op("dve", lambda e, c=c, wwin=wwin: e.scalar_tensor_tensor(
                        out=Z[:, c, TM:TM + 4], in0=RS[:, 1, 0:4], scalar=1.0 / wwin, in1=US[:, c, :, 15],
                        op0=ALU.mult, op1=ALU.subtract),
                         reads=[("rs", 1), ("usn", c)], writes=[("z", c)])
                    p.op("dve", lambda e, c=c: e.memset(Z[:, c, TM + 4:NT], 0.0), writes=[("z", c)])
            if pas == 1:
                dma("sp", poolp_d, PST[0:15, :], [("pst", c) for c in range(8)], [], out=True)
                for half in range(2):
                    b = next_bank()

                    def fn(e, half=half, b=b):
                        ins = None
                        for cc in range(4):
                            c = half * 4 + cc
                            ins = e.transpose(PS[b][0:4, cc * 128:(cc + 1) * 128], in_=US[:, c, :, 15], identity=IDENT)
                        return ins

                    p.op("pe", fn, reads=[("usn", c) for c in range(8)] + [("ident",)], writes=[("ps", b)])
                    p.op("dve", lambda e, half=half, b=b: e.tensor_copy(out=PST[32:36, half * 512:(half + 1) * 512],
                                                                        in_=PS[b][0:4, :]),
                         reads=[("ps", b)], writes=[("pst2", half)])
                dma("sp", pools_d[:, 14, :], PST[32:36, :], [("pst2", 0), ("pst2", 1)], [], out=True)
            mt = [0, 1, 2, 3] + ([4] if pas == 1 else [])
            for c in range(8):
                g = c // 2
                s = next_slot()
                dma("pool", WGU[:, s, 0, 0:2, :],
                    w["pool_w_group"][0, g][:, (c % 2) * 128:(c % 2) * 128 + 128].rearrange("(k p) n -> p k n", p=128),
                    [], [("wgu", s, 0)])
                for ti in mt:
                    c0, n = TILES[ti]
                    b = next_bank(0, 4)
                    mm_group(PS[b][:, 0:n], [(WGU[:, s, 0, k, :], Z[:, 2 * g + k, c0:c0 + n]) for k in range(2)],
                             reads=[("z", 2 * g), ("z", 2 * g + 1), ("wgu", s, 0)], writes=[("ps", b)])
                    p.op("act", lambda e, b=b, n=n, c=c, c0=c0: e.activation(
                        out=HB[:, c, c0:c0 + n], in_=PS[b][:, 0:n], func=AF.Copy, scale=NRM[:, NV["pscale"], c:c + 1]),
                         reads=[("ps", b), ("nrm", NV["pscale"])], writes=[("h", c, ti)])
            for tg in ([0, 1], [t_ for t_ in mt if t_ >= 2]):
                for m in range(8):
                    s = next_slot()
                    load_wslot(w["pool_w_out"][0], m * 128, s, 0)
                    for ti in tg:
                        c0, n = TILES[ti]
                        b = 4 + (psr[0] % 2)
                        psr[0] += 1
                        mm_group(PS[b][:, 0:n], [(WGU[:, s, 0, k, :], HB[:, k, c0:c0 + n]) for k in range(8)],
                                 reads=[("h", k, ti) for k in range(8)] + [("wgu", s, 0)], writes=[("ps", b)])
                        p.op("dve", lambda e, b=b, n=n, m=m, c0=c0: e.tensor_tensor(
                            out=X[:, m, c0:c0 + n], in0=PS[b][:, 0:n], in1=X[:, m, c0:c0 + n], op=ALU.add),
                             reads=[("ps", b), ("x", m, ti)], writes=[("x", m, ti)])
                        if m == 7 and nxt is not None:
                            nxt.tile_done(ti)
            if nxt is not None:
                nxt.flush()


        flip = {"kst": 0, "vst": 0, "ts": 0, "acc": 0}

        def attn_qkv(pas, prenormed=False):
            tiles = [0, 1, 2, 3] + ([4] if pas == 1 else [])
            if not prenormed:
                rmsnorm(NV["mix1"], tiles)
            wq = w["attn_w_qkv"][0]
            KST = vb(oA, 2 * TM).rearrange("p (s n) -> p s n", s=2)
            WKV = vb(oS, 8 * 1024).rearrange("p (k n) -> p k n", k=8)
            VST = vb(oS + 4096, 1024).rearrange("p (s n) -> p s n", s=2)
            KVF = vf(oS + 4608, 1024)
            hall = [("h", k, ti) for k in range(8) for ti in range(4)]
            for g in range(3):
                d = DIL[g]
                nb = 16 // d
                for which in ((0, 1) if pas == 1 else (1,)):
                    for hp in range(4):
                        col = g * 1536 + which * 512 + hp * 128
                        s = next_slot()
                        load_wslot(wq, col, s, 0)
                        ks = flip["kst"]
                        flip["kst"] ^= 1
                        for ti in tiles:
                            c0, n = TILES[ti]
                            b = next_bank(0, 4)
                            mm_group(PS[b][:, 0:n], [(WGU[:, s, 0, k, :], HB[:, k, c0:c0 + n]) for k in range(8)],
                                     reads=[("h", k, ti) for k in range(8)] + [("wgu", s, 0)], writes=[("ps", b)])
                            if ti < 4:
                                dst = KST[:, ks, :].rearrange("p (r m) -> p r m", r=d)[:, :, c0 // d:(c0 + n) // d]
                                src = PS[b][:, 0:n].rearrange("p (m r) -> p r m", r=d)
                                if ti % 2 == 0:
                                    p.op("act", lambda e, dst=dst, src=src: e.copy(out=dst, in_=src),
                                         reads=[("ps", b)], writes=[("kst", ks)])
                                else:
                                    p.op("dve", lambda e, dst=dst, src=src: e.tensor_copy(out=dst, in_=src),
                                         reads=[("ps", b)], writes=[("kst", ks)])
                            elif which == 0:
                                p.op("dve", lambda e, b=b, g=g, hp=hp: e.tensor_copy(out=QS[:, g, hp, :], in_=PS[b][:, 0:4]),
                                     reads=[("ps", b)], writes=[("qs", g, hp)])
                            else:
                                p.op("dve", lambda e, b=b, g=g, hp=hp: e.tensor_copy(out=KS[:, g, hp, :], in_=PS[b][:, 0:4]),
                                     reads=[("ps", b)], writes=[("ks", g, hp)])
                                p.op("dve", lambda e, b=b, g=g, hp=hp: e.tensor_copy(out=KSF[:, g, hp, :], in_=PS[b][:, 0:4]),
                                     reads=[("ps", b)], writes=[("ksf", g, hp)])
                        dd = q_s[g, hp] if which == 0 else kt_s[pas, g, hp]
                        dma("sp", dd, KST[:, ks, :], [("kst", ks)], [("qk_s", pas, g, hp, which)])
                dma("pool", WKV, wq[:, g * 1536 + 512:g * 1536 + 1536].rearrange("(k p) n -> p k n", p=128), [],
                    [("wkv",)])
                W = WIN[g]
                for blk in range(16):
                    r, n_ = blk // nb, blk % nb
                    start = n_ * 128 * d + r
                    lhs = [HB[:, k, start:start + 127 * d + 1:d] for k in range(8)]
                    need_out = (pas == 1) and (start >= TM - W)
                    bv = next_bank(0, 4)
                    mm_group(PS[bv][:, :], [(lhs[k], WKV[:, k, 512:1024]) for k in range(8)],
                             reads=hall + [("wkv",)], writes=[("ps", bv)])
                    vs = flip["vst"]
                    flip["vst"] ^= 1
                    p.op("act", lambda e, vs=vs, bv=bv: e.copy(out=VST[:, vs, :], in_=PS[bv][:, :]),
                         reads=[("ps", bv)], writes=[("stg", "v", vs)])
                    dma("sp", v_s[pas, g, blk], VST[:, vs, :], [("stg", "v", vs)], [("v_s", pas, g)])
                    if need_out:
                        bk = next_bank(0, 4)
                        mm_group(PS[bk][:, :], [(lhs[k], WKV[:, k, 0:512]) for k in range(8)],
                                 reads=hall + [("wkv",)], writes=[("ps", bk)])
                        p.op("dve", lambda e, bk=bk: e.tensor_copy(out=KVF[:, 0:512], in_=PS[bk][:, :]),
                             reads=[("ps", bk)], writes=[("stg", "kf")])
                        p.op("dve", lambda e, bv=bv: e.tensor_copy(out=KVF[:, 512:1024], in_=PS[bv][:, :]),
                             reads=[("ps", bv)], writes=[("stg", "vf")])
                        t0 = start - (TM - W)
                        dma("sp", kp_d[g][t0:t0 + 127 * d + 1:d, :], KVF[:, 0:512], [("stg", "kf")], [], out=True)
                        dma("sp", vp_d[g][t0:t0 + 127 * d + 1:d, :], KVF[:, 512:1024], [("stg", "vf")], [], out=True)
                if pas == 1:
                    for hp in range(4):
                        b = next_bank(0, 4)
                        mm_group(PS[b][:, 0:XT],
                                 [(WKV[:, k, 512 + hp * 128:512 + (hp + 1) * 128], HB[:, k, TM:TM + XT]) for k in range(8)],
                                 reads=[("h", k, 4) for k in range(8)] + [("wkv",)], writes=[("ps", b)])
                        p.op("dve", lambda e, b=b, g=g, hp=hp: e.tensor_copy(out=VSF[:, g, hp, :], in_=PS[b][:, 0:4]),
                             reads=[("ps", b)], writes=[("vsf", g, hp)])

        def build_tt():
            OH = vf(oS, 1536)
            MKT = vf(oS + 1536, 512)
            TST = vf(oS + 2048, 2048)
            dma("sp", OH[0:32, :], oh_d, [], [("tb", "oh")])
            dma("sp", MKT, mk_d, [], [("tb", "mk")])
            for g in range(3):
                for cp in range(2):
                    def fn(e, g=g, cp=cp):
                        ins = None
                        for h in range(8):
                            ins = e.matmul(PS[h // 2][:, (h % 2) * 256:(h % 2) * 256 + 256],
                                           lhsT=RB[0:32, 8 * g + h:8 * g + h + 1].broadcast_to([32, 128]),
                                           rhs=OH[0:32, (g * 2 + cp) * 256:(g * 2 + cp) * 256 + 256], start=True, stop=True)
                        return ins

                    p.op("pe", fn, reads=[("rb",), ("tb", "oh")], writes=[("ps", b) for b in range(4)])
                    for h in range(8):
                        p.op("dve", lambda e, h=h, cp=cp: e.tensor_tensor(
                            out=TST[:, h * 256:(h + 1) * 256], in0=PS[h // 2][:, (h % 2) * 256:(h % 2) * 256 + 256],
                            in1=MKT[:, cp * 256:(cp + 1) * 256], op=ALU.add),
                             reads=[("ps", h // 2), ("tb", "mk")], writes=[("tb", "tst")])
                    dma("sp", tt_s.ap()[g * 2 + cp], TST, [("tb", "tst")], [("tt_s", g, cp)])

        def attention():
            ET = vb(oH + 5120, 2048).rearrange("p (s n) -> p s n", s=2)
            TS = vf(oH + 6144, 2048).rearrange("p (s n) -> p s n", s=2)
            TTH = vf(oA, 3072).rearrange("p (s g k j a) -> p s g k j a", s=2, g=3, k=2, j=2)
            OACC = vf(oA + 3072, TM)
            DACC = vf(oA + 3072 + TM, TM)
            ON = vb(oS, 4 * NT).rearrange("p (k t) -> p k t", k=4)
            p.op("pool", lambda e: e.memset(ON[:, :, TM:NT], 0.0), writes=[("on", "x")])
            bufs = [
                dict(kto=(vb(oH, TM), ("at", "kto")), kth=(vb(oH + 1024, TM), ("at", "kth")),
                     q=(vb(oH + 2048, TM), ("at", "q")),
                     vo=(vb(oH + 3072, TM).rearrange("p (b f) -> p b f", b=16), ("at", "vo")),
                     vh=(vb(oH + 4096, TM).rearrange("p (b f) -> p b f", b=16), ("at", "vh"))),
                dict(kto=(vb(oS + 4128, TM), ("atS", "kto")), kth=(vb(oS + 5152, TM), ("atS", "kth")),
                     q=(vb(oW, TM), ("atW", "q")),
                     vo=(vb(oW + 1024, TM).rearrange("p (b f) -> p b f", b=16), ("atW", "vo")),
                     vh=(vb(oA + 7168, TM).rearrange("p (b f) -> p b f", b=16), ("atA", "vh"))),
            ]

            def emit_loads(it):
                hp_, g_ = it // 3, it % 3
                bf = bufs[it % 2]
                if g_ == 0:
                    for g2 in range(3):
                        for cp in range(2):
                            src = bass.AP(tt_s, (g2 * 2 + cp) * 128 * 2048 + 127 + hp_ * 512, [[2047, 128], [256, 2], [1, 128]])
                            dma("sp", TTH[:, hp_ % 2, g2, cp, :, :], src, [("tt_s", g2, cp)], [("tth", hp_ % 2)])
                dma("sp", bf["kto"][0], kt_s[1, g_, hp_], [("qk_s", 1, g_, hp_, 1)], [bf["kto"][1]])
                dma("sp", bf["kth"][0], kt_s[0, g_, hp_], [("qk_s", 0, g_, hp_, 1)], [bf["kth"][1]])
                dma("sp", bf["q"][0], q_s[g_, hp_], [("qk_s", 1, g_, hp_, 0)], [bf["q"][1]])
                dma("sp", bf["vo"][0], v_s[1, g_].rearrange("b a f -> a b f")[:, :, hp_ * 128:(hp_ + 1) * 128],
                    [("v_s", 1, g_)], [bf["vo"][1]])
                dma("sp", bf["vh"][0], v_s[0, g_].rearrange("b a f -> a b f")[:, :, hp_ * 128:(hp_ + 1) * 128],
                    [("v_s", 0, g_)], [bf["vh"][1]])

            emit_loads(0)
            for hp in range(4):
                sl = hp % 2
                for g in range(3):
                    it = hp * 3 + g
                    if it + 1 < 12:
                        emit_loads(it + 1)
                    bf = bufs[it % 2]
                    KTo, KTh, Q, Vo, Vh = bf["kto"][0], bf["kth"][0], bf["q"][0], bf["vo"][0], bf["vh"][0]
                    kkeys_ = [bf["kto"][1], bf["kth"][1], bf["q"][1]]
                    vkeys_ = [bf["vo"][1], bf["vh"][1]]
                    d = DIL[g]
                    nb = 16 // d

                    def blkinfo(blk, nb=nb, KTo=KTo, KTh=KTh, Vo=Vo, Vh=Vh):
                        r, n_ = blk // nb, blk % nb
                        if n_ == 0:
                            pb = r * nb + nb - 1
                            return True, KTh[:, pb * 128:(pb + 1) * 128], Vh[:, pb, :]
                        return False, KTo[:, (blk - 1) * 128:blk * 128], Vo[:, blk - 1, :]

                    def emit_S(pr):
                        banks = (0, 1) if pr % 2 == 0 else (2, 3)
                        ts = pr % 2
                        info = [blkinfo(2 * pr + j) for j in range(2)]

                        def fn(e, pr=pr, banks=banks, info=info, Q=Q, KTo=KTo):
                            ins = None
                            for hh in range(2):
                                for j in range(2):
                                    blk = 2 * pr + j
                                    qc = Q[:, blk * 128:(blk + 1) * 128]
                                    for kb, KK in enumerate((info[j][1], KTo[:, blk * 128:(blk + 1) * 128])):
                                        ins = e.matmul(PS[banks[hh]][:, (j * 2 + kb) * 128:(j * 2 + kb + 1) * 128],
                                                       lhsT=KK[64 * hh:64 * hh + 64, :], rhs=qc[64 * hh:64 * hh + 64, :],
                                                       start=True, stop=True)
                            return ins

                        p.op("pe", fn, reads=kkeys_, writes=[("ps", banks[0]), ("ps", banks[1])])
                        for hh in range(2):
                            for j in range(2):
                                p.op("dve", lambda e, ts=ts, hh=hh, j=j, banks=banks, sl=sl, g=g: e.scalar_tensor_tensor(
                                    out=TS[:, ts, hh * 512 + j * 256:hh * 512 + (j + 1) * 256].rearrange("p (k a) -> p k a", k=2),
                                    in0=PS[banks[hh]][:, j * 256:(j + 1) * 256].rearrange("p (k a) -> p k a", k=2),
                                    scalar=0.125, in1=TTH[:, sl, g, :, hh, :], op0=ALU.mult, op1=ALU.add),
                                     reads=[("ps", banks[hh]), ("tth", sl)], writes=[("ts", ts, hh, j)])
                        tsk = [("ts", ts, hh_, j_) for hh_ in range(2) for j_ in range(2)]
                        p.op("act", lambda e, ts=ts: e.activation(out=ET[:, ts, :], in_=TS[:, ts, :], func=AF.Exp),
                             reads=tsk, writes=[("et", ts)])
                        TSv = TS[:, ts, :].rearrange("p (h j k a) -> p h j k a", h=2, j=2, k=2)
                        ETv = ET[:, ts, :].rearrange("p (h j k a) -> p h j k a", h=2, j=2, k=2)
                        for j in range(2):
                            if info[j][0]:
                                p.op("act", lambda e, j=j, TSv=TSv, ETv=ETv: e.activation(
                                    out=ETv[:, :, j, 0, :], in_=TSv[:, :, j, 0, :], func=AF.Exp, bias=HM),
                                     reads=tsk + [("hm",)], writes=[("et", ts)])

                    def emit_PV(pr):
                        ts = pr % 2
                        blk4 = pr // 2
                        bo = 4 + 2 * (blk4 % 2)
                        bd = bo + 1
                        info = [blkinfo(2 * pr + j) for j in range(2)]

                        def fn2(e, pr=pr, ts=ts, bo=bo, bd=bd, info=info, Vo=Vo):
                            ins = None
                            for j in range(2):
                                blk = 2 * pr + j
                                i = (pr % 2) * 2 + j
                                vv = (info[j][2], Vo[:, blk, :])
                                for hh in range(2):
                                    for kb in range(2):
                                        col = ((hh * 2 + j) * 2 + kb) * 128
                                        ins = e.matmul(PS[bo][64 * hh:64 * hh + 64, i * 128:(i + 1) * 128],
                                                       lhsT=vv[kb][:, 64 * hh:64 * hh + 64], rhs=ET[:, ts, col:col + 128],
                                                       start=(kb == 0), stop=(kb == 1))
                                for hh in range(2):
                                    for kb in range(2):
                                        col = ((hh * 2 + j) * 2 + kb) * 128
                                        ins = e.matmul(PS[bd][64 * hh:64 * hh + 64, i * 128:(i + 1) * 128],
                                                       lhsT=ONESB[:, 0:64], rhs=ET[:, ts, col:col + 128],
                                                       start=(kb == 0), stop=(kb == 1))
                            return ins

                        p.op("pe", fn2, reads=[("et", ts), ("ones",)] + vkeys_, writes=[("ps", bo), ("ps", bd)])
                        if pr % 2 == 0:
                            return
                        if d == 1:
                            dsts = [a_[:, blk4 * 512:(blk4 + 1) * 512] for a_ in (OACC, DACC)]
                            srcs = [PS[bo][:, :], PS[bd][:, :]]
                        elif d == 4:
                            dsts = [a_.rearrange("p (m r) -> p r m", r=4)[:, blk4, :] for a_ in (OACC, DACC)]
                            srcs = [PS[bo][:, :], PS[bd][:, :]]
                        else:
                            dsts = [a_.rearrange("p (m r) -> p r m", r=16)[:, blk4 * 4:blk4 * 4 + 4, :] for a_ in (OACC, DACC)]
                            srcs = [PS[bo][:, :].rearrange("p (r m) -> p r m", r=4),
                                    PS[bd][:, :].rearrange("p (r m) -> p r m", r=4)]
                        for j2, (dst, src, bb) in enumerate(zip(dsts, srcs, (bo, bd))):
                            allk = [("acc", j2)] + [("acc", j2, q_) for q_ in range(4)]
                            if g == 0:
                                if j2 == 0:
                                    p.op("act", lambda e, dst=dst, src=src: e.copy(out=dst, in_=src),
                                         reads=[("ps", bb)], writes=allk)
                                else:
                                    p.op("dve", lambda e, dst=dst, src=src: e.tensor_copy(out=dst, in_=src),
                                         reads=[("ps", bb)], writes=allk)
                            else:
                                p.op("dve", lambda e, dst=dst, src=src: e.tensor_tensor(out=dst, in0=src, in1=dst, op=ALU.add),
                                     reads=[("ps", bb)] + allk, writes=allk)

                    emit_S(0)
                    for pr in range(8):
                        if pr + 1 < 8:
                            emit_S(pr + 1)
                        emit_PV(pr)
                allk0 = [("acc", 0)] + [("acc", 0, q_) for q_ in range(4)]
                allk1 = [("acc", 1)] + [("acc", 1, q_) for q_ in range(4)]
                p.op("dve", lambda e: e.reciprocal(out=DACC, in_=DACC), reads=allk1, writes=allk1)
                p.op("dve", lambda e, hp=hp: e.tensor_tensor(out=ON[:, hp, 0:TM], in0=OACC, in1=DACC, op=ALU.mult),
                     reads=allk0 + allk1, writes=[("on", hp)])
            return ON

        def sample_attn(ON):
            KCs = [vb(oH, 512), vb(oH + 256, 512)]
            VCs = [vb(oH + 512, 512), vb(oH + 768, 512)]
            KCTs = [vb(oH + 1024, 512).rearrange("p (h k) -> p h k", h=4),
                    vb(oH + 1280, 512).rearrange("p (h k) -> p h k", h=4)]
            SM = vf(oH + 1536, 832)
            S0 = SM[:, 0:48]
            E0f = SM[:, 48:96]
            B0T = SM[:, 96:108]
            PRD = SM[:, 108:156]
            NUM = SM[:, 156:204]
            DEN = SM[:, 204:252]
            NUMT = SM[:, 252:268]
            DENT = SM[:, 268:284]
            ESf = SM[:, 284:380]
            ESb = SM[:, 380:428].bitcast(BF16)
            BD = SM[:, 428:492].bitcast(BF16)
            PRB = SM[:, 492:516].bitcast(BF16)
            QSF = SM[:, 516:564]
            KNR = SM[:, 564:820]
            OHS = vf(oW, 384)
            dma("sp", OHS[0:32, :], ohs_d, [], [("ohs",)])
            p.op("dve", lambda e: e.memset(BD, 0.0), writes=[("smp", "bd")])
            p.op("dve", lambda e: e.memset(BD[0:64, 0:64], 1.0), writes=[("smp", "bd")])
            p.op("dve", lambda e: e.memset(BD[64:128, 64:128], 1.0), writes=[("smp", "bd")])
            for hh in range(2):
                src = bass.AP(w["rel_bias"].tensor, hh, [[0, 64], [2, 12]])
                p.op("sp", lambda e, hh=hh, src=src: e.dma_start(out=B0T[64 * hh:64 * hh + 64, :], in_=src,
                                                                 allow_slow_non_contiguous=True),
                     writes=[("smp", "b0t", hh)], dma=True)
            b = next_bank(0, 4)

            def fnb(e, b=b):
                ins = None
                for g in range(3):
                    ins = e.matmul(PS[b][:, g * 8:(g + 1) * 8], lhsT=OHS[0:32, g * 128:(g + 1) * 128],
                                   rhs=RB[0:32, 8 * g:8 * g + 8], start=True, stop=True)
                return ins

            p.op("pe", fnb, reads=[("ohs",), ("rb",)], writes=[("ps", b)])
            p.op("dve", lambda e, b=b: e.tensor_copy(out=BS, in_=PS[b][:, 0:24].rearrange("p (g h) -> p g h", g=3)),
                 reads=[("ps", b)], writes=[("smp", "bs")])
            qkeys = [("qs", g, hp) for g in range(3) for hp in range(4)]
            kkeys = [("ks", g, hp) for g in range(3) for hp in range(4)]
            kfkeys = [("ksf", g, hp) for g in range(3) for hp in range(4)]
            vfkeys = [("vsf", g, hp) for g in range(3) for hp in range(4)]
            p.op("dve", lambda e: e.tensor_copy(out=QSF, in_=QS.rearrange("p g h s -> p (g h s)")), reads=qkeys,
                 writes=[("smp", "qsf")])
            p.op("dve", lambda e: e.tensor_tensor(out=PRD, in0=QSF, in1=KSF.rearrange("p g h s -> p (g h s)"), op=ALU.mult),
                 reads=[("smp", "qsf")] + kfkeys, writes=[("smp", "prd")])
            p.op("dve", lambda e: e.tensor_copy(out=PRB, in_=PRD), reads=[("smp", "prd")], writes=[("smp", "prb")])
            p.op("dve", lambda e: e.tensor_tensor(out=NUM, in0=PRD, in1=PRB, op=ALU.subtract),
                 reads=[("smp", "prd"), ("smp", "prb")], writes=[("smp", "num")])
            p.op("dve", lambda e: e.tensor_copy(out=ESb[:, 0:48], in_=NUM), reads=[("smp", "num")], writes=[("smp", "esb")])
            b0 = next_bank(0, 4)

            def fn0(e, b0=b0):
                e.matmul(PS[b0][:, 0:48], lhsT=BD, rhs=PRB, start=True, stop=False)
                return e.matmul(PS[b0][:, 0:48], lhsT=BD, rhs=ESb[:, 0:48], start=False, stop=True)

            p.op("pe", fn0, reads=[("smp", "bd"), ("smp", "prb"), ("smp", "esb")], writes=[("ps", b0)])
            p.op("dve", lambda e, b0=b0: e.scalar_tensor_tensor(
                out=S0.rearrange("p (x s) -> p x s", s=4), in0=PS[b0][:, 0:48].rearrange("p (x s) -> p x s", s=4),
                scalar=0.125, in1=B0T.unsqueeze(2).broadcast_to([128, 12, 4]), op0=ALU.mult, op1=ALU.add),
                 reads=[("ps", b0), ("smp", "b0t", 0), ("smp", "b0t", 1)], writes=[("smp", "s0")])
            p.op("act", lambda e: e.activation(out=E0f, in_=S0, func=AF.Exp), reads=[("smp", "s0")], writes=[("smp", "e0")])
            bso, bsd = 4, 5
            for s in range(NS):
                for g in range(3):
                    W = WIN[g]
                    dma("sp", ks_d[g][s, 0:W - 1, :], ck_d[g][s, 1:W, :], [], [], out=True)
                    dma("sp", vs_d[g][s, 0:W - 1, :], cv_d[g][s, 1:W, :], [], [], out=True)
            first = True

            def smp_loads(i):
                s_, g_ = i // 3, i % 3
                dma("pool", KCs[i % 2], ck_d[g_][s_, 0:WIN[g_]:DIL[g_], :], [], [("smp", "kc", i % 2)])
                dma("pool", VCs[i % 2], cv_d[g_][s_, 0:WIN[g_]:DIL[g_], :], [], [("smp", "vc", i % 2)])

            smp_loads(0)
            for s in range(NS):
                for g in range(3):
                    i_ = s * 3 + g
                    if i_ + 1 < NS * 3:
                        smp_loads(i_ + 1)
                    sb_ = i_ % 2
                    KC, VC, KCT = KCs[sb_], VCs[sb_], KCTs[sb_]
                    d = DIL[g]
                    W = WIN[g]
                    bt = 7
                    PT = PS[bt][:, 0:256].bitcast(BF16)

                    def fnt(e, PT=PT, KC=KC):
                        ins = None
                        for hp in range(4):
                            ins = e.transpose(PT[:, hp * 128:(hp + 1) * 128], in_=KC[:, hp * 128:(hp + 1) * 128], identity=IDB)
                        return ins

                    p.op("pe", fnt, reads=[("smp", "kc", sb_), ("idb",)], writes=[("ps", bt)])
                    p.op("act", lambda e, PT=PT, KCT=KCT: e.copy(out=KCT.rearrange("p h k -> p (h k)"), in_=PT),
                         reads=[("ps", bt)], writes=[("smp", "kct", sb_)])
                    bsc = (next_bank(0, 4), next_bank(0, 4))

                    def fns(e, g=g, s=s, bsc=bsc, KCT=KCT):
                        ins = None
                        for hh in range(2):
                            for hp in range(4):
                                ins = e.matmul(PS[bsc[hh]][:, hp:hp + 1], lhsT=KCT[64 * hh:64 * hh + 64, hp, :],
                                               rhs=QS[64 * hh:64 * hh + 64, g, hp, s:s + 1], start=True, stop=True)
                        return ins

                    p.op("pe", fns, reads=[("smp", "kct", sb_)] + qkeys, writes=[("ps", bsc[0]), ("ps", bsc[1])])
                    for hh in range(2):
                        p.op("dve", lambda e, g=g, bsc=bsc, hh=hh: e.scalar_tensor_tensor(
                            out=ESf[:, 0:8].rearrange("p (a b) -> p b a", b=2)[:, hh, :], in0=PS[bsc[hh]][:, 0:4], scalar=0.125,
                            in1=BS[:, g, :].rearrange("p (a b) -> p b a", b=2)[:, hh, :], op0=ALU.mult, op1=ALU.add),
                             reads=[("ps", bsc[hh]), ("smp", "bs")], writes=[("smp", "esf", hh)])
                    p.op("act", lambda e: e.activation(out=ESb[:, 48:56], in_=ESf[:, 0:8], func=AF.Exp),
                         reads=[("smp", "esf", 0), ("smp", "esf", 1)], writes=[("smp", "esx")])

                    def fnp(e, g=g, s=s, first=first, VC=VC):
                        ins = None
                        for h in range(8):
                            hp, hh = h // 2, h % 2
                            col = (g * 4 + hp) * 4 + s
                            ins = e.matmul(PS[bso][64 * hh:64 * hh + 64, col:col + 1], lhsT=VC[:, h * 64:(h + 1) * 64],
                                           rhs=ESb[:, 48 + h:49 + h], start=True, stop=True)
                            ins = e.matmul(PS[bsd][64 * hh:64 * hh + 64, col:col + 1], lhsT=ONESB[:, 0:64],
                                           rhs=ESb[:, 48 + h:49 + h], start=True, stop=True)
                        return ins

                    p.op("pe", fnp, reads=[("smp", "esx"), ("smp", "vc", sb_), ("ones",)], writes=[("ps", bso), ("ps", bsd)])
                    first = False
            p.op("dve", lambda e: e.tensor_tensor(out=PRD, in0=E0f, in1=VSF.rearrange("p g h s -> p (g h s)"), op=ALU.mult),
                 reads=[("smp", "e0")] + vfkeys, writes=[("smp", "prd")])
            p.op("dve", lambda e: e.tensor_tensor(out=NUM, in0=PS[bso][:, 0:48], in1=PRD, op=ALU.add),
                 reads=[("ps", bso), ("smp", "prd")], writes=[("smp", "num")])
            p.op("dve", lambda e: e.tensor_tensor(out=DEN, in0=PS[bsd][:, 0:48], in1=E0f, op=ALU.add),
                 reads=[("ps", bsd), ("smp", "e0")], writes=[("smp", "den")])
            p.op("dve", lambda e: e.tensor_reduce(out=NUMT, in_=NUM.rearrange("p (g x) -> p x g", g=3), axis=AX.X, op=ALU.add),
                 reads=[("smp", "num")], writes=[("smp", "numt")])
            p.op("dve", lambda e: e.tensor_reduce(out=DENT, in_=DEN.rearrange("p (g x) -> p x g", g=3), axis=AX.X, op=ALU.add),
                 reads=[("smp", "den")], writes=[("smp", "dent")])
            p.op("dve", lambda e: e.reciprocal(out=DENT, in_=DENT), reads=[("smp", "dent")], writes=[("smp", "dent")])
            p.op("dve", lambda e: e.tensor_tensor(out=ON[:, :, TM:TM + 4], in0=NUMT.rearrange("p (h s) -> p h s", h=4),
                                                   in1=DENT.rearrange("p (h s) -> p h s", h=4), op=ALU.mult),
                 reads=[("smp", "numt"), ("smp", "dent"), ("on", "x")], writes=[("on", "x")])
            for which, SRC, keys, outd in ((0, KSF, kfkeys, ks_d), (1, VSF, vfkeys, vs_d)):
                for g in range(3):
                    W = WIN[g]
                    b = next_bank(0, 4)

                    def fnr(e, SRC=SRC, g=g, b=b):
                        ins = None
                        for hp in range(4):
                            ins = e.transpose(PS[b][0:4, hp * 128:(hp + 1) * 128], in_=SRC[:, g, hp, :], identity=IDENT)
                        return ins

                    p.op("pe", fnr, reads=keys + [("ident",)], writes=[("ps", b)])
                    st = PST[64:68, 0:512] if which == 0 else PST[64:68, 512:1024]
                    p.op("dve", lambda e, st=st, b=b: e.tensor_copy(out=st, in_=PS[b][0:4, :]), reads=[("ps", b)],
                         writes=[("pst3", which)])
                    dma("sp", outd[g][:, W - 1, :], st, [("pst3", which)], [], out=True)

        def attn_out(ON, nxt=None):
            for tg in ([0, 1], [2, 3, 4]):
                for m in range(8):
                    s = next_slot()
                    dma("pool", WGU[:, s, 0, 0:4, :], wchunk(w["attn_w_out"][0], m * 128), [], [("wgu", s, 0)])
                    for ti in tg:
                        c0, n = TILES[ti]
                        b = 4 + (psr[0] % 2)
                        psr[0] += 1
                        mm_group(PS[b][:, 0:n], [(WGU[:, s, 0, k, :], ON[:, k, c0:c0 + n]) for k in range(4)],
                                 reads=[("on", k) for k in range(4)] + [("on", "x"), ("wgu", s, 0)], writes=[("ps", b)])
                        p.op("dve", lambda e, b=b, n=n, m=m, c0=c0: e.tensor_tensor(
                            out=X[:, m, c0:c0 + n], in0=PS[b][:, 0:n], in1=X[:, m, c0:c0 + n], op=ALU.add),
                             reads=[("ps", b), ("x", m, ti)], writes=[("x", m, ti)])
                        if m == 7 and nxt is not None:
                            nxt.tile_done(ti)
            if nxt is not None:
                nxt.flush()

        def final_out():
            YST = vf(oA, 4096).rearrange("p (c n) -> p c n", c=8)
            YO = vf(oS, 2048).rearrange("p (s n) -> p s n", s=2)
            yo = [0]
            for ti in range(5):
                c0, n = TILES[ti]
                rmsnorm(NV["final"], [ti], f32_out=lambda c, ti_: YST[:, c, 0:TILES[ti_][1]])
                for j in range(max(1, n // 128)):
                    rows = min(128, n)
                    s = yo[0]
                    yo[0] ^= 1
                    for half in range(2):
                        b = next_bank(0, 4)

                        def fn(e, half=half, b=b, j=j, rows=rows):
                            ins = None
                            for cc in range(4):
                                c = half * 4 + cc
                                ins = e.transpose(PS[b][0:rows, cc * 128:(cc + 1) * 128],
                                                  in_=YST[:, c, j * 128:j * 128 + rows], identity=IDENT)
                            return ins

                        p.op("pe", fn, reads=[("yst", c) for c in range(8)] + [("ident",)], writes=[("ps", b)])
                        if half == 0:
                            p.op("act", lambda e, s=s, b=b, rows=rows: e.copy(out=YO[0:rows, s, 0:512], in_=PS[b][0:rows, :]),
                                 reads=[("ps", b)], writes=[("yo", s, 0)])
                        else:
                            p.op("dve", lambda e, s=s, b=b, rows=rows: e.tensor_copy(out=YO[0:rows, s, 512:1024],
                                                                                     in_=PS[b][0:rows, :]),
                                 reads=[("ps", b)], writes=[("yo", s, 1)])
                    r0 = c0 + j * 128
                    dma("sp", y_d[r0:r0 + rows, :], YO[0:rows, s, :], [("yo", s, 0), ("yo", s, 1)], [], out=True)

        EPSB = cf(1)
        p.op("dve", lambda e: e.memset(EPSB, EPS), writes=[("epsb",)])
        PSTo = carve(1024)
        PST = vf(PSTo, 1024)

        def dump_x():
            dma("sp", dbg_d, big[:, oX:oX + 8 * NT], [("x", c, t) for c in range(8) for t in range(5)], [], out=True)

        L0 = 0
        stop = [False]

        def chk(name):
            if DEBUG_STOP == name:
                dump_x()
                stop[0] = True
            return stop[0]

        for pas in (0, 1):
            tag = "AB"[pas]
            allt = [0, 1, 2, 3, 4]
            mt = [0, 1, 2, 3] + ([4] if pas == 1 else [])
            pipe = PIPE
            load_x(xa_d if pas == 0 else xb_d, nxt=NormPipe(NV["f1l0"]) if pipe else None)
            if chk(tag + "load"):
                break
            ffn(w["ffn1_w_gate"][0], w["ffn1_w_up"][0], w["ffn1_w_down"][0], NV["f1l0"], allt, prenormed=pipe,
                nxt=NormPipe(NV["mix0"]) if pipe else None)
            if chk(tag + "f1l0"):
                break
            pool_mixer(pas, prenormed=pipe, nxt=NormPipe(NV["f2l0"]) if pipe else None)
            if chk(tag + "pool"):
                break
            ffn(w["ffn2_w_gate"][0], w["ffn2_w_up"][0], w["ffn2_w_down"][0], NV["f2l0"], mt, prenormed=pipe,
                nxt=NormPipe(NV["f1l1"]) if pipe else None)
            if chk(tag + "f2l0"):
                break
            ffn(w["ffn1_w_gate"][1], w["ffn1_w_up"][1], w["ffn1_w_down"][1], NV["f1l1"], mt, prenormed=pipe,
                nxt=NormPipe(NV["mix1"]) if pipe else None)
            if chk(tag + "f1l1"):
                break
            attn_qkv(pas, prenormed=pipe)
            if chk(tag + "qkv"):
                break
            if pas == 1:
                build_tt()
                if chk("Btt"):
                    break
                ON = attention()
                if chk("Batt"):
                    break
                sample_attn(ON)
                if chk("Bsmp"):
                    break
                attn_out(ON, nxt=NormPipe(NV["f2l1"]) if pipe else None)
                if chk("Battn"):
                    break
                ffn(w["ffn2_w_gate"][1], w["ffn2_w_up"][1], w["ffn2_w_down"][1], NV["f2l1"], allt, prenormed=pipe)
                if chk("Bf2l1"):
                    break
                final_out()

        p.finish()
        block = es.enter_context(nc.Block())

        @block.tensor
        def _(e):
            for f in p.q["pe"]:
                f(e)

        @block.scalar
        def _(e):
            for f in p.q["act"]:
                f(e)

        @block.vector
        def _(e):
            for f in p.q["dve"]:
                f(e)

        @block.gpsimd
        def _(e):
            for f in p.q["pool"]:
                f(e)

        @block.sync
        def _(e):
            for f in p.q["sp"]:
                f(e)

    return nc


_CACHE = {}


def kernel(**inputs):
    inp = {k: np.ascontiguousarray(np.asarray(v)) for k, v in inputs.items()}
    xp = inp["x_prompt"][0]
    xs = inp["x_sample"][:, 0, :]
    consts = host_consts()
    if "nc" not in _CACHE:
        _CACHE["nc"] = build_program()
    nc = _CACHE["nc"]
    in_maps = []
    wnames = ["ffn1_norm", "ffn1_w_gate", "ffn1_w_up", "ffn1_w_down", "mix_norm", "pool_w_in", "pool_w_group",
              "pool_scale", "pool_w_out", "attn_w_qkv", "attn_w_out", "rel_bias", "ffn2_norm", "ffn2_w_gate",
              "ffn2_w_up", "ffn2_w_down"]
    for c in CORES:
        m = {}
        xa = np.zeros((NT, D), np.float32)
        xb = np.zeros((NT, D), np.float32)
        if c > 0:
            xa[0:TM] = xp[TM * (c - 1):TM * c]
            lo = TM * (c - 1) - 15
            if lo >= 0:
                xa[TM:TM + 15] = xp[lo:lo + 15]
        xb[0:TM] = xp[TM * c:TM * (c + 1)]
        xb[TM:TM + NS] = xs[NS * c:NS * (c + 1)]
        m["xa"] = xa
        m["xb"] = xb
        m["stp"] = inp["state_pool"][0, NS * c:NS * (c + 1)]
        cks = (inp["cache_k_w128"], inp["cache_k_w512"], inp["cache_k_w2048"])
        cvs = (inp["cache_v_w128"], inp["cache_v_w512"], inp["cache_v_w2048"])
        for g in range(3):
            m["ck%d" % g] = cks[g][0, NS * c:NS * (c + 1)].reshape(NS, WIN[g], 512)
            m["cv%d" % g] = cvs[g][0, NS * c:NS * (c + 1)].reshape(NS, WIN[g], 512)
        for nm in wnames:
            if nm in PADDED:
                a = inp[nm]
                flat = a.reshape(-1, a.shape[-1])
                m[nm] = np.concatenate([flat, np.full((1, a.shape[-1]), float(c), np.float32)], axis=0)
            else:
                m[nm] = inp[nm]
        m["final_norm"] = inp["final_norm"].reshape(1, D)
        m["ident"] = consts["ident"]
        m["oh"] = consts["oh"]
        m["mk"] = consts["mk"]
        m["ohs"] = consts["ohs"]
        rc = np.zeros((128, 2, 4, 16), np.float32)
        for pa in range(2):
            for g in range(4):
                if (pa == 1 and c == 0) or (pa == 0 and c == 1):
                    cnt = np.minimum(POOLW[g], np.arange(16) + 1).astype(np.float32)
                else:
                    cnt = np.full(16, POOLW[g], np.float32)
                rc[:, pa, g, :] = 1.0 / cnt
        m["rc"] = rc.reshape(128, 128)
        m["hm"] = np.full((128, 1), NEG if c == 0 else 0.0, np.float32)
        in_maps.append({k: np.ascontiguousarray(v, dtype=np.float32) for k, v in m.items()})
    res = run_bass_kernel_spmd(nc, in_maps, core_ids=list(range(len(CORES))))
    r = res.results
    _CACHE["last"] = r
    if len(CORES) != NC:
        return None
    y_prompt = np.concatenate([r[c]["y"][0:TM] for c in range(NC)], axis=0)[None]
    y_sample = np.concatenate([r[c]["y"][TM:TM + NS] for c in range(NC)], axis=0)[:, None, :]
    pool_p = r[NC - 1]["poolp"][None, None]
    pool_s = np.concatenate([r[c]["pools"] for c in range(NC)], axis=0)[None]
    outs = [y_prompt, y_sample, pool_p, pool_s]
    for g in range(3):
        W = WIN[g]
        outs.append(r[NC - 1]["kp%d" % g].reshape(1, 1, W, 8, 64))
        outs.append(r[NC - 1]["vp%d" % g].reshape(1, 1, W, 8, 64))
        outs.append(np.concatenate([r[c]["ks%d" % g] for c in range(NC)], axis=0).reshape(1, NC * NS, W, 8, 64))
        outs.append(np.concatenate([r[c]["vs%d" % g] for c in range(NC)], axis=0).reshape(1, NC * NS, W, 8, 64))
    return tuple(np.ascontiguousarray(o, dtype=np.float32) for o in outs)
```

```python
import contextlib
import math
import numpy as np
import concourse.bass as bass
import concourse.mybir as mybir
from concourse.bass_utils import run_bass_kernel_spmd

F32 = mybir.dt.float32
BF16 = mybir.dt.bfloat16
AF = mybir.ActivationFunctionType
ALU = mybir.AluOpType
AX = mybir.AxisListType

NC = 8
D = 1024
DFF = 2816
NF = DFF // 128
FSPLIT = [(0, 8), (8, 8), (16, 6)]
TM = 2048
XT = 16
NT = TM + XT
NS = 4
NEG = -30000.0
WIN = (128, 512, 2048)
DIL = (1, 4, 16)
POOLW = (2, 4, 8, 16)
EPS = 1e-6
PADDED = ("ffn1_w_gate", "ffn1_w_up", "ffn1_w_down", "ffn2_w_gate", "ffn2_w_up", "ffn2_w_down", "pool_w_in",
          "pool_w_group", "pool_w_out", "attn_w_qkv", "attn_w_out")
DEBUG_STOP = None
CORES = list(range(8))
ATT_LVL = 9
PIPE = True


class Prog:
    def __init__(self, nc, es):
        self.nc = nc
        self.q = {e: [] for e in ("pe", "act", "dve", "pool", "sp")}
        self.sems = {}
        for e in ("pe", "act", "dve", "pool"):
            self.sems[e] = es.enter_context(nc.semaphore("s_" + e))
        self.ecnt = {e: 0 for e in ("pe", "act", "dve", "pool")}
        self.dsems = {"sp": [], "pool": [], "act": []}
        for e, n in (("sp", 24), ("pool", 16), ("act", 8)):
            for i in range(n):
                k = "d_%s_%d" % (e, i)
                self.sems[k] = es.enter_context(nc.semaphore(k))
                self.dsems[e].append(k)
        self.dval = {k: 0 for e in self.dsems for k in self.dsems[e]}
        self.dnext = {e: 0 for e in self.dsems}
        self.waited = {e: {} for e in self.q}
        self.lastw = {}
        self.readers = {}
        self.out_tickets = []
        self.rgroup = {}
        self.rout = {}
        self.rpre = {}

    REGION = {"xst": ("A", 0), "a": ("A", 2), "z": ("A", 3), "tt": ("A", 4), "ys": ("A", 5),
              "wd": ("S", 0), "sg": ("S", 0), "u": ("S", 1), "t": ("S", 1), "wkv": ("S", 2), "stg": ("S", 2),
              "h": ("H", 0), "at": ("H", 1), "ts": ("H", 1), "et": ("H", 1), "smp": ("H", 2),
              "kst": ("A", 6), "yst": ("A", 5), "tth": ("A", 4), "acc": ("A", 4), "tb": ("S", 3), "on": ("S", 4),
              "yo": ("S", 5), "ohs": ("W", 1), "wgu": ("W", 0), "atS": ("S", 4), "atW": ("W", 2), "atA": ("A", 4)}

    def op(self, eng, fn, reads=(), writes=(), dma=False, out=False):
        deps = {}

        def add(k, v):
            if v > deps.get(k, 0):
                deps[k] = v

        regs = set()
        for key in list(reads) + list(writes):
            rg = self.REGION.get(key[0])
            if rg is None:
                continue
            r, grp = rg
            if self.rgroup.get(r) != grp:
                self.rpre[r] = dict(self.rout.get(r, {}))
                for k, v in self.rpre[r].items():
                    pass
                self.rout[r] = dict(self.rpre[r])
                self.rgroup[r] = grp
            regs.add(r)
        for r in regs:
            for k, v in self.rpre.get(r, {}).items():
                add(k, v)

        for key in reads:
            t = self.lastw.get(key)
            if t is not None:
                add(*t)
        for key in writes:
            t = self.lastw.get(key)
            if t is not None:
                add(*t)
            for k, v in self.readers.get(key, {}).items():
                add(k, v)
        waits = []
        wd = self.waited[eng]
        for k, v in deps.items():
            if eng == "pe" and k == "pe":
                continue
            if wd.get(k, 0) < v:
                waits.append((self.sems[k], v))
                wd[k] = v
        if dma:
            pool = self.dsems[eng]
            i = self.dnext[eng]
            self.dnext[eng] = (i + 1) % len(pool)
            k = pool[i]
            prev = self.dval[k]
            if prev > 0 and wd.get(k, 0) < prev:
                waits.append((self.sems[k], prev))
                wd[k] = prev
            val = prev + 16
            self.dval[k] = val
            tk = (k, val)
            inc = 16
        else:
            self.ecnt[eng] += 1
            tk = (eng, self.ecnt[eng])
            inc = 1
        sem = self.sems[tk[0]]

        def run(e, waits=waits, fn=fn, sem=sem, inc=inc):
            for s, v in waits:
                e.wait_ge(s, v)
            fn(e).then_inc(sem, inc)

        self.q[eng].append(run)
        for key in writes:
            self.lastw[key] = tk
            self.readers[key] = {}
        for key in reads:
            r = self.readers.setdefault(key, {})
            if r.get(tk[0], 0) < tk[1]:
                r[tk[0]] = tk[1]
        for r in regs:
            ro = self.rout.setdefault(r, {})
            if ro.get(tk[0], 0) < tk[1]:
                ro[tk[0]] = tk[1]
        if out:
            self.out_tickets.append(tk)
        return tk

    def finish(self):
        need = {}
        for k, v in self.out_tickets:
            need[k] = max(need.get(k, 0), v)
        items = [(self.sems[k], v) for k, v in need.items()]

        def run(e, items=items):
            for s, v in items:
                e.wait_ge(s, v)

        self.q["sp"].append(run)


def t5_bucket_np(dist):
    dist = np.asarray(dist, np.int32)
    distf = np.maximum(dist, 1).astype(np.float32)
    v = np.log(distf / np.float32(16.0)) / np.float32(math.log(2048 / 16)) * np.float32(16.0)
    log_b = np.minimum(16 + v.astype(np.int32), 31)
    return np.where(dist < 16, dist, log_b)


def host_consts():
    c = {}
    c["ident"] = np.eye(128, dtype=np.float32)
    oh = np.zeros((3, 2, 32, 256), np.float32)
    mk = np.zeros((2, 256), np.float32)
    ohs = np.zeros((3, 32, 128), np.float32)
    for g in range(3):
        d = DIL[g]
        for y in range(256):
            if y <= 127:
                oh[g, 0, t5_bucket_np((y + 1) * d), y] = 1.0
            j = y - 127
            if 0 <= j <= 127:
                oh[g, 1, t5_bucket_np(j * d), y] = 1.0
        for i in range(128):
            ohs[g, t5_bucket_np((128 - i) * d), i] = 1.0
    mk[0, 128:] = NEG
    mk[1, :127] = NEG
    mk[1, 255] = NEG
    c["oh"] = np.ascontiguousarray(oh.transpose(2, 0, 1, 3).reshape(32, 3 * 2 * 256))
    c["mk"] = np.ascontiguousarray(np.broadcast_to(mk.reshape(1, 512), (128, 512)))
    c["ohs"] = np.ascontiguousarray(ohs.transpose(1, 0, 2).reshape(32, 3 * 128))
    return c


def build_program():
    nc = bass.Bass("TRN2", target_bir_lowering=False)

    def din(name, shape, dt=F32):
        return nc.dram_tensor(name, list(shape), dt, kind="ExternalInput")

    def dout(name, shape, dt=F32):
        return nc.dram_tensor(name, list(shape), dt, kind="ExternalOutput")

    def dscr(name, shape, dt):
        return nc.dram_tensor(name, list(shape), dt, kind="Internal")

    xa_d = din("xa", [NT, D]).ap()
    xb_d = din("xb", [NT, D]).ap()
    stp_d = din("stp", [NS, 15, D]).ap()
    ck_d = [din("ck%d" % g, [NS, WIN[g], 512]).ap() for g in range(3)]
    cv_d = [din("cv%d" % g, [NS, WIN[g], 512]).ap() for g in range(3)]
    w = {}
    for nm, shp in (("ffn1_norm", [2, D]), ("ffn1_w_gate", [2, D, DFF]), ("ffn1_w_up", [2, D, DFF]),
                    ("ffn1_w_down", [2, DFF, D]), ("mix_norm", [2, D]), ("pool_w_in", [1, D, D]),
                    ("pool_w_group", [1, 4, 256, 256]), ("pool_scale", [1, D]), ("pool_w_out", [1, D, D]),
                    ("attn_w_qkv", [1, D, 4608]), ("attn_w_out", [1, 512, D]), ("rel_bias", [32, 24]),
                    ("ffn2_norm", [2, D]), ("ffn2_w_gate", [2, D, DFF]), ("ffn2_w_up", [2, D, DFF]),
                    ("ffn2_w_down", [2, DFF, D]), ("final_norm", [1, D])):
        if nm in PADDED:
            rows = int(np.prod(shp[:-1]))
            flat = din(nm, [rows + 1, shp[-1]]).ap()[0:rows, :]
            if len(shp) == 3:
                w[nm] = flat.rearrange("(l k) n -> l k n", l=shp[0])
            else:
                w[nm] = flat.rearrange("(l g k) n -> l g k n", l=shp[0], g=shp[1])
        else:
            w[nm] = din(nm, shp).ap()
    ident_d = din("ident", [128, 128]).ap()
    oh_d = din("oh", [32, 1536]).ap()
    mk_d = din("mk", [128, 512]).ap()
    ohs_d = din("ohs", [32, 384]).ap()
    rc_d = din("rc", [128, 128]).ap()
    hm_d = din("hm", [128, 1]).ap()

    y_d = dout("y", [NT, D]).ap()
    poolp_d = dout("poolp", [15, D]).ap()
    pools_d = dout("pools", [NS, 15, D]).ap()
    kp_d = [dout("kp%d" % g, [WIN[g], 512]).ap() for g in range(3)]
    vp_d = [dout("vp%d" % g, [WIN[g], 512]).ap() for g in range(3)]
    ks_d = [dout("ks%d" % g, [NS, WIN[g], 512]).ap() for g in range(3)]
    vs_d = [dout("vs%d" % g, [NS, WIN[g], 512]).ap() for g in range(3)]
    dbg_d = dout("dbg", [128, 8 * NT]).ap() if DEBUG_STOP else None

    kt_s = dscr("kt_s", [2, 3, 4, 128, TM], BF16).ap()
    v_s = dscr("v_s", [2, 3, 16, 128, 512], BF16).ap()
    q_s = dscr("q_s", [3, 4, 128, TM], BF16).ap()
    tt_s = dscr("tt_s", [6, 128, 2048], F32)

    es = contextlib.ExitStack()
    with es:
        BIG_WORDS = 49104
        big = es.enter_context(nc.sbuf_tensor("big", [128, BIG_WORDS], F32))
        ptr = [0]

        def carve(nwords):
            o = ptr[0]
            ptr[0] += nwords
            assert ptr[0] <= BIG_WORDS, ptr[0]
            return o

        def vf(off, n):
            return big[:, off:off + n]

        def vb(off, n):
            return big[:, off:off + n // 2].bitcast(BF16)

        oX = carve(8 * NT)
        oH = carve(8 * NT // 2)
        oA = carve(8 * NT // 2)
        oW = carve(2 * 2 * 8 * 128 // 2)
        oS = carve(6208)
        oR = carve(2 * 512)
        oQ = carve(2048)
        oU2 = carve(NT)
        oC = carve(1664)
        X = vf(oX, 8 * NT).rearrange("p (c t) -> p c t", c=8)
        HB = vb(oH, 8 * NT).rearrange("p (c t) -> p c t", c=8)
        A = vb(oA, 8 * NT).rearrange("p (c t) -> p c t", c=8)
        WGU = vb(oW, 4096).rearrange("p (s j k n) -> p s j k n", s=2, j=2, k=8)
        WD = vb(oS, 8 * 1024).rearrange("p (f n) -> p f n", f=8)
        SG = vf(oS + 4096, 1024).rearrange("p (s n) -> p s n", s=2)
        RS = vf(oR, 1024).rearrange("p (s n) -> p s n", s=2)
        co = [oC]

        def cf(n):
            o = co[0]
            co[0] += n
            assert co[0] <= oC + 1664
            return vf(o, n)

        IDENT = cf(128)
        ONESB = cf(64).bitcast(BF16)
        IDB = cf(64).bitcast(BF16)
        NRM = cf(64).rearrange("p (v c) -> p v c", v=8)
        RCT = cf(128).rearrange("p (a g n) -> p a g n", a=2, g=4)
        HM = cf(1)
        UPRE = cf(120).rearrange("p (c n) -> p c n", c=8)
        US = cf(512).rearrange("p (c s n) -> p c s n", c=8, s=4)
        QS = cf(24).bitcast(BF16).rearrange("p (g h s) -> p g h s", g=3, h=4)
        KS = cf(24).bitcast(BF16).rearrange("p (g h s) -> p g h s", g=3, h=4)
        KSF = cf(48).rearrange("p (g h s) -> p g h s", g=3, h=4)
        VSF = cf(48).rearrange("p (g h s) -> p g h s", g=3, h=4)
        RB = cf(24)
        BS = cf(24).rearrange("p (g h) -> p g h", g=3)
        B0 = cf(24)
        ES = cf(96)
        E0 = cf(96)
        ESB = cf(48).bitcast(BF16)
        E0B = cf(48).bitcast(BF16)

        PS = [es.enter_context(nc.psum_tensor("ps%d" % i, [128, 512], F32)) for i in range(8)]
        p = Prog(nc, es)

        TILES = [(0, 512), (512, 512), (1024, 512), (1536, 512), (TM, XT)]

        def mm_group(out_ap, pairs, reads, writes):
            n = len(pairs)

            def fn(e):
                ins = None
                for i, (l, r) in enumerate(pairs):
                    ins = e.matmul(out_ap, lhsT=l, rhs=r, start=(i == 0), stop=(i == n - 1))
                return ins

            return p.op("pe", fn, reads=reads, writes=writes)

        def dma(eng, out_ap, in_ap, reads, writes, out=False):
            return p.op(eng, lambda e: e.dma_start(out=out_ap, in_=in_ap), reads=reads, writes=writes, dma=True,
                        out=out)

        def wchunk(wap, c0, ncols=128):
            return wap[:, c0:c0 + ncols].rearrange("(k p) n -> p k n", p=128)

        slot_ctr = [0]

        def next_slot():
            s = slot_ctr[0]
            slot_ctr[0] ^= 1
            return s

        def load_wslot(wap, c0, s, j, nk=8):
            dma("pool", WGU[:, s, j, 0:nk, :], wchunk(wap, c0), reads=[], writes=[("wgu", s, j)])

        psr = [0]

        def next_bank(lo=0, hi=6):
            b = lo + psr[0] % (hi - lo)
            psr[0] += 1
            return b

        dma("sp", IDENT, ident_d, [], [("ident",)])
        p.op("dve", lambda e: e.memset(ONESB, 1.0), writes=[("ones",)])
        p.op("dve", lambda e: e.tensor_copy(out=IDB, in_=IDENT), reads=[("ident",)], writes=[("idb",)])
        nvec = [w["ffn1_norm"][0], w["ffn1_norm"][1], w["ffn2_norm"][0], w["ffn2_norm"][1], w["mix_norm"][0],
                w["mix_norm"][1], w["pool_scale"][0], w["final_norm"][0]]
        for i, v in enumerate(nvec):
            p.op("sp", lambda e, i=i, v=v: e.dma_start(out=NRM[:, i, :], in_=v.rearrange("(c p) -> p c", p=128),
                                                       allow_slow_non_contiguous=True),
                 writes=[("nrm", i)], dma=True)
        dma("sp", RCT, rc_d.rearrange("p (a g n) -> p a g n", a=2, g=4), [], [("rct",)])
        dma("sp", HM, hm_d, [], [("hm",)])
        dma("sp", RB[0:32, :], w["rel_bias"], [], [("rb",)])
        dma("sp", B0[0:1, :], w["rel_bias"][0:1, :], [], [("b0",)])
        NV = {"f1l0": 0, "f1l1": 1, "f2l0": 2, "f2l1": 3, "mix0": 4, "mix1": 5, "pscale": 6, "final": 7}

        def load_x(x_d, nxt=None):
            XST = vf(oA, 2048).rearrange("p (s n) -> p s n", s=2)
            for tb in range(17):
                rows = 128 if tb < 16 else XT
                s = tb % 2
                dma("sp", XST[0:rows, s, :], x_d[tb * 128:tb * 128 + rows, :], [], [("xst", s)])
                for half in range(2):
                    b = next_bank()

                    def fn(e, s=s, rows=rows, half=half, b=b):
                        ins = None
                        for cc in range(4):
                            c = half * 4 + cc
                            ins = e.transpose(PS[b][:, cc * 128:cc * 128 + rows],
                                              in_=XST[0:rows, s, c * 128:(c + 1) * 128],
                                              identity=IDENT[0:rows, 0:rows])
                        return ins

                    p.op("pe", fn, reads=[("xst", s), ("ident",)], writes=[("ps", b)])
                    t = min(tb // 4, 4)
                    src = PS[b][:, :].rearrange("p (c n) -> p c n", c=4)[:, :, 0:rows]
                    dst = X[:, half * 4:half * 4 + 4, tb * 128:tb * 128 + rows]
                    eng = "act" if half == 0 else "dve"
                    if eng == "act":
                        p.op("act", lambda e, dst=dst, src=src: e.copy(out=dst, in_=src), reads=[("ps", b)],
                             writes=[("x", c, t) for c in range(half * 4, half * 4 + 4)])
                    else:
                        p.op("dve", lambda e, dst=dst, src=src: e.tensor_copy(out=dst, in_=src), reads=[("ps", b)],
                             writes=[("x", c, t) for c in range(half * 4, half * 4 + 4)])
                if nxt is not None and tb in (3, 7, 11, 15, 16):
                    nxt.tile_done(min(tb // 4, 4))
            if nxt is not None:
                nxt.flush()

        SQ = vb(oQ, 4096).rearrange("p (c n) -> p c n", c=8)

        def norm_p1(ti):
            c0, n = TILES[ti]
            p.op("act", lambda e, c0=c0, n=n: e.activation(out=SQ[:, :, 0:n], in_=X[:, :, c0:c0 + n], func=AF.Square),
                 reads=[("x", c, ti) for c in range(8)], writes=[("sq",)])

        def norm_p2(vec, ti, f32_out=None):
            c0, n = TILES[ti]
            s = ti % 2
            b = 6
            mm_group(PS[b][:, 0:n], [(ONESB, SQ[:, c, 0:n]) for c in range(8)],
                     reads=[("sq",), ("ones",)], writes=[("ps", b)])
            p.op("act", lambda e, s=s, n=n, b=b: e.activation(out=RS[:, s, 0:n], in_=PS[b][:, 0:n], func=AF.Sqrt,
                                                               scale=1.0 / D, bias=EPSB),
                 reads=[("ps", b), ("epsb",)], writes=[("rs", s)])
            p.op("dve", lambda e, s=s, n=n: e.reciprocal(out=RS[:, s, 0:n], in_=RS[:, s, 0:n]),
                 reads=[("rs", s)], writes=[("rs", s)])
            for c in range(8):
                if f32_out is None:
                    o = HB[:, c, c0:c0 + n]
                    wk = [("h", c, ti)]
                else:
                    o = f32_out(c, ti)
                    wk = [("yst", c)]
                p.op("dve", lambda e, o=o, c=c, c0=c0, n=n, s=s: e.scalar_tensor_tensor(
                    out=o, in0=X[:, c, c0:c0 + n], scalar=NRM[:, vec, c:c + 1], in1=RS[:, s, 0:n],
                    op0=ALU.mult, op1=ALU.mult),
                     reads=[("x", c, ti), ("rs", s), ("nrm", vec)], writes=wk)

        def rmsnorm(vec, tiles, f32_out=None):
            for ti in tiles:
                norm_p1(ti)
                norm_p2(vec, ti, f32_out)

        class NormPipe:
            def __init__(self, vec):
                self.vec = vec
                self.pending = None

            def tile_done(self, ti):
                if self.pending is not None:
                    norm_p2(self.vec, self.pending)
                norm_p1(ti)
                self.pending = ti

            def flush(self):
                if self.pending is not None:
                    norm_p2(self.vec, self.pending)
                    self.pending = None

        def ffn(wg, wu, wd_, vec, tiles, prenormed=False, nxt=None):
            if not prenormed:
                rmsnorm(vec, tiles)
            for gi, (f0, nf) in enumerate(FSPLIT):
                for fi in range(nf):
                    f = f0 + fi
                    s = next_slot()
                    load_wslot(wg, f * 128, s, 0)
                    load_wslot(wu, f * 128, s, 1)
                    if fi == 1:
                        dma("pool", WD[:, 0:nf, :],
                            wd_[f0 * 128:(f0 + nf) * 128, :].rearrange("(f p) n -> p f n", p=128), [], [("wd",)])
                    for ti in tiles:
                        c0, n = TILES[ti]
                        bg = (psr[0] % 2)
                        bu = 2 + (psr[0] % 2)
                        psr[0] += 1
                        hk = [("h", c, ti) for c in range(8)]
                        mm_group(PS[bg][:, 0:n], [(WGU[:, s, 0, k, :], HB[:, k, c0:c0 + n]) for k in range(8)],
                                 reads=hk + [("wgu", s, 0)], writes=[("ps", bg)])
                        mm_group(PS[bu][:, 0:n], [(WGU[:, s, 1, k, :], HB[:, k, c0:c0 + n]) for k in range(8)],
                                 reads=hk + [("wgu", s, 1)], writes=[("ps", bu)])
                        sg = bg
                        p.op("act", lambda e, sg=sg, bg=bg, n=n: e.activation(out=SG[:, sg, 0:n], in_=PS[bg][:, 0:n],
                                                                              func=AF.Silu),
                             reads=[("ps", bg)], writes=[("sg", sg)])
                        p.op("dve", lambda e, sg=sg, bu=bu, n=n, fi=fi, c0=c0: e.tensor_tensor(
                            out=A[:, fi, c0:c0 + n], in0=SG[:, sg, 0:n], in1=PS[bu][:, 0:n], op=ALU.mult),
                             reads=[("sg", sg), ("ps", bu)], writes=[("a", fi, ti)])
                last = (gi == len(FSPLIT) - 1) and (nxt is not None)
                order = [(m, ti) for ti in tiles for m in range(8)] if last else [(m, ti) for m in range(8) for ti in tiles]
                for (m, ti) in order:
                    c0, n = TILES[ti]
                    b = 4 + (psr[0] % 2)
                    psr[0] += 1
                    mm_group(PS[b][:, 0:n],
                             [(WD[:, fi, m * 128:(m + 1) * 128], A[:, fi, c0:c0 + n]) for fi in range(nf)],
                             reads=[("a", fi, ti) for fi in range(nf)] + [("wd",)], writes=[("ps", b)])
                    p.op("dve", lambda e, b=b, n=n, m=m, c0=c0: e.scalar_tensor_tensor(
                        out=X[:, m, c0:c0 + n], in0=PS[b][:, 0:n], scalar=0.5, in1=X[:, m, c0:c0 + n],
                        op0=ALU.mult, op1=ALU.add),
                         reads=[("ps", b), ("x", m, ti)], writes=[("x", m, ti)])
                    if last and m == 7:
                        nxt.tile_done(ti)
                if last:
                    nxt.flush()

        def pool_mixer(pas, prenormed=False, nxt=None):
            LU = NT
            Us = [vf(oS, LU), vf(oU2, LU)]
            T1 = vf(oS + LU, LU)
            T2 = vf(oS + 2 * LU, LU)
            Z = A
            tiles = [0, 1, 2, 3, 4]
            if not prenormed:
                rmsnorm(NV["mix0"], tiles)
            for us_ in range(2):
                p.op("pool", lambda e, us_=us_: e.memset(Us[us_][:, 0:1], 0.0), writes=[("u", us_)])
            if pas == 1:
                ST = vf(oS + 3 * LU - 1024, 1024)
                for s in range(NS):
                    dma("sp", ST[0:15, :], stp_d[s], [], [("t", 1)])
                    for half in range(2):
                        b = next_bank()

                        def fn(e, half=half, b=b):
                            ins = None
                            for cc in range(4):
                                c = half * 4 + cc
                                ins = e.transpose(PS[b][:, cc * 128:cc * 128 + 15], in_=ST[0:15, c * 128:(c + 1) * 128],
                                                  identity=IDENT[0:15, 0:15])
                            return ins

                        p.op("pe", fn, reads=[("t", 1), ("ident",)], writes=[("ps", b)])
                        p.op("act", lambda e, half=half, b=b, s=s: e.copy(
                            out=US[:, half * 4:half * 4 + 4, s, 0:15],
                            in_=PS[b][:, :].rearrange("p (c n) -> p c n", c=4)[:, :, 0:15]),
                             reads=[("ps", b)], writes=[("us", s, half)])
                for s in range(NS):
                    dma("sp", pools_d[s, 0:14, :], stp_d[s, 1:15, :], [], [], out=True)
            for c in range(8):
                g = c // 2
                wwin = POOLW[g]
                U = Us[c % 2]
                uk = ("u", c % 2)
                s = next_slot()
                load_wslot(w["pool_w_in"][0], c * 128, s, 0)
                for ti in tiles:
                    c0, n = TILES[ti]
                    b = next_bank(0, 4)
                    mm_group(PS[b][:, 0:n], [(WGU[:, s, 0, k, :], HB[:, k, c0:c0 + n]) for k in range(8)],
                             reads=[("h", k, ti) for k in range(8)] + [("wgu", s, 0)], writes=[("ps", b)])
                    if ti < 4:
                        p.op("act", lambda e, b=b, c0=c0, n=n, U=U: e.copy(out=U[:, 16 + c0:16 + c0 + n], in_=PS[b][:, 0:n]),
                             reads=[("ps", b)], writes=[uk])
                    elif pas == 0:
                        p.op("act", lambda e, b=b, U=U: e.copy(out=U[:, 1:16], in_=PS[b][:, 0:15]),
                             reads=[("ps", b)], writes=[uk])
                    else:
                        p.op("act", lambda e, b=b, c=c: e.copy(out=US[:, c, :, 15], in_=PS[b][:, 0:4]),
                             reads=[("ps", b)] + [("us", s_, c // 4) for s_ in range(NS)],
                             writes=[("usn", c)] + [("us", s_, c // 4) for s_ in range(NS)])
                if pas == 0:
                    p.op("act", lambda e, c=c, U=U: e.copy(out=UPRE[:, c, :], in_=U[:, 16 + TM - 15:16 + TM]),
                         reads=[uk], writes=[("upre", c)])
                else:
                    p.op("act", lambda e, c=c, U=U: e.copy(out=U[:, 1:16], in_=UPRE[:, c, :]),
                         reads=[("upre", c)], writes=[uk])
                    b = 7
                    p.op("pe", lambda e, b=b, U=U: e.transpose(PS[b][0:15, 0:128], in_=U[:, 16 + TM - 15:16 + TM],
                                                           identity=IDENT),
                         reads=[uk, ("ident",)], writes=[("ps", b)])
                    p.op("act", lambda e, b=b, c=c: e.copy(out=PST[0:15, c * 128:(c + 1) * 128], in_=PS[b][0:15, 0:128]),
                         reads=[("ps", b)], writes=[("pst", c)])
                src = U
                bufs = [T1, T2]
                sh = 1
                lo = 1
                for lvl in range(g + 1):
                    dstb = bufs[lvl % 2]
                    lo2 = lo + sh
                    p.op("dve", lambda e, dstb=dstb, src=src, lo2=lo2, sh=sh: e.tensor_tensor(
                        out=dstb[:, lo2:LU], in0=src[:, lo2:LU], in1=src[:, lo2 - sh:LU - sh], op=ALU.add),
                         reads=[uk] if lvl == 0 else [("t", (lvl - 1) % 2)],
                         writes=[("t", lvl % 2)])
                    src = dstb
                    lo = lo2
                    sh *= 2
                lastk = ("t", g % 2)
                p.op("dve", lambda e, src=src, c=c, wwin=wwin, U=U: e.scalar_tensor_tensor(
                    out=Z[:, c, 0:TM], in0=src[:, 16:16 + TM], scalar=1.0 / wwin, in1=U[:, 16:16 + TM],
                    op0=ALU.mult, op1=ALU.subtract),
                     reads=[lastk, uk], writes=[("z", c)])
                p.op("dve", lambda e, src=src, g=g: e.tensor_tensor(out=RS[:, 0, 0:16], in0=src[:, 16:32],
                                                                     in1=RCT[:, pas, g, :], op=ALU.mult),
                     reads=[lastk, ("rct",)], writes=[("rs", 0)])
                p.op("dve", lambda e, c=c, U=U: e.tensor_tensor(out=Z[:, c, 0:16], in0=RS[:, 0, 0:16], in1=U[:, 16:32],
                                                            op=ALU.subtract),
                     reads=[("rs", 0), uk], writes=[("z", c)])
                if pas == 1:
                    p.op("dve", lambda e, c=c, wwin=wwin: e.tensor_reduce(out=RS[:, 1, 0:4], in_=US[:, c, :, 16 - wwin:16],
                                                                          axis=AX.X, op=ALU.add),
                         reads=[("usn", c)], writes=[("rs", 1)])
                    p.op("dve", lambda e, c=c, wwin=wwin: e.scalar_tensor_tensor(
                        out=Z[:, c, TM:TM + 4], in0=RS[:, 1, 0:4], scalar=1.0 / wwin, in1=US[:, c, :, 15],
                        op0=ALU.mult, op1=ALU.subtract),
                         reads=[("rs", 1), ("usn", c)], writes=[("z", c)])
                    p.op("dve", lambda e, c=c: e.memset(Z[:, c, TM + 4:NT], 0.0), writes=[("z", c)])
            if pas == 1:
                dma("sp", poolp_d, PST[0:15, :], [("pst", c) for c in range(8)], [], out=True)
                for half in range(2):
                    b = next_bank()

                    def fn(e, half=half, b=b):
                        ins = None
                        for cc in range(4):
                            c = half * 4 + cc
                            ins = e.transpose(PS[b][0:4, cc * 128:(cc + 1) * 128], in_=US[:, c, :, 15], identity=IDENT)
                        return ins

                    p.op("pe", fn, reads=[("usn", c) for c in range(8)] + [("ident",)], writes=[("ps", b)])
                    p.op("dve", lambda e, half=half, b=b: e.tensor_copy(out=PST[32:36, half * 512:(half + 1) * 512],
                                                                        in_=PS[b][0:4, :]),
                         reads=[("ps", b)], writes=[("pst2", half)])
                dma("sp", pools_d[:, 14, :], PST[32:36, :], [("pst2", 0), ("pst2", 1)], [], out=True)
            mt = [0, 1, 2, 3] + ([4] if pas == 1 else [])
            for c in range(8):
                g = c // 2
                s = next_slot()
                dma("pool", WGU[:, s, 0, 0:2, :],
                    w["pool_w_group"][0, g][:, (c % 2) * 128:(c % 2) * 128 + 128].rearrange("(k p) n -> p k n", p=128),
                    [], [("wgu", s, 0)])
                for ti in mt:
                    c0, n = TILES[ti]
                    b = next_bank(0, 4)
                    mm_group(PS[b][:, 0:n], [(WGU[:, s, 0, k, :], Z[:, 2 * g + k, c0:c0 + n]) for k in range(2)],
                             reads=[("z", 2 * g), ("z", 2 * g + 1), ("wgu", s, 0)], writes=[("ps", b)])
                    p.op("act", lambda e, b=b, n=n, c=c, c0=c0: e.activation(
                        out=HB[:, c, c0:c0 + n], in_=PS[b][:, 0:n], func=AF.Copy, scale=NRM[:, NV["pscale"], c:c + 1]),
                         reads=[("ps", b), ("nrm", NV["pscale"])], writes=[("h", c, ti)])
            for tg in ([0, 1], [t_ for t_ in mt if t_ >= 2]):
                for m in range(8):
                    s = next_slot()
                    load_wslot(w["pool_w_out"][0], m * 128, s, 0)
                    for ti in tg:
                        c0, n = TILES[ti]
                        b = 4 + (psr[0] % 2)
                        psr[0] += 1
                        mm_group(PS[b][:, 0:n], [(WGU[:, s, 0, k, :], HB[:, k, c0:c0 + n]) for k in range(8)],
                                 reads=[("h", k, ti) for k in range(8)] + [("wgu", s, 0)], writes=[("ps", b)])
                        p.op("dve", lambda e, b=b, n=n, m=m, c0=c0: e.tensor_tensor(
                            out=X[:, m, c0:c0 + n], in0=PS[b][:, 0:n], in1=X[:, m, c0:c0 + n], op=ALU.add),
                             reads=[("ps", b), ("x", m, ti)], writes=[("x", m, ti)])
                        if m == 7 and nxt is not None:
                            nxt.tile_done(ti)
            if nxt is not None:
                nxt.flush()


        flip = {"kst": 0, "vst": 0, "ts": 0, "acc": 0}

        def attn_qkv(pas, prenormed=False):
            tiles = [0, 1, 2, 3] + ([4] if pas == 1 else [])
            if not prenormed:
                rmsnorm(NV["mix1"], tiles)
            wq = w["attn_w_qkv"][0]
            KST = vb(oA, 2 * TM).rearrange("p (s n) -> p s n", s=2)
            WKV = vb(oS, 8 * 1024).rearrange("p (k n) -> p k n", k=8)
            VST = vb(oS + 4096, 1024).rearrange("p (s n) -> p s n", s=2)
            KVF = vf(oS + 4608, 1024)
            hall = [("h", k, ti) for k in range(8) for ti in range(4)]
            for g in range(3):
                d = DIL[g]
                nb = 16 // d
                for which in ((0, 1) if pas == 1 else (1,)):
                    for hp in range(4):
                        col = g * 1536 + which * 512 + hp * 128
                        s = next_slot()
                        load_wslot(wq, col, s, 0)
                        ks = flip["kst"]
                        flip["kst"] ^= 1
                        for ti in tiles:
                            c0, n = TILES[ti]
                            b = next_bank(0, 4)
                            mm_group(PS[b][:, 0:n], [(WGU[:, s, 0, k, :], HB[:, k, c0:c0 + n]) for k in range(8)],
                                     reads=[("h", k, ti) for k in range(8)] + [("wgu", s, 0)], writes=[("ps", b)])
                            if ti < 4:
                                dst = KST[:, ks, :].rearrange("p (r m) -> p r m", r=d)[:, :, c0 // d:(c0 + n) // d]
                                src = PS[b][:, 0:n].rearrange("p (m r) -> p r m", r=d)
                                if ti % 2 == 0:
                                    p.op("act", lambda e, dst=dst, src=src: e.copy(out=dst, in_=src),
                                         reads=[("ps", b)], writes=[("kst", ks)])
                                else:
                                    p.op("dve", lambda e, dst=dst, src=src: e.tensor_copy(out=dst, in_=src),
                                         reads=[("ps", b)], writes=[("kst", ks)])
                            elif which == 0:
                                p.op("dve", lambda e, b=b, g=g, hp=hp: e.tensor_copy(out=QS[:, g, hp, :], in_=PS[b][:, 0:4]),
                                     reads=[("ps", b)], writes=[("qs", g, hp)])
                            else:
                                p.op("dve", lambda e, b=b, g=g, hp=hp: e.tensor_copy(out=KS[:, g, hp, :], in_=PS[b][:, 0:4]),
                                     reads=[("ps", b)], writes=[("ks", g, hp)])
                                p.op("dve", lambda e, b=b, g=g, hp=hp: e.tensor_copy(out=KSF[:, g, hp, :], in_=PS[b][:, 0:4]),
                                     reads=[("ps", b)], writes=[("ksf", g, hp)])
                        dd = q_s[g, hp] if which == 0 else kt_s[pas, g, hp]
                        dma("sp", dd, KST[:, ks, :], [("kst", ks)], [("qk_s", pas, g, hp, which)])
                dma("pool", WKV, wq[:, g * 1536 + 512:g * 1536 + 1536].rearrange("(k p) n -> p k n", p=128), [],
                    [("wkv",)])
                W = WIN[g]
                for blk in range(16):
                    r, n_ = blk // nb, blk % nb
                    start = n_ * 128 * d + r
                    lhs = [HB[:, k, start:start + 127 * d + 1:d] for k in range(8)]
                    need_out = (pas == 1) and (start >= TM - W)
                    bv = next_bank(0, 4)
                    mm_group(PS[bv][:, :], [(lhs[k], WKV[:, k, 512:1024]) for k in range(8)],
                             reads=hall + [("wkv",)], writes=[("ps", bv)])
                    vs = flip["vst"]
                    flip["vst"] ^= 1
                    p.op("act", lambda e, vs=vs, bv=bv: e.copy(out=VST[:, vs, :], in_=PS[bv][:, :]),
                         reads=[("ps", bv)], writes=[("stg", "v", vs)])
                    dma("sp", v_s[pas, g, blk], VST[:, vs, :], [("stg", "v", vs)], [("v_s", pas, g)])
                    if need_out:
                        bk = next_bank(0, 4)
                        mm_group(PS[bk][:, :], [(lhs[k], WKV[:, k, 0:512]) for k in range(8)],
                                 reads=hall + [("wkv",)], writes=[("ps", bk)])
                        p.op("dve", lambda e, bk=bk: e.tensor_copy(out=KVF[:, 0:512], in_=PS[bk][:, :]),
                             reads=[("ps", bk)], writes=[("stg", "kf")])
                        p.op("dve", lambda e, bv=bv: e.tensor_copy(out=KVF[:, 512:1024], in_=PS[bv][:, :]),
                             reads=[("ps", bv)], writes=[("stg", "vf")])
                        t0 = start - (TM - W)
                        dma("sp", kp_d[g][t0:t0 + 127 * d + 1:d, :], KVF[:, 0:512], [("stg", "kf")], [], out=True)
                        dma("sp", vp_d[g][t0:t0 + 127 * d + 1:d, :], KVF[:, 512:1024], [("stg", "vf")], [], out=True)
                if pas == 1:
                    for hp in range(4):
                        b = next_bank(0, 4)
                        mm_group(PS[b][:, 0:XT],
                                 [(WKV[:, k, 512 + hp * 128:512 + (hp + 1) * 128], HB[:, k, TM:TM + XT]) for k in range(8)],
                                 reads=[("h", k, 4) for k in range(8)] + [("wkv",)], writes=[("ps", b)])
                        p.op("dve", lambda e, b=b, g=g, hp=hp: e.tensor_copy(out=VSF[:, g, hp, :], in_=PS[b][:, 0:4]),
                             reads=[("ps", b)], writes=[("vsf", g, hp)])

        def build_tt():
            OH = vf(oS, 1536)
            MKT = vf(oS + 1536, 512)
            TST = vf(oS + 2048, 2048)
            dma("sp", OH[0:32, :], oh_d, [], [("tb", "oh")])
            dma("sp", MKT, mk_d, [], [("tb", "mk")])
            for g in range(3):
                for cp in range(2):
                    def fn(e, g=g, cp=cp):
                        ins = None
                        for h in range(8):
                            ins = e.matmul(PS[h // 2][:, (h % 2) * 256:(h % 2) * 256 + 256],
                                           lhsT=RB[0:32, 8 * g + h:8 * g + h + 1].broadcast_to([32, 128]),
                                           rhs=OH[0:32, (g * 2 + cp) * 256:(g * 2 + cp) * 256 + 256], start=True, stop=True)
                        return ins

                    p.op("pe", fn, reads=[("rb",), ("tb", "oh")], writes=[("ps", b) for b in range(4)])
                    for h in range(8):
                        p.op("dve", lambda e, h=h, cp=cp: e.tensor_tensor(
                            out=TST[:, h * 256:(h + 1) * 256], in0=PS[h // 2][:, (h % 2) * 256:(h % 2) * 256 + 256],
                            in1=MKT[:, cp * 256:(cp + 1) * 256], op=ALU.add),
                             reads=[("ps", h // 2), ("tb", "mk")], writes=[("tb", "tst")])
                    dma("sp", tt_s.ap()[g * 2 + cp], TST, [("tb", "tst")], [("tt_s", g, cp)])

        def attention():
            ET = vb(oH + 5120, 2048).rearrange("p (s n) -> p s n", s=2)
            TS = vf(oH + 6144, 2048).rearrange("p (s n) -> p s n", s=2)
            TTH = vf(oA, 3072).rearrange("p (s g k j a) -> p s g k j a", s=2, g=3, k=2, j=2)
            OACC = vf(oA + 3072, TM)
            DACC = vf(oA + 3072 + TM, TM)
            ON = vb(oS, 4 * NT).rearrange("p (k t) -> p k t", k=4)
            p.op("pool", lambda e: e.memset(ON[:, :, TM:NT], 0.0), writes=[("on", "x")])
            bufs = [
                dict(kto=(vb(oH, TM), ("at", "kto")), kth=(vb(oH + 1024, TM), ("at", "kth")),
                     q=(vb(oH + 2048, TM), ("at", "q")),
                     vo=(vb(oH + 3072, TM).rearrange("p (b f) -> p b f", b=16), ("at", "vo")),
                     vh=(vb(oH + 4096, TM).rearrange("p (b f) -> p b f", b=16), ("at", "vh"))),
                dict(kto=(vb(oS + 4128, TM), ("atS", "kto")), kth=(vb(oS + 5152, TM), ("atS", "kth")),
                     q=(vb(oW, TM), ("atW", "q")),
                     vo=(vb(oW + 1024, TM).rearrange("p (b f) -> p b f", b=16), ("atW", "vo")),
                     vh=(vb(oA + 7168, TM).rearrange("p (b f) -> p b f", b=16), ("atA", "vh"))),
            ]

            def emit_loads(it):
                hp_, g_ = it // 3, it % 3
                bf = bufs[it % 2]
                if g_ == 0:
                    for g2 in range(3):
                        for cp in range(2):
                            src = bass.AP(tt_s, (g2 * 2 + cp) * 128 * 2048 + 127 + hp_ * 512, [[2047, 128], [256, 2], [1, 128]])
                            dma("sp", TTH[:, hp_ % 2, g2, cp, :, :], src, [("tt_s", g2, cp)], [("tth", hp_ % 2)])
                dma("sp", bf["kto"][0], kt_s[1, g_, hp_], [("qk_s", 1, g_, hp_, 1)], [bf["kto"][1]])
                dma("sp", bf["kth"][0], kt_s[0, g_, hp_], [("qk_s", 0, g_, hp_, 1)], [bf["kth"][1]])
                dma("sp", bf["q"][0], q_s[g_, hp_], [("qk_s", 1, g_, hp_, 0)], [bf["q"][1]])
                dma("sp", bf["vo"][0], v_s[1, g_].rearrange("b a f -> a b f")[:, :, hp_ * 128:(hp_ + 1) * 128],
                    [("v_s", 1, g_)], [bf["vo"][1]])
                dma("sp", bf["vh"][0], v_s[0, g_].rearrange("b a f -> a b f")[:, :, hp_ * 128:(hp_ + 1) * 128],
                    [("v_s", 0, g_)], [bf["vh"][1]])

            emit_loads(0)
            for hp in range(4):
                sl = hp % 2
                for g in range(3):
                    it = hp * 3 + g
                    if it + 1 < 12:
                        emit_loads(it + 1)
                    bf = bufs[it % 2]
                    KTo, KTh, Q, Vo, Vh = bf["kto"][0], bf["kth"][0], bf["q"][0], bf["vo"][0], bf["vh"][0]
                    kkeys_ = [bf["kto"][1], bf["kth"][1], bf["q"][1]]
                    vkeys_ = [bf["vo"][1], bf["vh"][1]]
                    d = DIL[g]
                    nb = 16 // d

                    def blkinfo(blk, nb=nb, KTo=KTo, KTh=KTh, Vo=Vo, Vh=Vh):
                        r, n_ = blk // nb, blk % nb
                        if n_ == 0:
                            pb = r * nb + nb - 1
                            return True, KTh[:, pb * 128:(pb + 1) * 128], Vh[:, pb, :]
                        return False, KTo[:, (blk - 1) * 128:blk * 128], Vo[:, blk - 1, :]

                    def emit_S(pr):
                        banks = (0, 1) if pr % 2 == 0 else (2, 3)
                        ts = pr % 2
                        info = [blkinfo(2 * pr + j) for j in range(2)]

                        def fn(e, pr=pr, banks=banks, info=info, Q=Q, KTo=KTo):
                            ins = None
                            for hh in range(2):
                                for j in range(2):
                                    blk = 2 * pr + j
                                    qc = Q[:, blk * 128:(blk + 1) * 128]
                                    for kb, KK in enumerate((info[j][1], KTo[:, blk * 128:(blk + 1) * 128])):
                                        ins = e.matmul(PS[banks[hh]][:, (j * 2 + kb) * 128:(j * 2 + kb + 1) * 128],
                                                       lhsT=KK[64 * hh:64 * hh + 64, :], rhs=qc[64 * hh:64 * hh + 64, :],
                                                       start=True, stop=True)
                            return ins

                        p.op("pe", fn, reads=kkeys_, writes=[("ps", banks[0]), ("ps", banks[1])])
                        for hh in range(2):
                            for j in range(2):
                                p.op("dve", lambda e, ts=ts, hh=hh, j=j, banks=banks, sl=sl, g=g: e.scalar_tensor_tensor(
                                    out=TS[:, ts, hh * 512 + j * 256:hh * 512 + (j + 1) * 256].rearrange("p (k a) -> p k a", k=2),
                                    in0=PS[banks[hh]][:, j * 256:(j + 1) * 256].rearrange("p (k a) -> p k a", k=2),
                                    scalar=0.125, in1=TTH[:, sl, g, :, hh, :], op0=ALU.mult, op1=ALU.add),
                                     reads=[("ps", banks[hh]), ("tth", sl)], writes=[("ts", ts, hh, j)])
                            tsk = [("ts", ts, hh, j_) for j_ in range(2)]
                            p.op("act", lambda e, ts=ts, hh=hh: e.activation(out=ET[:, ts, hh * 512:(hh + 1) * 512],
                                                                             in_=TS[:, ts, hh * 512:(hh + 1) * 512], func=AF.Exp),
                                 reads=tsk, writes=[("et", ts, hh)])
                            TSv = TS[:, ts, hh * 512:(hh + 1) * 512].rearrange("p (j k a) -> p j k a", j=2, k=2)
                            ETv = ET[:, ts, hh * 512:(hh + 1) * 512].rearrange("p (j k a) -> p j k a", j=2, k=2)
                            hj = [j for j in range(2) if info[j][0]]
                            if len(hj) == 2:
                                p.op("act", lambda e, TSv=TSv, ETv=ETv: e.activation(
                                    out=ETv[:, :, 0, :], in_=TSv[:, :, 0, :], func=AF.Exp, bias=HM),
                                     reads=tsk + [("hm",)], writes=[("et", ts, hh)])
                            elif len(hj) == 1:
                                p.op("act", lambda e, j=hj[0], TSv=TSv, ETv=ETv: e.activation(
                                    out=ETv[:, j, 0, :], in_=TSv[:, j, 0, :], func=AF.Exp, bias=HM),
                                     reads=tsk + [("hm",)], writes=[("et", ts, hh)])

                    def emit_PV(pr):
                        ts = pr % 2
                        blk4 = pr // 2
                        bo = 4 + 2 * (blk4 % 2)
                        bd = bo + 1
                        info = [blkinfo(2 * pr + j) for j in range(2)]

                        for hh in range(2):
                            def fn2(e, pr=pr, ts=ts, bo=bo, bd=bd, info=info, Vo=Vo, hh=hh):
                                ins = None
                                for j in range(2):
                                    blk = 2 * pr + j
                                    i = (pr % 2) * 2 + j
                                    vv = (info[j][2], Vo[:, blk, :])
                                    for kb in range(2):
                                        col = ((hh * 2 + j) * 2 + kb) * 128
                                        ins = e.matmul(PS[bo][64 * hh:64 * hh + 64, i * 128:(i + 1) * 128],
                                                       lhsT=vv[kb][:, 64 * hh:64 * hh + 64], rhs=ET[:, ts, col:col + 128],
                                                       start=(kb == 0), stop=(kb == 1))
                                    for kb in range(2):
                                        col = ((hh * 2 + j) * 2 + kb) * 128
                                        ins = e.matmul(PS[bd][64 * hh:64 * hh + 64, i * 128:(i + 1) * 128],
                                                       lhsT=ONESB[:, 0:64], rhs=ET[:, ts, col:col + 128],
                                                       start=(kb == 0), stop=(kb == 1))
                                return ins

                            p.op("pe", fn2, reads=[("et", ts, hh), ("ones",)] + vkeys_, writes=[("ps", bo), ("ps", bd)])
                        if pr % 2 == 0:
                            return
                        if d == 1:
                            dsts = [a_[:, blk4 * 512:(blk4 + 1) * 512] for a_ in (OACC, DACC)]
                            srcs = [PS[bo][:, :], PS[bd][:, :]]
                        elif d == 4:
                            dsts = [a_.rearrange("p (m r) -> p r m", r=4)[:, blk4, :] for a_ in (OACC, DACC)]
                            srcs = [PS[bo][:, :], PS[bd][:, :]]
                        else:
                            dsts = [a_.rearrange("p (m r) -> p r m", r=16)[:, blk4 * 4:blk4 * 4 + 4, :] for a_ in (OACC, DACC)]
                            srcs = [PS[bo][:, :].rearrange("p (r m) -> p r m", r=4),
                                    PS[bd][:, :].rearrange("p (r m) -> p r m", r=4)]
                        for j2, (dst, src, bb) in enumerate(zip(dsts, srcs, (bo, bd))):
                            allk = [("acc", j2)] + [("acc", j2, q_) for q_ in range(4)]
                            if g == 0:
                                if j2 == 0:
                                    p.op("act", lambda e, dst=dst, src=src: e.copy(out=dst, in_=src),
                                         reads=[("ps", bb)], writes=allk)
                                else:
                                    p.op("dve", lambda e, dst=dst, src=src: e.tensor_copy(out=dst, in_=src),
                                         reads=[("ps", bb)], writes=allk)
                            else:
                                p.op("dve", lambda e, dst=dst, src=src: e.tensor_tensor(out=dst, in0=src, in1=dst, op=ALU.add),
                                     reads=[("ps", bb)] + allk, writes=allk)

                    emit_S(0)
                    for pr in range(8):
                        if pr + 1 < 8:
                            emit_S(pr + 1)
                        emit_PV(pr)
                allk0 = [("acc", 0)] + [("acc", 0, q_) for q_ in range(4)]
                allk1 = [("acc", 1)] + [("acc", 1, q_) for q_ in range(4)]
                p.op("dve", lambda e: e.reciprocal(out=DACC, in_=DACC), reads=allk1, writes=allk1)
                p.op("dve", lambda e, hp=hp: e.tensor_tensor(out=ON[:, hp, 0:TM], in0=OACC, in1=DACC, op=ALU.mult),
                     reads=allk0 + allk1, writes=[("on", hp)])
            return ON

        def sample_attn(ON):
            KCs = [vb(oH, 512), vb(oH + 256, 512)]
            VCs = [vb(oH + 512, 512), vb(oH + 768, 512)]
            KCTs = [vb(oH + 1024, 512).rearrange("p (h k) -> p h k", h=4),
                    vb(oH + 1280, 512).rearrange("p (h k) -> p h k", h=4)]
            SM = vf(oH + 1536, 832)
            S0 = SM[:, 0:48]
            E0f = SM[:, 48:96]
            B0T = SM[:, 96:108]
            PRD = SM[:, 108:156]
            NUM = SM[:, 156:204]
            DEN = SM[:, 204:252]
            NUMT = SM[:, 252:268]
            DENT = SM[:, 268:284]
            ESf = SM[:, 284:380]
            ESb = SM[:, 380:428].bitcast(BF16)
            BD = SM[:, 428:492].bitcast(BF16)
            PRB = SM[:, 492:516].bitcast(BF16)
            QSF = SM[:, 516:564]
            KNR = SM[:, 564:820]
            OHS = vf(oW, 384)
            dma("sp", OHS[0:32, :], ohs_d, [], [("ohs",)])
            p.op("dve", lambda e: e.memset(BD, 0.0), writes=[("smp", "bd")])
            p.op("dve", lambda e: e.memset(BD[0:64, 0:64], 1.0), writes=[("smp", "bd")])
            p.op("dve", lambda e: e.memset(BD[64:128, 64:128], 1.0), writes=[("smp", "bd")])
            for hh in range(2):
                src = bass.AP(w["rel_bias"].tensor, hh, [[0, 64], [2, 12]])
                p.op("sp", lambda e, hh=hh, src=src: e.dma_start(out=B0T[64 * hh:64 * hh + 64, :], in_=src,
                                                                 allow_slow_non_contiguous=True),
                     writes=[("smp", "b0t", hh)], dma=True)
            b = next_bank(0, 4)

            def fnb(e, b=b):
                ins = None
                for g in range(3):
                    ins = e.matmul(PS[b][:, g * 8:(g + 1) * 8], lhsT=OHS[0:32, g * 128:(g + 1) * 128],
                                   rhs=RB[0:32, 8 * g:8 * g + 8], start=True, stop=True)
                return ins

            p.op("pe", fnb, reads=[("ohs",), ("rb",)], writes=[("ps", b)])
            p.op("dve", lambda e, b=b: e.tensor_copy(out=BS, in_=PS[b][:, 0:24].rearrange("p (g h) -> p g h", g=3)),
                 reads=[("ps", b)], writes=[("smp", "bs")])
            qkeys = [("qs", g, hp) for g in range(3) for hp in range(4)]
            kkeys = [("ks", g, hp) for g in range(3) for hp in range(4)]
            kfkeys = [("ksf", g, hp) for g in range(3) for hp in range(4)]
            vfkeys = [("vsf", g, hp) for g in range(3) for hp in range(4)]
            p.op("dve", lambda e: e.tensor_copy(out=QSF, in_=QS.rearrange("p g h s -> p (g h s)")), reads=qkeys,
                 writes=[("smp", "qsf")])
            p.op("dve", lambda e: e.tensor_tensor(out=PRD, in0=QSF, in1=KSF.rearrange("p g h s -> p (g h s)"), op=ALU.mult),
                 reads=[("smp", "qsf")] + kfkeys, writes=[("smp", "prd")])
            p.op("dve", lambda e: e.tensor_copy(out=PRB, in_=PRD), reads=[("smp", "prd")], writes=[("smp", "prb")])
            p.op("dve", lambda e: e.tensor_tensor(out=NUM, in0=PRD, in1=PRB, op=ALU.subtract),
                 reads=[("smp", "prd"), ("smp", "prb")], writes=[("smp", "num")])
            p.op("dve", lambda e: e.tensor_copy(out=ESb[:, 0:48], in_=NUM), reads=[("smp", "num")], writes=[("smp", "esb")])
            b0 = next_bank(0, 4)

            def fn0(e, b0=b0):
                e.matmul(PS[b0][:, 0:48], lhsT=BD, rhs=PRB, start=True, stop=False)
                return e.matmul(PS[b0][:, 0:48], lhsT=BD, rhs=ESb[:, 0:48], start=False, stop=True)

            p.op("pe", fn0, reads=[("smp", "bd"), ("smp", "prb"), ("smp", "esb")], writes=[("ps", b0)])
            p.op("dve", lambda e, b0=b0: e.scalar_tensor_tensor(
                out=S0.rearrange("p (x s) -> p x s", s=4), in0=PS[b0][:, 0:48].rearrange("p (x s) -> p x s", s=4),
                scalar=0.125, in1=B0T.unsqueeze(2).broadcast_to([128, 12, 4]), op0=ALU.mult, op1=ALU.add),
                 reads=[("ps", b0), ("smp", "b0t", 0), ("smp", "b0t", 1)], writes=[("smp", "s0")])
            p.op("act", lambda e: e.activation(out=E0f, in_=S0, func=AF.Exp), reads=[("smp", "s0")], writes=[("smp", "e0")])
            bso, bsd = 4, 5
            for s in range(NS):
                for g in range(3):
                    W = WIN[g]
                    dma("sp", ks_d[g][s, 0:W - 1, :], ck_d[g][s, 1:W, :], [], [], out=True)
                    dma("sp", vs_d[g][s, 0:W - 1, :], cv_d[g][s, 1:W, :], [], [], out=True)
            first = True

            def smp_loads(i):
                s_, g_ = i // 3, i % 3
                dma("pool", KCs[i % 2], ck_d[g_][s_, 0:WIN[g_]:DIL[g_], :], [], [("smp", "kc", i % 2)])

            def smp_vloads(i):
                s_, g_ = i // 3, i % 3
                dma("pool", VCs[i % 2], cv_d[g_][s_, 0:WIN[g_]:DIL[g_], :], [], [("smp", "vc", i % 2)])

            smp_loads(0)
            smp_vloads(0)
            smp_vloads(1)

            def smp_A(i_):
                s, g = i_ // 3, i_ % 3
                if i_ + 1 < NS * 3:
                    smp_loads(i_ + 1)
                sb_ = i_ % 2
                KC, KCT = KCs[sb_], KCTs[sb_]
                bt = 7
                PT = PS[bt][:, 0:256].bitcast(BF16)

                def fnt(e, PT=PT, KC=KC):
                    ins = None
                    for hp in range(4):
                        ins = e.transpose(PT[:, hp * 128:(hp + 1) * 128], in_=KC[:, hp * 128:(hp + 1) * 128], identity=IDB)
                    return ins

                p.op("pe", fnt, reads=[("smp", "kc", sb_), ("idb",)], writes=[("ps", bt)])
                p.op("act", lambda e, PT=PT, KCT=KCT: e.copy(out=KCT.rearrange("p h k -> p (h k)"), in_=PT),
                     reads=[("ps", bt)], writes=[("smp", "kct", sb_)])
                bsc = (next_bank(0, 4), next_bank(0, 4))

                def fns(e, g=g, s=s, bsc=bsc, KCT=KCT):
                    ins = None
                    for hh in range(2):
                        for hp in range(4):
                            ins = e.matmul(PS[bsc[hh]][:, hp:hp + 1], lhsT=KCT[64 * hh:64 * hh + 64, hp, :],
                                           rhs=QS[64 * hh:64 * hh + 64, g, hp, s:s + 1], start=True, stop=True)
                    return ins

                p.op("pe", fns, reads=[("smp", "kct", sb_)] + qkeys, writes=[("ps", bsc[0]), ("ps", bsc[1])])
                for hh in range(2):
                    p.op("dve", lambda e, g=g, bsc=bsc, hh=hh, sb_=sb_: e.scalar_tensor_tensor(
                        out=ESf[:, sb_ * 8:sb_ * 8 + 8].rearrange("p (a b) -> p b a", b=2)[:, hh, :], in0=PS[bsc[hh]][:, 0:4],
                        scalar=0.125, in1=BS[:, g, :].rearrange("p (a b) -> p b a", b=2)[:, hh, :], op0=ALU.mult, op1=ALU.add),
                         reads=[("ps", bsc[hh]), ("smp", "bs")], writes=[("smp", "esf", sb_, hh)])
                p.op("act", lambda e, sb_=sb_: e.activation(out=ESb[:, 48 + sb_ * 8:56 + sb_ * 8],
                                                            in_=ESf[:, sb_ * 8:sb_ * 8 + 8], func=AF.Exp),
                     reads=[("smp", "esf", sb_, 0), ("smp", "esf", sb_, 1)], writes=[("smp", "esx", sb_)])

            def smp_B(i_):
                s, g = i_ // 3, i_ % 3
                sb_ = i_ % 2
                VC = VCs[sb_]

                def fnp(e, g=g, s=s, VC=VC, sb_=sb_):
                    ins = None
                    for h in range(8):
                        hp, hh = h // 2, h % 2
                        col = (g * 4 + hp) * 4 + s
                        ec = 48 + sb_ * 8 + h
                        ins = e.matmul(PS[bso][64 * hh:64 * hh + 64, col:col + 1], lhsT=VC[:, h * 64:(h + 1) * 64],
                                       rhs=ESb[:, ec:ec + 1], start=True, stop=True)
                        ins = e.matmul(PS[bsd][64 * hh:64 * hh + 64, col:col + 1], lhsT=ONESB[:, 0:64],
                                       rhs=ESb[:, ec:ec + 1], start=True, stop=True)
                    return ins

                p.op("pe", fnp, reads=[("smp", "esx", sb_), ("smp", "vc", sb_), ("ones",)], writes=[("ps", bso), ("ps", bsd)])
                if i_ + 2 < NS * 3:
                    smp_vloads(i_ + 2)

            smp_A(0)
            for i_ in range(NS * 3):
                if i_ + 1 < NS * 3:
                    smp_A(i_ + 1)
                smp_B(i_)
            p.op("dve", lambda e: e.tensor_tensor(out=PRD, in0=E0f, in1=VSF.rearrange("p g h s -> p (g h s)"), op=ALU.mult),
                 reads=[("smp", "e0")] + vfkeys, writes=[("smp", "prd")])
            p.op("dve", lambda e: e.tensor_tensor(out=NUM, in0=PS[bso][:, 0:48], in1=PRD, op=ALU.add),
                 reads=[("ps", bso), ("smp", "prd")], writes=[("smp", "num")])
            p.op("dve", lambda e: e.tensor_tensor(out=DEN, in0=PS[bsd][:, 0:48], in1=E0f, op=ALU.add),
                 reads=[("ps", bsd), ("smp", "e0")], writes=[("smp", "den")])
            p.op("dve", lambda e: e.tensor_reduce(out=NUMT, in_=NUM.rearrange("p (g x) -> p x g", g=3), axis=AX.X, op=ALU.add),
                 reads=[("smp", "num")], writes=[("smp", "numt")])
            p.op("dve", lambda e: e.tensor_reduce(out=DENT, in_=DEN.rearrange("p (g x) -> p x g", g=3), axis=AX.X, op=ALU.add),
                 reads=[("smp", "den")], writes=[("smp", "dent")])
            p.op("dve", lambda e: e.reciprocal(out=DENT, in_=DENT), reads=[("smp", "dent")], writes=[("smp", "dent")])
            p.op("dve", lambda e: e.tensor_tensor(out=ON[:, :, TM:TM + 4], in0=NUMT.rearrange("p (h s) -> p h s", h=4),
                                                   in1=DENT.rearrange("p (h s) -> p h s", h=4), op=ALU.mult),
                 reads=[("smp", "numt"), ("smp", "dent"), ("on", "x")], writes=[("on", "x")])
            for which, SRC, keys, outd in ((0, KSF, kfkeys, ks_d), (1, VSF, vfkeys, vs_d)):
                for g in range(3):
                    W = WIN[g]
                    b = next_bank(0, 4)

                    def fnr(e, SRC=SRC, g=g, b=b):
                        ins = None
                        for hp in range(4):
                            ins = e.transpose(PS[b][0:4, hp * 128:(hp + 1) * 128], in_=SRC[:, g, hp, :], identity=IDENT)
                        return ins

                    p.op("pe", fnr, reads=keys + [("ident",)], writes=[("ps", b)])
                    st = PST[64:68, 0:512] if which == 0 else PST[64:68, 512:1024]
                    p.op("dve", lambda e, st=st, b=b: e.tensor_copy(out=st, in_=PS[b][0:4, :]), reads=[("ps", b)],
                         writes=[("pst3", which)])
                    dma("sp", outd[g][:, W - 1, :], st, [("pst3", which)], [], out=True)

        def attn_out(ON, nxt=None):
            for tg in ([0, 1], [2, 3, 4]):
                for m in range(8):
                    s = next_slot()
                    dma("pool", WGU[:, s, 0, 0:4, :], wchunk(w["attn_w_out"][0], m * 128), [], [("wgu", s, 0)])
                    for ti in tg:
                        c0, n = TILES[ti]
                        b = 4 + (psr[0] % 2)
                        psr[0] += 1
                        mm_group(PS[b][:, 0:n], [(WGU[:, s, 0, k, :], ON[:, k, c0:c0 + n]) for k in range(4)],
                                 reads=[("on", k) for k in range(4)] + [("on", "x"), ("wgu", s, 0)], writes=[("ps", b)])
                        p.op("dve", lambda e, b=b, n=n, m=m, c0=c0: e.tensor_tensor(
                            out=X[:, m, c0:c0 + n], in0=PS[b][:, 0:n], in1=X[:, m, c0:c0 + n], op=ALU.add),
                             reads=[("ps", b), ("x", m, ti)], writes=[("x", m, ti)])
                        if m == 7 and nxt is not None:
                            nxt.tile_done(ti)
            if nxt is not None:
                nxt.flush()

        def final_out():
            YST = vf(oA, 4096).rearrange("p (c n) -> p c n", c=8)
            YO = vf(oS, 2048).rearrange("p (s n) -> p s n", s=2)
            yo = [0]
            for ti in range(5):
                c0, n = TILES[ti]
                rmsnorm(NV["final"], [ti], f32_out=lambda c, ti_: YST[:, c, 0:TILES[ti_][1]])
                for j in range(max(1, n // 128)):
                    rows = min(128, n)
                    s = yo[0]
                    yo[0] ^= 1
                    for half in range(2):
                        b = next_bank(0, 4)

                        def fn(e, half=half, b=b, j=j, rows=rows):
                            ins = None
                            for cc in range(4):
                                c = half * 4 + cc
                                ins = e.transpose(PS[b][0:rows, cc * 128:(cc + 1) * 128],
                                                  in_=YST[:, c, j * 128:j * 128 + rows], identity=IDENT)
                            return ins

                        p.op("pe", fn, reads=[("yst", c) for c in range(8)] + [("ident",)], writes=[("ps", b)])
                        if half == 0:
                            p.op("act", lambda e, s=s, b=b, rows=rows: e.copy(out=YO[0:rows, s, 0:512], in_=PS[b][0:rows, :]),
                                 reads=[("ps", b)], writes=[("yo", s, 0)])
                        else:
                            p.op("dve", lambda e, s=s, b=b, rows=rows: e.tensor_copy(out=YO[0:rows, s, 512:1024],
                                                                                     in_=PS[b][0:rows, :]),
                                 reads=[("ps", b)], writes=[("yo", s, 1)])
                    r0 = c0 + j * 128
                    dma("sp", y_d[r0:r0 + rows, :], YO[0:rows, s, :], [("yo", s, 0), ("yo", s, 1)], [], out=True)

        EPSB = cf(1)
        p.op("dve", lambda e: e.memset(EPSB, EPS), writes=[("epsb",)])
        PSTo = carve(1024)
        PST = vf(PSTo, 1024)

        def dump_x():
            dma("sp", dbg_d, big[:, oX:oX + 8 * NT], [("x", c, t) for c in range(8) for t in range(5)], [], out=True)

        L0 = 0
        stop = [False]

        def chk(name):
            if DEBUG_STOP == name:
                dump_x()
                stop[0] = True
            return stop[0]

        for pas in (0, 1):
            tag = "AB"[pas]
            allt = [0, 1, 2, 3, 4]
            mt = [0, 1, 2, 3] + ([4] if pas == 1 else [])
            pipe = PIPE
            load_x(xa_d if pas == 0 else xb_d, nxt=NormPipe(NV["f1l0"]) if pipe else None)
            if chk(tag + "load"):
                break
            ffn(w["ffn1_w_gate"][0], w["ffn1_w_up"][0], w["ffn1_w_down"][0], NV["f1l0"], allt, prenormed=pipe,
                nxt=NormPipe(NV["mix0"]) if pipe else None)
            if chk(tag + "f1l0"):
                break
            pool_mixer(pas, prenormed=pipe, nxt=NormPipe(NV["f2l0"]) if pipe else None)
            if chk(tag + "pool"):
                break
            ffn(w["ffn2_w_gate"][0], w["ffn2_w_up"][0], w["ffn2_w_down"][0], NV["f2l0"], mt, prenormed=pipe,
                nxt=NormPipe(NV["f1l1"]) if pipe else None)
            if chk(tag + "f2l0"):
                break
            ffn(w["ffn1_w_gate"][1], w["ffn1_w_up"][1], w["ffn1_w_down"][1], NV["f1l1"], mt, prenormed=pipe,
                nxt=NormPipe(NV["mix1"]) if pipe else None)
            if chk(tag + "f1l1"):
                break
            attn_qkv(pas, prenormed=pipe)
            if chk(tag + "qkv"):
                break
            if pas == 1:
                build_tt()
                if chk("Btt"):
                    break
                ON = attention()
                if chk("Batt"):
                    break
                sample_attn(ON)
                if chk("Bsmp"):
                    break
                attn_out(ON, nxt=NormPipe(NV["f2l1"]) if pipe else None)
                if chk("Battn"):
                    break
                ffn(w["ffn2_w_gate"][1], w["ffn2_w_up"][1], w["ffn2_w_down"][1], NV["f2l1"], allt, prenormed=pipe)
                if chk("Bf2l1"):
                    break
                final_out()

        p.finish()
        block = es.enter_context(nc.Block())

        @block.tensor
        def _(e):
            for f in p.q["pe"]:
                f(e)

        @block.scalar
        def _(e):
            for f in p.q["act"]:
                f(e)

        @block.vector
        def _(e):
            for f in p.q["dve"]:
                f(e)

        @block.gpsimd
        def _(e):
            for f in p.q["pool"]:
                f(e)

        @block.sync
        def _(e):
            for f in p.q["sp"]:
                f(e)

    return nc


_CACHE = {}


def kernel(**inputs):
    inp = {k: np.ascontiguousarray(np.asarray(v)) for k, v in inputs.items()}
    xp = inp["x_prompt"][0]
    xs = inp["x_sample"][:, 0, :]
    consts = host_consts()
    if "nc" not in _CACHE:
        _CACHE["nc"] = build_program()
    nc = _CACHE["nc"]
    in_maps = []
    wnames = ["ffn1_norm", "ffn1_w_gate", "ffn1_w_up", "ffn1_w_down", "mix_norm", "pool_w_in", "pool_w_group",
              "pool_scale", "pool_w_out", "attn_w_qkv", "attn_w_out", "rel_bias", "ffn2_norm", "ffn2_w_gate",
              "ffn2_w_up", "ffn2_w_down"]
    for c in CORES:
        m = {}
        xa = np.zeros((NT, D), np.float32)
        xb = np.zeros((NT, D), np.float32)
        if c > 0:
            xa[0:TM] = xp[TM * (c - 1):TM * c]
            lo = TM * (c - 1) - 15
            if lo >= 0:
                xa[TM:TM + 15] = xp[lo:lo + 15]
        xb[0:TM] = xp[TM * c:TM * (c + 1)]
        xb[TM:TM + NS] = xs[NS * c:NS * (c + 1)]
        m["xa"] = xa
        m["xb"] = xb
        m["stp"] = inp["state_pool"][0, NS * c:NS * (c + 1)]
        cks = (inp["cache_k_w128"], inp["cache_k_w512"], inp["cache_k_w2048"])
        cvs = (inp["cache_v_w128"], inp["cache_v_w512"], inp["cache_v_w2048"])
        for g in range(3):
            m["ck%d" % g] = cks[g][0, NS * c:NS * (c + 1)].reshape(NS, WIN[g], 512)
            m["cv%d" % g] = cvs[g][0, NS * c:NS * (c + 1)].reshape(NS, WIN[g], 512)
        for nm in wnames:
            if nm in PADDED:
                a = inp[nm]
                flat = a.reshape(-1, a.shape[-1])
                m[nm] = np.concatenate([flat, np.full((1, a.shape[-1]), float(c), np.float32)], axis=0)
            else:
                m[nm] = inp[nm]
        m["final_norm"] = inp["final_norm"].reshape(1, D)
        m["ident"] = consts["ident"]
        m["oh"] = consts["oh"]
        m["mk"] = consts["mk"]
        m["ohs"] = consts["ohs"]
        rc = np.zeros((128, 2, 4, 16), np.float32)
        for pa in range(2):
            for g in range(4):
                if (pa == 1 and c == 0) or (pa == 0 and c == 1):
                    cnt = np.minimum(POOLW[g], np.arange(16) + 1).astype(np.float32)
                else:
                    cnt = np.full(16, POOLW[g], np.float32)
                rc[:, pa, g, :] = 1.0 / cnt
        m["rc"] = rc.reshape(128, 128)
        m["hm"] = np.full((128, 1), NEG if c == 0 else 0.0, np.float32)
        in_maps.append({k: np.ascontiguousarray(v, dtype=np.float32) for k, v in m.items()})
    res = run_bass_kernel_spmd(nc, in_maps, core_ids=list(range(len(CORES))))
    r = res.results
    _CACHE["last"] = r
    if len(CORES) != NC:
        return None
    y_prompt = np.concatenate([r[c]["y"][0:TM] for c in range(NC)], axis=0)[None]
    y_sample = np.concatenate([r[c]["y"][TM:TM + NS] for c in range(NC)], axis=0)[:, None, :]
    pool_p = r[NC - 1]["poolp"][None, None]
    pool_s = np.concatenate([r[c]["pools"] for c in range(NC)], axis=0)[None]
    outs = [y_prompt, y_sample, pool_p, pool_s]
    for g in range(3):
        W = WIN[g]
        outs.append(r[NC - 1]["kp%d" % g].reshape(1, 1, W, 8, 64))
        outs.append(r[NC - 1]["vp%d" % g].reshape(1, 1, W, 8, 64))
        outs.append(np.concatenate([r[c]["ks%d" % g] for c in range(NC)], axis=0).reshape(1, NC * NS, W, 8, 64))
        outs.append(np.concatenate([r[c]["vs%d" % g] for c in range(NC)], axis=0).reshape(1, NC * NS, W, 8, 64))
    return tuple(np.ascontiguousarray(o, dtype=np.float32) for o in outs)
```
